# Optimizing a Trainium2 kernel written in Bass

```python
import math
import jax, jax.numpy as jnp
from jax import lax
import numpy as np

D_MODEL = 1024
BATCH = 8
SEQ = 2048
DEPTH = 1
DEC_BATCH = 128
DEC_SEQ = 4
PAST_LEN = 16384
PAGE_SIZE = 128

M_HEADS = 4
M_DK = 128
M_DV = 256
G_HEADS = 4
G_DK = 128
G_DV = 256
G_RANK = 16
G_TAU = 16.0
CHUNK = 64
N_MEM = 256
C_HEADS = 4
C_HD = D_MODEL // C_HEADS
D_FF = 2816
EPS = 1e-6

M_QK = M_HEADS * M_DK
M_V = M_HEADS * M_DV
G_QK = G_HEADS * G_DK
G_V = G_HEADS * G_DV
IN_SPLITS = (M_QK, M_QK, M_V, M_V, M_HEADS, M_HEADS, G_QK, G_QK, G_V, G_V, G_RANK, D_MODEL, D_MODEL)
IN_WIDTH = 2 * M_QK + 2 * M_V + 2 * M_HEADS + 2 * G_QK + 2 * G_V + G_RANK + 2 * D_MODEL

kernel_name = 'hybrid_mlstm_gla_macaron_memxattn_step'


def rmsnorm(x, g):
    xf = x.astype(jnp.float32)
    xf = xf * lax.rsqrt(jnp.mean(xf * xf, axis=-1, keepdims=True) + EPS)
    return (xf * g.astype(jnp.float32)).astype(x.dtype)


def head_rmsnorm(h):
    return h * lax.rsqrt(jnp.mean(h * h, axis=-1, keepdims=True) + EPS)


def swiglu(x, wg, wu, wd):
    return (jax.nn.silu(x @ wg) * (x @ wu)) @ wd


def chunk_len(T):
    return CHUNK if T % CHUNK == 0 else T


def to_chunks(a, L):
    B, T = a.shape[:2]
    return jnp.swapaxes(a.reshape((B, T // L, L) + a.shape[2:]), 0, 1)


def from_chunks(a):
    N, B, L = a.shape[:3]
    return jnp.swapaxes(a, 0, 1).reshape((B, N * L) + a.shape[3:])


def mlstm_chunk(carry, inp):
    C0, n0, m0 = carry
    q, k, v, ig, lf = inp
    L = q.shape[1]
    tri = jnp.tril(jnp.ones((L, L), dtype=bool))
    F = jnp.swapaxes(jnp.cumsum(lf, axis=1), 1, 2)
    igh = jnp.swapaxes(ig, 1, 2)
    D = F[..., :, None] - F[..., None, :] + igh[..., None, :]
    D = jnp.where(tri, D, -jnp.inf)
    m_inter = m0[..., None] + F
    m = jnp.maximum(m_inter, jnp.max(D, axis=-1))
    W = jnp.exp(D - m[..., None])
    a = jnp.exp(m_inter - m)
    S = jnp.einsum('bthd,bshd->bhts', q, k) * W
    num = a[..., None] * jnp.einsum('bhvd,bthd->bhtv', C0, q) + jnp.einsum('bhts,bshv->bhtv', S, v)
    den = a * jnp.einsum('bhd,bthd->bht', n0, q) + jnp.sum(S, axis=-1)
    h = num / jnp.maximum(jnp.abs(den), jnp.exp(-m))[..., None]
    w_end = W[:, :, -1, :]
    a_end = a[:, :, -1]
    C1 = a_end[..., None, None] * C0 + jnp.einsum('bhs,bshv,bshd->bhvd', w_end, v, k)
    n1 = a_end[..., None] * n0 + jnp.einsum('bhs,bshd->bhd', w_end, k)
    return (C1, n1, m[:, :, -1]), jnp.swapaxes(h, 1, 2)


def gla_chunk(S0, inp):
    q, k, v, la = inp
    L = q.shape[1]
    tri = jnp.tril(jnp.ones((L, L), dtype=bool))
    b = jnp.cumsum(la, axis=1)
    inter = jnp.einsum('bthd,bhdv->bthv', q * jnp.exp(b), S0)
    diff = b[:, :, None] - b[:, None, :]
    decay = jnp.exp(jnp.where(tri[None, :, :, None, None], diff, -jnp.inf))
    A = jnp.einsum('bthd,bshd,btshd->bhts', q, k, decay)
    o = inter + jnp.einsum('bhts,bshv->bthv', A, v)
    b_end = b[:, -1]
    S1 = jnp.exp(b_end)[..., None] * S0 + jnp.einsum('bshd,bshv->bhdv', k * jnp.exp(b_end[:, None] - b), v)
    return S1, o


def token_mix(h, C0, n0, m0, S0, P):
    B, T, _ = h.shape
    dt = h.dtype
    f32 = jnp.float32
    z = h @ P['w_in']
    idx = np.cumsum(np.array(IN_SPLITS))[:-1].tolist()
    (mq, mk, mv, mo, mi, mf, gq, gk, gv, gr, ga, gate_m, gate_g) = jnp.split(z, idx, axis=-1)
    L = chunk_len(T)
    q = mq.reshape(B, T, M_HEADS, M_DK).astype(f32)
    k = mk.reshape(B, T, M_HEADS, M_DK).astype(f32) * (M_DK ** -0.5)
    v = mv.reshape(B, T, M_HEADS, M_DV).astype(f32)
    b_if = P['b_if'].astype(f32)
    ig = mi.astype(f32) + b_if[:M_HEADS]
    lf = jax.nn.log_sigmoid(mf.astype(f32) + b_if[M_HEADS:])
    (C1, n1, m1), hm = lax.scan(
        mlstm_chunk, (C0.astype(f32), n0.astype(f32), m0.astype(f32)),
        (to_chunks(q, L), to_chunks(k, L), to_chunks(v, L), to_chunks(ig, L), to_chunks(lf, L)))
    hm = head_rmsnorm(from_chunks(hm)).reshape(B, T, M_V)
    hm = (hm * P['mlstm_norm'].astype(f32) * jax.nn.sigmoid(mo.astype(f32))).astype(dt)
    gq_ = gq.reshape(B, T, G_HEADS, G_DK).astype(f32) * (G_DK ** -0.5)
    gk_ = gk.reshape(B, T, G_HEADS, G_DK).astype(f32)
    gv_ = gv.reshape(B, T, G_HEADS, G_DV).astype(f32)
    la = jax.nn.log_sigmoid((ga @ P['gla_wa2'] + P['gla_ba']).astype(f32)) / G_TAU
    la = la.reshape(B, T, G_HEADS, G_DK)
    S1, hg = lax.scan(gla_chunk, S0.astype(f32),
                      (to_chunks(gq_, L), to_chunks(gk_, L), to_chunks(gv_, L), to_chunks(la, L)))
    hg = head_rmsnorm(from_chunks(hg)).reshape(B, T, G_V)
    hg = (hg * P['gla_norm'].astype(f32) * jax.nn.silu(gr.astype(f32))).astype(dt)
    y = jax.nn.sigmoid(gate_m) * (hm @ P['w_br_m']) + jax.nn.sigmoid(gate_g) * (hg @ P['w_br_g'])
    return y @ P['w_out'], (C1, n1, m1, S1)


def mem_kv(mem, g_mem, wk, wv):
    B = mem.shape[0]
    mn = rmsnorm(mem, g_mem)
    return (mn @ wk).reshape(B, N_MEM, C_HEADS, C_HD), (mn @ wv).reshape(B, N_MEM, C_HEADS, C_HD)


def cross_attn(h, mk, mv, wq, wo):
    B, T, _ = h.shape
    q = (h @ wq).reshape(B, T, C_HEADS, C_HD)
    s = jnp.einsum('bthd,bmhd->bhtm', q, mk).astype(jnp.float32) * (C_HD ** -0.5)
    p = jax.nn.softmax(s, axis=-1).astype(h.dtype)
    o = jnp.einsum('bhtm,bmhd->bthd', p, mv).reshape(B, T, D_MODEL)
    return o @ wo


def layer(x, mk, mv, C0, n0, m0, S0, P):
    x = x + 0.5 * swiglu(rmsnorm(x, P['ffn1_norm']), P['ffn1_wg'], P['ffn1_wu'], P['ffn1_wd'])
    mix, st = token_mix(rmsnorm(x, P['mix_norm']), C0, n0, m0, S0, P)
    x = x + mix
    x = x + cross_attn(rmsnorm(x, P['ca_norm']), mk, mv, P['ca_wq'], P['ca_wo'])
    x = x + 0.5 * swiglu(rmsnorm(x, P['ffn2_norm']), P['ffn2_wg'], P['ffn2_wu'], P['ffn2_wd'])
    return x, st


def setup_inputs(seed: int = 0) -> dict:
    key = jax.random.key(seed)
    ks = iter(jax.random.split(key, 48))
    f32 = jnp.float32

    def nrm(shape, scale):
        return jax.random.normal(next(ks), shape, f32) * scale

    def gain(shape):
        return 1.0 + nrm(shape, 0.02)

    Dp = DEPTH
    b_i = -1.0 + nrm((Dp, M_HEADS), 0.1)
    b_f = 3.0 + nrm((Dp, M_HEADS), 0.5)
    return {
        'x_prompt': nrm((BATCH, SEQ, D_MODEL), 1.0),
        'x_sample': nrm((DEC_BATCH, DEC_SEQ, D_MODEL), 1.0),
        'mem_prompt': nrm((BATCH, N_MEM, D_MODEL), 1.0),
        'state_mlstm_C': nrm((Dp, DEC_BATCH, M_HEADS, M_DV, M_DK), 0.1),
        'state_mlstm_n': nrm((Dp, DEC_BATCH, M_HEADS, M_DK), 0.1),
        'state_mlstm_m': nrm((Dp, DEC_BATCH, M_HEADS), 1.0),
        'state_gla_S': nrm((Dp, DEC_BATCH, G_HEADS, G_DK, G_DV), 0.1),
        'cache_mem_k': nrm((Dp, DEC_BATCH, N_MEM, C_HEADS, C_HD), 1.0),
        'cache_mem_v': nrm((Dp, DEC_BATCH, N_MEM, C_HEADS, C_HD), 1.0),
        'ffn1_norm': gain((Dp, D_MODEL)),
        'ffn1_wg': nrm((Dp, D_MODEL, D_FF), D_MODEL ** -0.5),
        'ffn1_wu': nrm((Dp, D_MODEL, D_FF), D_MODEL ** -0.5),
        'ffn1_wd': nrm((Dp, D_FF, D_MODEL), D_FF ** -0.5),
        'mix_norm': gain((Dp, D_MODEL)),
        'w_in': nrm((Dp, D_MODEL, IN_WIDTH), D_MODEL ** -0.5),
        'b_if': jnp.concatenate([b_i, b_f], axis=-1),
        'gla_wa2': nrm((Dp, G_RANK, G_QK), G_RANK ** -0.5),
        'gla_ba': nrm((Dp, G_QK), 0.1),
        'mlstm_norm': gain((Dp, M_V)),
        'gla_norm': gain((Dp, G_V)),
        'w_br_m': nrm((Dp, M_V, D_MODEL), M_V ** -0.5),
        'w_br_g': nrm((Dp, G_V, D_MODEL), G_V ** -0.5),
        'w_out': nrm((Dp, D_MODEL, D_MODEL), D_MODEL ** -0.5),
        'ca_norm': gain((Dp, D_MODEL)),
        'mem_norm': gain((Dp, D_MODEL)),
        'ca_wq': nrm((Dp, D_MODEL, D_MODEL), D_MODEL ** -0.5),
        'ca_wk': nrm((Dp, D_MODEL, D_MODEL), D_MODEL ** -0.5),
        'ca_wv': nrm((Dp, D_MODEL, D_MODEL), D_MODEL ** -0.5),
        'ca_wo': nrm((Dp, D_MODEL, D_MODEL), D_MODEL ** -0.5),
        'ffn2_norm': gain((Dp, D_MODEL)),
        'ffn2_wg': nrm((Dp, D_MODEL, D_FF), D_MODEL ** -0.5),
        'ffn2_wu': nrm((Dp, D_MODEL, D_FF), D_MODEL ** -0.5),
        'ffn2_wd': nrm((Dp, D_FF, D_MODEL), D_FF ** -0.5),
        'final_norm': gain((D_MODEL,)),
    }


def reference(x_prompt, x_sample, mem_prompt, state_mlstm_C, state_mlstm_n, state_mlstm_m, state_gla_S,
              cache_mem_k, cache_mem_v, ffn1_norm, ffn1_wg, ffn1_wu, ffn1_wd, mix_norm, w_in, b_if,
              gla_wa2, gla_ba, mlstm_norm, gla_norm, w_br_m, w_br_g, w_out, ca_norm, mem_norm,
              ca_wq, ca_wk, ca_wv, ca_wo, ffn2_norm, ffn2_wg, ffn2_wu, ffn2_wd, final_norm):
    f32 = jnp.float32
    Bp = x_prompt.shape[0]
    sdt = state_mlstm_C.dtype
    xp, xs = x_prompt, x_sample
    Cp_l, np_l, mp_l, Sp_l, mkp_l, mvp_l = [], [], [], [], [], []
    Cs_l, ns_l, ms_l, Ss_l = [], [], [], []
    for l in range(DEPTH):
        P = {
            'ffn1_norm': ffn1_norm[l], 'ffn1_wg': ffn1_wg[l], 'ffn1_wu': ffn1_wu[l], 'ffn1_wd': ffn1_wd[l],
            'mix_norm': mix_norm[l], 'w_in': w_in[l], 'b_if': b_if[l], 'gla_wa2': gla_wa2[l],
            'gla_ba': gla_ba[l], 'mlstm_norm': mlstm_norm[l], 'gla_norm': gla_norm[l],
            'w_br_m': w_br_m[l], 'w_br_g': w_br_g[l], 'w_out': w_out[l], 'ca_norm': ca_norm[l],
            'ca_wq': ca_wq[l], 'ca_wo': ca_wo[l], 'ffn2_norm': ffn2_norm[l], 'ffn2_wg': ffn2_wg[l],
            'ffn2_wu': ffn2_wu[l], 'ffn2_wd': ffn2_wd[l],
        }
        mk_p, mv_p = mem_kv(mem_prompt, mem_norm[l], ca_wk[l], ca_wv[l])
        C0 = jnp.zeros((Bp, M_HEADS, M_DV, M_DK), f32)
        n0 = jnp.zeros((Bp, M_HEADS, M_DK), f32)
        m0 = jnp.zeros((Bp, M_HEADS), f32)
        S0 = jnp.zeros((Bp, G_HEADS, G_DK, G_DV), f32)
        xp, (Cp, npr, mp, Sp) = layer(xp, mk_p, mv_p, C0, n0, m0, S0, P)
        xs, (Cs, ns, ms, Ss) = layer(xs, cache_mem_k[l], cache_mem_v[l], state_mlstm_C[l], state_mlstm_n[l],
                                     state_mlstm_m[l], state_gla_S[l], P)
        Cp_l.append(Cp.astype(sdt)); np_l.append(npr.astype(sdt)); mp_l.append(mp.astype(sdt))
        Sp_l.append(Sp.astype(sdt)); mkp_l.append(mk_p); mvp_l.append(mv_p)
        Cs_l.append(Cs.astype(sdt)); ns_l.append(ns.astype(sdt)); ms_l.append(ms.astype(sdt))
        Ss_l.append(Ss.astype(sdt))
    y_prompt = rmsnorm(xp, final_norm)
    y_sample = rmsnorm(xs, final_norm)
    return (y_prompt, y_sample,
            jnp.stack(Cp_l), jnp.stack(np_l), jnp.stack(mp_l), jnp.stack(Sp_l),
            jnp.stack(mkp_l), jnp.stack(mvp_l),
            jnp.stack(Cs_l), jnp.stack(ns_l), jnp.stack(ms_l), jnp.stack(Ss_l))
```

```python
import numpy as np
from contextlib import ExitStack
import concourse.bass as bass
import concourse.mybir as mybir
from concourse.bass_utils import run_bass_kernel_spmd

F32 = mybir.dt.float32
BF16 = mybir.dt.bfloat16
AF = mybir.ActivationFunctionType
ALU = mybir.AluOpType

NCORES = 8
D = 1024
DFF = 2816
SEQ = 2048
TT = 512
NTILE = 4
NS = 16
STK = 4
NCOL = 576
INW = 8216
EPS = 1e-6
MQ, MK, MV, MO, MI, MF, GQ, GK, GV, GR, GA, GM, GG = (0, 512, 1024, 2048, 3072, 3076, 3080, 3592, 4104, 5128,
                                                      6152, 6168, 7192)
NSLOT = 4
SLAB_ELEMS = 8 * 528
NZ = 36
BIG = 1.0e30
ZMQ, ZMK, ZMO, ZGQ, ZGK, ZGR, ZKD = 0, 4, 8, 16, 20, 24, 32


class Buf:
    __slots__ = ("name", "w", "r", "dsem", "dcnt", "excl")

    def __init__(self, name):
        self.name = name
        self.excl = name.startswith("ps")
        self.w = {}
        self.r = {}
        self.dsem = None
        self.dcnt = 0


class V:
    __slots__ = ("ap", "bufs")

    def __init__(self, ap, bufs):
        self.ap = ap
        self.bufs = bufs


class Eng:
    def __init__(self, name, sem):
        self.name = name
        self.sem = sem
        self.cnt = 0
        self.seen = {}
        self.prog = []


class KB:
    def __init__(self, nc, stack):
        self.nc = nc
        self.stack = stack
        self.eng = {n: Eng(n, stack.enter_context(nc.semaphore("s_" + n))) for n in ("pe", "act", "dve", "pool", "sp")}
        self.final = []
        self.snap = {}
        self.nwaits = {}
        self.nbuf = 0
        self.ps_i = 0
        self.slot_i = 0
        self.tog = 0

    def buf(self, name=None):
        self.nbuf += 1
        return Buf(name or ("b%d" % self.nbuf))

    def sb(self, name, shape, dt=F32):
        return self.stack.enter_context(self.nc.sbuf_tensor(name, shape, dt))

    def sbv(self, name, shape, dt=F32):
        t = self.sb(name, shape, dt)
        return V(t[:], [self.buf(name)])

    def _waits(self, e, reads, writes):
        need = {}

        def add(d, war):
            for k, (s, v) in d.items():
                if k == e.name and e.name in ("pe", "sp"):
                    continue
                if k not in need or need[k][1] < v:
                    need[k] = (s, v)
        for b in reads:
            add(b.w, False)
            if b.excl:
                add(b.r, True)
        for b in writes:
            add(b.w, False)
            add(b.r, True)
        cand = [(k, s, v) for k, (s, v) in need.items() if e.seen.get(k, 0) < v]
        keep = []
        for (k, s, v) in cand:
            implied = False
            for (k2, s2, v2) in cand:
                if k2 == k and v2 == v:
                    continue
                if self.snap.get((k2, v2), {}).get(k, 0) >= v:
                    implied = True
                    break
            if not implied:
                keep.append((k, s, v))
        out = []
        for (k, s, v) in keep:
            if e.seen.get(k, 0) >= v:
                continue
            e.seen[k] = v
            out.append((s, v))
            for kk, v3 in self.snap.get((k, v), {}).items():
                if e.seen.get(kk, 0) < v3:
                    e.seen[kk] = v3
        self.nwaits[e.name] = self.nwaits.get(e.name, 0) + len(out)
        return out

    def op(self, en, fn, reads=(), writes=()):
        e = self.eng[en]
        waits = self._waits(e, reads, writes)
        e.cnt += 1
        ev = (e.sem, e.cnt)
        self.snap[(en, e.cnt)] = dict(e.seen)
        e.prog.append((waits, fn, (e.sem, 1)))
        for b in reads:
            b.r[en] = ev
        for b in writes:
            b.w[en] = ev
            b.r = {}

    def dma(self, qn, out, in_, reads=(), writes=(), store=False, **kw):
        e = self.eng[qn]
        waits = self._waits(e, reads, writes)
        tb = writes[0] if writes else reads[0]
        if tb.dsem is None:
            tb.dsem = {}
        if qn not in tb.dsem:
            tb.dsem[qn] = [self.stack.enter_context(self.nc.semaphore("d_%s_%s" % (tb.name, qn))), 0]
        ds = tb.dsem[qn]
        ds[1] += 16
        dsem = ds[0]
        ev = (dsem, ds[1])
        key = "d_%s_%s" % (tb.name, qn)
        self.snap[(key, ds[1])] = dict(e.seen)
        e.prog.append((waits, (lambda h: h.dma_start(out=out, in_=in_, **kw)), (dsem, 16)))
        for b in reads:
            b.r[key] = ev
        for b in writes:
            b.w[key] = ev
            b.r = {}
        self.final.append(ev)

    @staticmethod
    def _bufs(*vs):
        out = []
        for v in vs:
            if isinstance(v, V):
                out.extend(v.bufs)
        return out

    @staticmethod
    def _ap(v):
        return v.ap if isinstance(v, V) else v

    def mm(self, out, lhsT, rhs, start=True, stop=True):
        o, l, r = out.ap, lhsT.ap, rhs.ap
        self.op("pe", lambda h: h.matmul(o, lhsT=l, rhs=r, start=start, stop=stop),
                reads=self._bufs(lhsT, rhs), writes=out.bufs)

    def tr(self, out, in_, ident):
        o, i, d = out.ap, in_.ap, ident.ap
        self.op("pe", lambda h: h.transpose(o, i, d), reads=self._bufs(in_, ident), writes=out.bufs)

    def act(self, out, in_, func, bias=None, scale=None, accum=None, eng="act"):
        o, i = out.ap, in_.ap
        kw = {}
        if bias is not None:
            kw["bias"] = self._ap(bias)
        if scale is not None:
            kw["scale"] = self._ap(scale)
        if accum is not None:
            kw["accum_out"] = accum.ap
        w = out.bufs + (accum.bufs if accum is not None else [])
        self.op("act", lambda h: h.activation(out=o, in_=i, func=func, **kw),
                reads=self._bufs(in_, bias, scale), writes=w)

    def tt(self, out, in0, in1, op, eng="dve"):
        o, a, b = out.ap, in0.ap, in1.ap
        self.op(eng, lambda h: h.tensor_tensor(out=o, in0=a, in1=b, op=op), reads=self._bufs(in0, in1), writes=out.bufs)

    def ts(self, out, in0, s1, op0, s2=None, op1=None, eng="dve"):
        o, a = out.ap, in0.ap
        if eng == "pool" and op1 is None and op0 == ALU.mult:
            op1, s2 = ALU.mult, 1.0
        x1, x2 = self._ap(s1), self._ap(s2)
        if op1 is None:
            fn = lambda h: h.tensor_scalar(out=o, in0=a, scalar1=x1, scalar2=None, op0=op0)
        else:
            fn = lambda h: h.tensor_scalar(out=o, in0=a, scalar1=x1, scalar2=x2, op0=op0, op1=op1)
        self.op(eng, fn, reads=self._bufs(in0, s1, s2), writes=out.bufs)

    def stt(self, out, in0, scalar, in1, op0, op1):
        o, a, b = out.ap, in0.ap, in1.ap
        s = self._ap(scalar)
        self.op("dve", lambda h: h.scalar_tensor_tensor(out=o, in0=a, scalar=s, in1=b, op0=op0, op1=op1),
                reads=self._bufs(in0, scalar, in1), writes=out.bufs)

    def cp(self, out, in_, eng="dve"):
        o, i = out.ap, in_.ap
        if eng == "act":
            self.op("act", lambda h: h.activation(out=o, in_=i, func=AF.Identity), reads=in_.bufs, writes=out.bufs)
        else:
            self.op(eng, lambda h: h.tensor_copy(out=o, in_=i), reads=in_.bufs, writes=out.bufs)

    def cpx(self, out, in_):
        self.tog ^= 1
        self.cp(out, in_, eng="act" if self.tog else "dve")

    def scan(self, out, d0, d1, initial, op0, op1):
        o, a, b = out.ap, d0.ap, d1.ap
        ini = self._ap(initial)
        self.op("dve", lambda h: h.tensor_tensor_scan(out=o, data0=a, data1=b, initial=ini, op0=op0, op1=op1),
                reads=self._bufs(d0, d1, initial), writes=out.bufs)

    def memset(self, out, val, eng="pool"):
        o = out.ap
        self.op(eng, lambda h: h.memset(o, val), writes=out.bufs)

    def asel(self, out, in_, pattern, cmp, fill, base, cm):
        o, i = out.ap, in_.ap
        self.op("pool", lambda h: h.affine_select(out=o, in_=i, pattern=pattern, compare_op=cmp, fill=fill, base=base,
                                                  channel_multiplier=cm), reads=in_.bufs, writes=out.bufs)


def vv(v, ap):
    return V(ap, v.bufs)


def build_nc():
    nc = bass.Bass("TRN2", target_bir_lowering=False)

    def di(n, s):
        return nc.dram_tensor(n, s, F32, kind="ExternalInput").ap()

    def do(n, s):
        return nc.dram_tensor(n, s, F32, kind="ExternalOutput").ap()

    x_p = di("x_p", [SEQ, D])
    x_s = di("x_s", [NS * STK, D])
    mem = di("mem", [256, D])
    Cs_in = di("C_s", [NS, 4, 256, 128])
    ns_in = di("n_s", [NS * 4, 128])
    ms_in = di("m_s", [NS, 4])
    Ss_in = di("S_s", [NS, 4, 128, 256])
    ck_in = di("ck", [NS, 256, D])
    cv_in = di("cv", [NS, 256, D])
    W = {}
    for n, s in [("ffn1_norm", [D]), ("ffn1_wg", [D, DFF]), ("ffn1_wu", [D, DFF]), ("ffn1_wd", [DFF, D]),
                 ("mix_norm", [D]), ("w_in", [D, INW]), ("b_if", [8]), ("gla_wa2", [16, 512]), ("gla_ba", [512]),
                 ("mlstm_norm", [D]), ("gla_norm", [D]), ("w_br_m", [D, D]), ("w_br_g", [D, D]), ("w_out", [D, D]),
                 ("ca_norm", [D]), ("mem_norm", [D]), ("ca_wq", [D, D]), ("ca_wk", [D, D]), ("ca_wv", [D, D]),
                 ("ca_wo", [D, D]), ("ffn2_norm", [D]), ("ffn2_wg", [D, DFF]), ("ffn2_wu", [D, DFF]),
                 ("ffn2_wd", [DFF, D]), ("final_norm", [D])]:
        W[n] = di(n, s)
    y_p = do("y_p", [SEQ, D])
    y_s = do("y_s", [NS * STK, D])
    Cp_o = do("Cp", [4, 256, 128])
    np_o = do("np", [4, 128])
    mp_o = do("mp", [4, 1])
    Sp_o = do("Sp", [4, 128, 256])
    mkp_o = do("mkp", [256, D])
    mvp_o = do("mvp", [256, D])
    Cs_o = do("Cs", [NS, 4, 256, 128])
    ns_o = do("ns", [4 * NS, 128])
    ms_o = do("ms", [4, NS])
    Ss_o = do("Ss", [NS, 4, 128, 256])

    with ExitStack() as stack:
        k = KB(nc, stack)

        xT_t = k.sb("xT", [128, 8, NCOL], F32)
        xT = [V(xT_t[:, i, :], [k.buf("xT%d" % i)]) for i in range(8)]
        hT_t = k.sb("hT", [128, 8, NCOL], BF16)
        hT = [V(hT_t[:, i, :], [k.buf("hT%d" % i)]) for i in range(8)]
        Z_t = k.sb("Z", [128, NZ, NCOL], BF16)
        Zb = [k.buf("Z%d" % i) for i in range(NZ)]
        Z = [V(Z_t[:, i, :], [Zb[i]]) for i in range(NZ)]
        TMW = 2564
        TM_t = k.sb("TM", [128, 5, TMW], BF16)
        TMb = [k.buf("TM%d" % i) for i in range(5)]
        mk_tm = [V(TM_t[:, p, 0:512], [TMb[p]]) for p in range(5)]
        mv_tm = [V(TM_t[:, p, 512:512 + 1028].rearrange("p (h v) -> p h v", v=257), [TMb[p]]) for p in range(5)]
        gv_tm = [V(TM_t[:, p, 1540:2564].rearrange("p (h v) -> p h v", v=256), [TMb[p]]) for p in range(5)]
        stg = [V(TM_t[:, p, 0:2048].bitcast(F32), [TMb[p]]) for p in range(5)]
        slots = []
        for i in range(NSLOT):
            t = k.sb("slab%d" % i, [128, SLAB_ELEMS], BF16)
            slots.append((t, k.buf("slab%d" % i)))
        NPS = 8
        PS = []
        for i in range(NPS):
            t = stack.enter_context(nc.psum_tensor("ps%d" % i, [128, 512], F32))
            PS.append(V(t[:], [k.buf("ps%d" % i)]))

        def psb():
            p = ps()
            return V(p.ap.bitcast(BF16), p.bufs)

        pinned = set()

        def ps(pin=False):
            while (k.ps_i % NPS) in pinned:
                k.ps_i += 1
            idx = k.ps_i % NPS
            k.ps_i += 1
            if pin:
                pinned.add(idx)
            return PS[idx]

        def unpin(p):
            pinned.discard(PS.index(p))


        ident_f = k.sbv("ident_f", [128, 128], F32)
        ident_b = k.sbv("ident_b", [128, 128], BF16)
        ones_b = k.sbv("ones_b", [128, 128], BF16)
        ones_c = k.sbv("ones_c", [128, 1], F32)
        sel = k.sbv("sel", [4, 4, 128], BF16)
        maskbigP = k.sbv("maskbigP", [128, 128], F32)
        mask01P = k.sbv("mask01P", [128, 128], F32)
        maskbigS = k.sbv("maskbigS", [64, 64], F32)
        mask01S = k.sbv("mask01S", [64, 64], F32)
        ind = k.sbv("ind", [64, NS], F32)
        CT = k.sbv("consts", [128, 72], F32)
        gcol = {"ffn1_norm": 0, "mix_norm": 8, "ca_norm": 16, "mem_norm": 24, "ffn2_norm": 32, "final_norm": 40,
                "mlstm_norm": 48, "gla_norm": 56}
        negba = k.sbv("negba", [128, 4], F32)
        bi = k.sbv("bi", [4, 1], F32)
        negbf = k.sbv("negbf", [4, 1], F32)
        wa2_b = k.sbv("wa2_b", [16, 512], BF16)

        mkT_mem = k.sbv("mkT_mem", [128, 8, 256], BF16)
        mv_mem = k.sbv("mv_mem", [128, 2, D], BF16)
        Cst = [k.sbv("Cst%d" % h, [128, 2, 128], F32) for h in range(4)]
        Sst = [k.sbv("Sst%d" % h, [128, 256], F32) for h in range(4)]
        nTp = k.sbv("nTp", [128, 4], F32)
        mcar = k.sbv("mcar", [4, 1], F32)
        IG = k.sbv("IG", [4, NCOL], F32)
        E1 = None
        SPm = None
        FNx = k.sbv("FNx", [4, NCOL + 1], F32)
        GGt = IG
        GL2 = [k.sbv("GL0", [4, 128], F32)] * 3
        MG2 = [k.sbv("MG0", [4, 128], F32)] * 3
        AE2 = [k.sbv("AE0", [4, 2, 128], F32)] * 3
        MGH = [k.sbv("MGH%d" % i, [4, 2, 128], BF16) for i in range(3)]
        AEH = [k.sbv("AEH%d" % i, [4, 2, 2, 128], BF16) for i in range(3)]
        TMPm = k.sbv("TMPm", [4, 128], F32)
        gaT = k.sbv("gaT", [16, NCOL], BF16)
        Ge1 = k.sbv("Ge1", [128, NCOL], F32)
        E1 = V(Ge1.ap[0:4, :], Ge1.bufs)
        SPm = E1
        BNx = k.sbv("BNx", [128, NCOL + 1], F32)
        NB5 = k.sbv("NB5", [128, 8], F32)
        carB = k.sbv("carB", [128, 4], F32)
        Earg = k.sbv("Earg", [128, 64], F32)
        Eq3 = [k.sbv("Eq%d" % i, [128, 128], F32) for i in range(3)]
        DEC = k.sbv("DEC", [128, 4, 4 + NS], F32)
        def zf32(c0):
            t = Z_t[:, c0:c0 + 2, :].rearrange("p c n -> p (c n)").bitcast(F32)
            return V(t[:, 0:512], [Zb[c0], Zb[c0 + 1]])
        lnv = zf32(32)
        rstd = zf32(34)
        sgt = [V(Z_t[:, 22 + i, 0:512], [Zb[22 + i]]) for i in range(2)]
        mrg = [lnv, rstd]
        NSTG = 4
        STQ = "sp"
        STG_t = k.sb("STG", [128, 2 * NSTG, 256], F32)
        STGb = [k.buf("STG%d" % i) for i in range(2 * NSTG)]
        Cstg = [V(STG_t[:, i, :].rearrange("p (a d) -> p a d", d=128), [STGb[i]]) for i in range(NSTG)]
        Sstg = [V(STG_t[:, NSTG + i, :], [STGb[NSTG + i]]) for i in range(NSTG)]
        nTs = k.sbv("nTs", [128, 64], F32)
        n_tm = k.sbv("n_tm", [64, 128], F32)
        m0T = k.sbv("m0T", [4, NS], F32)
        msT = k.sbv("msT", [4, NS], F32)

        class Pool2:
            def __init__(self, name, shape, dt, n=2):
                self.items = [k.sbv("%s_%d" % (name, i), shape, dt) for i in range(n)]
                self.i = 0

            def get(self):
                v = self.items[self.i % len(self.items)]
                self.i += 1
                return v
        P_WT = Pool2("WT", [128, 128], F32, 5)
        P_PT = Pool2("PT", [128, 128], BF16, 6)
        P_AB = Pool2("AB", [128, 2, 128], F32, 5)
        P_QA = Pool2("QA", [128, 128], BF16, 5)
        P_QN = Pool2("QN", [128, 128], BF16, 5)
        P_CTB = Pool2("CTB", [128, 256], BF16, 3)
        P_HD = Pool2("HD", [128, 128], F32)
        P_RD = Pool2("RD", [128, 128], F32)
        P_HR = Pool2("HR", [128, 2, 128], F32, 6)
        P_KW = Pool2("KW", [128, 128], BF16, 6)
        P_GT = Pool2("GT", [128, 4], F32, 3)
        P_SQ = Pool2("SQh", [128, 2, 128], BF16)
        P_T1 = Pool2("T1", [128, 2, 128], F32)
        P_LR = Pool2("LR", [128, 2, 128], F32)
        P_SBF = Pool2("SBF", [128, 256], BF16, 3)
        P_KD = Pool2("KD", [128, 128], BF16, 5)
        class PoolV:
            def __init__(self, items):
                self.items = items
                self.i = 0

            def get(self):
                v = self.items[self.i % len(self.items)]
                self.i += 1
                return v
        P_ET = PoolV([V(TM_t[:, 4, q * 1024:(q + 1) * 1024].rearrange("p (m n) -> p m n", n=512), [TMb[4]])
                      for q in range(2)])
        P_KB = PoolV([V(TM_t[:, q, 0:2048].rearrange("p (m n) -> p m n", n=D), [TMb[q]]) for q in range(4)])
        KTs8 = []
        for q in range(2):
            kt3 = STG_t[:, q * 4:q * 4 + 4, :].rearrange("p a d -> p (a d)").bitcast(BF16).rearrange("p (c m) -> p c m", m=256)
            KTs8.append([V(kt3[:, c, :], [k.buf("KT%d_%d" % (q, c))]) for c in range(8)])
        maskbigPb = k.sbv("maskbigPb", [128, 128], BF16)
        maskbigSb = k.sbv("maskbigSb", [64, 64], BF16)
        ETs = k.sbv("ETs", [128, 32], BF16)
        RDs = k.sbv("RDs", [128, 16], F32)
        LNs = k.sbv("LNs", [128, 16], F32)

        k.memset(ident_f, 0.0)
        k.asel(ident_f, ident_f, [[-1, 128]], ALU.not_equal, 1.0, 0, 1)
        k.cp(ident_b, ident_f, eng="pool")
        k.memset(ones_b, 1.0)
        k.memset(ones_c, 1.0)
        k.memset(sel, 0.0)
        k.asel(sel, sel, [[-1, 4], [0, 128]], ALU.not_equal, 1.0, 0, 1)
        k.memset(maskbigP, 0.0)
        k.asel(maskbigP, maskbigP, [[1, 128]], ALU.is_ge, BIG, 0, -1)
        k.memset(mask01P, 1.0)
        k.asel(mask01P, mask01P, [[1, 128]], ALU.is_ge, 0.0, 0, -1)
        k.memset(maskbigS, 0.0)
        k.asel(maskbigS, maskbigS, [[4, 16], [1, 4]], ALU.is_ge, BIG, 0, -1)
        k.asel(maskbigS, maskbigS, [[-4, 16], [0, 4]], ALU.is_ge, BIG, 0, 1)
        k.cp(maskbigPb, maskbigP, eng="pool")
        k.cp(maskbigSb, maskbigS, eng="pool")
        k.memset(mask01S, 1.0)
        k.asel(mask01S, mask01S, [[4, 16], [1, 4]], ALU.is_ge, 0.0, 0, -1)
        k.asel(mask01S, mask01S, [[-4, 16], [0, 4]], ALU.is_ge, 0.0, 0, 1)
        k.memset(ind, 1.0)
        k.asel(ind, ind, [[-4, NS]], ALU.is_ge, 0.0, 0, 1)
        k.asel(ind, ind, [[4, NS]], ALU.is_ge, 0.0, 3, -1)
        for h in range(4):
            k.memset(Cst[h], 0.0)
            k.memset(Sst[h], 0.0)
        k.memset(nTp, 0.0)
        k.memset(mcar, 0.0)
        k.memset(carB, 0.0)
        k.memset(vv(FNx, FNx.ap[:, 0:1]), 0.0)
        for p in range(5):
            k.memset(vv(mv_tm[p], mv_tm[p].ap[:, :, 256:257]), 1.0)
        for n, c in gcol.items():
            k.dma("sp", CT.ap[:, c:c + 8], W[n].rearrange("(c p) -> p c", p=128), writes=CT.bufs,
                  allow_slow_non_contiguous=True)
        k.dma("sp", CT.ap[:, 64:68], W["gla_ba"].rearrange("(c p) -> p c", p=128), writes=CT.bufs,
              allow_slow_non_contiguous=True)
        k.dma("sp", bi.ap, W["b_if"][0:4].rearrange("(p o) -> p o", o=1), writes=bi.bufs)
        k.dma("sp", negbf.ap, W["b_if"][4:8].rearrange("(p o) -> p o", o=1), writes=negbf.bufs)
        k.dma("pool", wa2_b.ap, W["gla_wa2"], writes=wa2_b.bufs)
        k.dma("sp", n_tm.ap, ns_in, writes=n_tm.bufs)
        k.dma("sp", m0T.ap, ms_in.rearrange("i h -> h i"), writes=m0T.bufs, allow_slow_non_contiguous=True)
        k.ts(negbf, negbf, -1.0, ALU.mult)
        k.ts(negba, vv(CT, CT.ap[:, 64:68]), -1.0, ALU.mult)

        def gain(name, c):
            col = gcol[name] + c
            return vv(CT, CT.ap[:, col:col + 1])

        pt = ps()
        k.tr(vv(pt, pt.ap[:, 0:64]), n_tm, vv(ident_f, ident_f.ap[0:64, 0:64]))
        k.cp(vv(nTs, nTs.ap.rearrange("p (h i) -> p h i", i=NS)),
             vv(pt, pt.ap[:, 0:64].rearrange("p (i h) -> p h i", h=4)))

        def slab(wap, KC, c0, w):
            si_ = k.slot_i % NSLOT
            t, b = slots[si_]
            k.slot_i += 1
            view = t[:, 0:KC * w].rearrange("p (c n) -> p c n", n=w)
            src = wap.rearrange("(c p) n -> p c n", p=128)[:, :, c0:c0 + w]
            k.dma("pool", view, src, writes=[b] + slot_subs[si_])
            return V(view, [b])

        slot_subs = [[] for _ in range(NSLOT)]

        def borrow_slot(si_, shape3):
            t, b = slots[si_]
            if not slot_subs[si_]:
                slot_subs[si_] = [k.buf("sub%d_%d" % (si_, q)) for q in range(8)]
            f = t[:, 0:4096].bitcast(F32)
            k.memset(V(f[:, 0:1], [b] + slot_subs[si_]), 0.0, eng="dve")
            out = []
            for q in range(8):
                v = f[:, q * 256:(q + 1) * 256]
                if shape3:
                    v = v.rearrange("p (a d) -> p a d", d=128)
                out.append(V(v, [slot_subs[si_][q]]))
            return out

        def fm_mm(sl, KC, s0, M, ins, c0, n):
            p = ps()
            pv = vv(p, p.ap[0:M, 0:n])
            for c in range(KC):
                k.mm(pv, vv(sl, sl.ap[:, c, s0:s0 + M]), vv(ins[c], ins[c].ap[:, c0:c0 + n]), start=(c == 0),
                     stop=(c == KC - 1))
            return pv

        def tm_mm(sl, KC, s0, w, ins, t0, M):
            p = ps()
            pv = vv(p, p.ap[0:M, 0:w])
            for c in range(KC):
                k.mm(pv, vv(ins[c], ins[c].ap[:, t0:t0 + M]), vv(sl, sl.ap[:, c, s0:s0 + w]), start=(c == 0),
                     stop=(c == KC - 1))
            return pv

        def rmsnorm(src, gname, dst, cgs, nch=8, div=float(D), sq=None):
            if sq is None:
                sq = Z[24:24 + nch]
            for (c0, n) in cgs:
                for c in range(nch):
                    k.act(vv(sq[c], sq[c].ap[:, c0:c0 + n]), vv(src[c], src[c].ap[:, c0:c0 + n]), AF.Square)
                p = ps()
                pv = vv(p, p.ap[:, 0:n])
                for c in range(nch):
                    k.mm(pv, ones_b, vv(sq[c], sq[c].ap[:, c0:c0 + n]), start=(c == 0), stop=(c == nch - 1))
                lv = vv(lnv, lnv.ap[:, 0:n])
                rv = vv(rstd, rstd.ap[:, 0:n])
                k.act(lv, pv, AF.Ln, bias=EPSC, scale=1.0 / div)
                k.act(rv, lv, AF.Exp, scale=-0.5)
                for c in range(nch):
                    k.stt(vv(dst[c], dst[c].ap[:, c0:c0 + n]), vv(src[c], src[c].ap[:, c0:c0 + n]), gain(gname, c), rv,
                          ALU.mult, ALU.mult)

        EPSC = k.sbv("epsc", [128, 1], F32)
        k.memset(EPSC, EPS)

        def ffn(pref, cgs):
            rmsnorm(xT, pref + "_norm", hT, cgs)
            wg, wu, wd = W[pref + "_wg"], W[pref + "_wu"], W[pref + "_wd"]
            for s0 in range(0, DFF, 512):
                w = min(512, DFF - s0)
                sg = slab(wg, 8, s0, w)
                su = slab(wu, 8, s0, w)
                for f in range(w // 128):
                    fc = s0 // 128 + f
                    for (c0, n) in cgs:
                        pu = fm_mm(su, 8, f * 128, 128, hT, c0, n)
                        pg = fm_mm(sg, 8, f * 128, 128, hT, c0, n)
                        st = sgt[fc % 2]
                        sv = vv(st, st.ap[:, 0:n])
                        k.act(sv, pg, AF.Silu)
                        k.tt(vv(Z[fc], Z[fc].ap[:, c0:c0 + n]), sv, pu, ALU.mult)
            for dc in range(8):
                sd = slab(wd, 22, dc * 128, 128)
                for (c0, n) in cgs:
                    pv = fm_mm(sd, 22, 0, 128, Z, c0, n)
                    xv = vv(xT[dc], xT[dc].ap[:, c0:c0 + n])
                    k.stt(xv, pv, 0.5, xv, ALU.mult, ALU.add)

        def fetch_x(j, with_s):
            for tc in range(4):
                r0 = (j * 4 + tc) * 128
                k.dma("sp", stg[tc].ap, x_p[r0:r0 + 128, :], writes=stg[tc].bufs)
            if with_s:
                k.dma("sp", stg[4].ap[0:64, :], x_s, writes=stg[4].bufs)

        def load_x(j, with_s, fetched):
            if not fetched:
                fetch_x(j, with_s)
            for c in range(8):
                p = ps()
                for tc in range(4):
                    k.tr(vv(p, p.ap[:, tc * 128:(tc + 1) * 128]), vv(stg[tc], stg[tc].ap[:, c * 128:(c + 1) * 128]),
                         ident_f)
                k.cpx(vv(xT[c], xT[c].ap[:, 0:512]), p)
            if with_s:
                for c in range(8):
                    p = ps()
                    k.tr(vv(p, p.ap[:, 0:64]), vv(stg[4], stg[4].ap[0:64, c * 128:(c + 1) * 128]),
                         vv(ident_f, ident_f.ap[0:64, 0:64]))
                    k.cpx(vv(xT[c], xT[c].ap[:, 512:576]), vv(p, p.ap[:, 0:64]))

        def final_out(j, cgs, with_s):
            yf_t = Z_t[:, 0:16, :].rearrange("p c n -> p (c n)").bitcast(F32).rearrange("p (c n) -> p c n", n=NCOL)
            yf = [V(yf_t[:, c, :], [Zb[2 * c], Zb[2 * c + 1]]) for c in range(8)]
            rmsnorm(xT, "final_norm", yf, cgs)
            nchunks = 5 if with_s else 4
            ystg = [V(Z_t[:, c0:c0 + 4, :].rearrange("p c n -> p (c n)")[:, 0:2048].bitcast(F32), Zb[c0:c0 + 4])
                    for c0 in (16, 20)]
            for tc in range(nchunks):
                M = 128 if tc < 4 else 64
                t0 = tc * 128
                yst = ystg[tc % 2]
                for half in range(2):
                    p = ps()
                    for q in range(4):
                        c = half * 4 + q
                        k.tr(vv(p, p.ap[0:M, q * 128:(q + 1) * 128]), vv(yf[c], yf[c].ap[:, t0:t0 + M]), ident_f)
                    k.cpx(vv(yst, yst.ap[0:M, half * 512:(half + 1) * 512]), vv(p, p.ap[0:M, :]))
                if tc < 4:
                    r0 = (j * 4 + tc) * 128
                    k.dma("sp", y_p[r0:r0 + 128, :], yst.ap, reads=yst.bufs, store=True)
                else:
                    k.dma("sp", y_s, yst.ap[0:64, :], reads=yst.bufs, store=True)

        def mem_kv():
            for mc in range(2):
                k.dma("sp", stg[mc].ap, mem[mc * 128:(mc + 1) * 128, :], writes=stg[mc].bufs)
            mT = [V(Z_t[:, c, 0:512].bitcast(F32), [Zb[c]]) for c in range(8)]
            for c in range(8):
                p = ps()
                for mc in range(2):
                    k.tr(vv(p, p.ap[:, mc * 128:(mc + 1) * 128]), vv(stg[mc], stg[mc].ap[:, c * 128:(c + 1) * 128]),
                         ident_f)
                k.cpx(mT[c], vv(p, p.ap[:, 0:256]))
            KSUB = 99
            if KSUB < 1:
                return
            mn = [vv(hT[c], hT[c].ap[:, 0:256]) for c in range(8)]
            rmsnorm(mT, "mem_norm", mn, [(0, 256)], sq=Z[24:32])
            if KSUB < 2:
                return
            for half in range(2):
                sk = slab(W["ca_wk"], 8, half * 512, 512)
                if KSUB < 3:
                    continue
                KMM = 4
                KCP = 1
                for f in range(min(4, KMM)):
                    pv = fm_mm(sk, 8, f * 128, 128, mn, 0, 256)
                    if KCP == 1:
                        k.cpx(vv(mkT_mem, mkT_mem.ap[:, half * 4 + f, :]), pv)
                    elif KCP == 2:
                        k.cp(vv(mkT_mem, mkT_mem.ap[:, half * 4 + f, :]), pv, eng="dve")
                    elif KCP == 3:
                        k.cp(vv(lnv, lnv.ap[:, 0:256]), pv, eng="act")
                if KSUB < 4:
                    continue
                for mc in range(2):
                    pv = tm_mm(sk, 8, 0, 512, mn, mc * 128, 128)
                    k.cpx(vv(stg[2 + mc], stg[2 + mc].ap[:, half * 512:(half + 1) * 512]), pv)
            if KSUB < 5:
                return
            for mc in range(2):
                k.dma("sp", mkp_o[mc * 128:(mc + 1) * 128, :], stg[2 + mc].ap, reads=stg[2 + mc].bufs, store=True)
            if KSUB < 6:
                return
            for half in range(2):
                sv = slab(W["ca_wv"], 8, half * 512, 512)
                for mc in range(2):
                    pv = tm_mm(sv, 8, 0, 512, mn, mc * 128, 128)
                    k.cp(vv(stg[mc], stg[mc].ap[:, half * 512:(half + 1) * 512]), pv, eng="act")
                    k.cp(vv(mv_mem, mv_mem.ap[:, mc, half * 512:(half + 1) * 512]), pv, eng="dve")
            for mc in range(2):
                k.dma("sp", mvp_o[mc * 128:(mc + 1) * 128, :], stg[mc].ap, reads=stg[mc].bufs, store=True)

        def mix(j, cgs, with_s):
            rmsnorm(xT, "mix_norm", hT, cgs)
            win = W["w_in"]
            ntc = 5 if with_s else 4
            tcs = [(tc, tc * 128, 128 if tc < 4 else 64) for tc in range(ntc)]
            SC = 128.0 ** -0.5
            for p_ in range(ntc):
                k.memset(vv(mv_tm[p_], mv_tm[p_].ap[:, :, 256:257]), 1.0, eng="dve")
            ncol = NCOL if with_s else 512

            def proj_mq():
                sl = slab(win, 8, MQ, 512)
                for h in range(4):
                    for (c0, n) in cgs:
                        pv = fm_mm(sl, 8, h * 128, 128, hT, c0, n)
                        k.cpx(vv(Z[ZMQ + h], Z[ZMQ + h].ap[:, c0:c0 + n]), pv)

            def proj_mk():
                sl = slab(win, 8, MK, 512)
                for h in range(4):
                    for (c0, n) in cgs:
                        pv = fm_mm(sl, 8, h * 128, 128, hT, c0, n)
                        k.act(vv(Z[ZMK + h], Z[ZMK + h].ap[:, c0:c0 + n]), pv, AF.Identity, scale=SC)
                for (tc, t0, M) in tcs:
                    pv = tm_mm(sl, 8, 0, 512, hT, t0, M)
                    k.ts(vv(mk_tm[tc], mk_tm[tc].ap[0:M, :]), pv, SC, ALU.mult)

            def proj_mv(half):
                sl = slab(win, 8, MV + half * 512, 512)
                for (tc, t0, M) in tcs:
                    pv = tm_mm(sl, 8, 0, 512, hT, t0, M)
                    k.cpx(vv(mv_tm[tc], mv_tm[tc].ap[0:M, half * 2:half * 2 + 2, 0:256]),
                          vv(pv, pv.ap.rearrange("p (h v) -> p h v", v=256)))

            def proj_mo(half):
                sl = slab(win, 8, MO + half * 512, 512)
                for f in range(4):
                    zc = Z[ZMO + half * 4 + f]
                    for (c0, n) in cgs:
                        pv = fm_mm(sl, 8, f * 128, 128, hT, c0, n)
                        k.act(vv(zc, zc.ap[:, c0:c0 + n]), pv, AF.Sigmoid)
                    k.ts(vv(zc, zc.ap[:, 0:ncol]), vv(zc, zc.ap[:, 0:ncol]), gain("mlstm_norm", half * 4 + f), ALU.mult)

            def proj_A():
                sl = slab(win, 8, MI, 520)
                for (c0, n) in cgs:
                    pv = fm_mm(sl, 8, 0, 4, hT, c0, n)
                    k.ts(vv(IG, IG.ap[:, c0:c0 + n]), pv, bi, ALU.add)
                    pv = fm_mm(sl, 8, 4, 4, hT, c0, n)
                    k.act(vv(E1, E1.ap[:, c0:c0 + n]), pv, AF.Exp, bias=negbf, scale=-1.0)
                for h in range(4):
                    for (c0, n) in cgs:
                        pv = fm_mm(sl, 8, 8 + h * 128, 128, hT, c0, n)
                        k.cpx(vv(Z[ZGQ + h], Z[ZGQ + h].ap[:, c0:c0 + n]), pv)

            def proj_gk():
                sl = slab(win, 8, GK, 512)
                for h in range(4):
                    for (c0, n) in cgs:
                        pv = fm_mm(sl, 8, h * 128, 128, hT, c0, n)
                        k.cpx(vv(Z[ZGK + h], Z[ZGK + h].ap[:, c0:c0 + n]), pv)

            def proj_gv(half):
                sl = slab(win, 8, GV + half * 512, 512)
                for (tc, t0, M) in tcs:
                    pv = tm_mm(sl, 8, 0, 512, hT, t0, M)
                    k.cpx(vv(gv_tm[tc], gv_tm[tc].ap[0:M, half * 2:half * 2 + 2, :]),
                          vv(pv, pv.ap.rearrange("p (h v) -> p h v", v=256)))

            def proj_gr(half):
                wdt = 512 if half == 0 else 528
                sl = slab(win, 8, GR + half * 512, wdt)
                if half == 1:
                    for (c0, n) in cgs:
                        pv = fm_mm(sl, 8, 512, 16, hT, c0, n)
                        k.cp(vv(gaT, gaT.ap[:, c0:c0 + n]), pv, eng="dve")
                for f in range(4):
                    zc = Z[ZGR + half * 4 + f]
                    for (c0, n) in cgs:
                        pv = fm_mm(sl, 8, f * 128, 128, hT, c0, n)
                        k.act(vv(zc, zc.ap[:, c0:c0 + n]), pv, AF.Silu)
                    k.ts(vv(zc, zc.ap[:, 0:ncol]), vv(zc, zc.ap[:, 0:ncol]), gain("gla_norm", half * 4 + f), ALU.mult)

            def prep_mlstm():
                k.act(vv(SPm, SPm.ap[:, 0:ncol]), vv(E1, E1.ap[:, 0:ncol]), AF.Ln, bias=ONEC4, scale=1.0)
                k.scan(vv(FNx, FNx.ap[:, 1:1 + ncol]), vv(ONES4, ONES4.ap[:, 0:1].to_broadcast([4, ncol])),
                       vv(SPm, SPm.ap[:, 0:ncol]), vv(FNx, FNx.ap[:, 0:1]), ALU.mult, ALU.add)
                k.tt(vv(GGt, GGt.ap[:, 0:ncol]), vv(IG, IG.ap[:, 0:ncol]), vv(FNx, FNx.ap[:, 1:1 + ncol]), ALU.add)

            def prep_gla(h):
                    for (c0, n) in cgs:
                        p = ps()
                        pv = vv(p, p.ap[:, 0:n])
                        k.mm(pv, vv(wa2_b, wa2_b.ap[:, h * 128:(h + 1) * 128]), vv(gaT, gaT.ap[:, c0:c0 + n]))
                        k.act(vv(Ge1, Ge1.ap[:, c0:c0 + n]), pv, AF.Exp, bias=vv(negba, negba.ap[:, h:h + 1]), scale=-1.0)
                    k.act(vv(Ge1, Ge1.ap[:, 0:ncol]), vv(Ge1, Ge1.ap[:, 0:ncol]), AF.Ln, bias=ONEC, scale=1.0)
                    k.ts(vv(Ge1, Ge1.ap[:, 0:ncol]), vv(Ge1, Ge1.ap[:, 0:ncol]), 1.0 / 16.0, ALU.mult)
                    k.cp(vv(BNx, BNx.ap[:, 0:1]), vv(carB, carB.ap[:, h:h + 1]), eng="dve")
                    k.scan(vv(BNx, BNx.ap[:, 1:1 + ncol]), vv(ones_c, ones_c.ap[:, 0:1].to_broadcast([128, ncol])),
                           vv(Ge1, Ge1.ap[:, 0:ncol]), vv(BNx, BNx.ap[:, 0:1]), ALU.mult, ALU.add)
                    k.cp(vv(carB, carB.ap[:, h:h + 1]), vv(BNx, BNx.ap[:, 512:513]), eng="dve")
                    k.ts(vv(NB5, NB5.ap[:, 0:5]), vv(BNx, BNx.ap[:, 0:513:128]), -1.0, ALU.mult)
                    gq, gk, kd = Z[ZGQ + h], Z[ZGK + h], Z[ZKD + h]
                    SCQ = 128.0 ** -0.5
                    for c in range(4):
                        a, b = c * 128, (c + 1) * 128
                        prevP = vv(BNx, BNx.ap[:, a:a + 1])
                        prevN = vv(NB5, NB5.ap[:, c:c + 1])
                        endN = vv(NB5, NB5.ap[:, c + 1:c + 2])
                        cur = vv(BNx, BNx.ap[:, a + 1:b + 1])
                        e0, e1, e2 = Eq3
                        k.act(e0, cur, AF.Exp, bias=endN, scale=1.0)
                        k.act(e1, cur, AF.Exp, bias=prevN, scale=1.0)
                        k.act(e2, cur, AF.Exp, bias=prevP, scale=-1.0)
                        k.tt(vv(kd, kd.ap[:, a:b]), vv(gk, gk.ap[:, a:b]), e0, ALU.mult)
                        k.tt(vv(gk, gk.ap[:, a:b]), vv(gk, gk.ap[:, a:b]), e1, ALU.mult)
                        k.stt(vv(gq, gq.ap[:, a:b]), vv(gq, gq.ap[:, a:b]), SCQ, e2, ALU.mult, ALU.mult)
                        k.act(vv(DEC, DEC.ap[:, h, c:c + 1]), vv(BNx, BNx.ap[:, b:b + 1]), AF.Exp, bias=prevP, scale=-1.0)
                    if with_s:
                        a, b = 512, 576
                        cur3 = BNx.ap[:, a + 1:b + 1].rearrange("p (i t) -> p i t", t=4)
                        prev3 = BNx.ap[:, a:b].rearrange("p (i t) -> p i t", t=4)[:, :, 0:1].to_broadcast([128, NS, 4])
                        end3 = BNx.ap[:, a + 1:b + 1].rearrange("p (i t) -> p i t", t=4)[:, :, 3:4].to_broadcast(
                            [128, NS, 4])
                        ea = vv(Earg, Earg.ap.rearrange("p (i t) -> p i t", t=4))
                        e = vv(Eq3[0], Eq3[0].ap[:, 0:64])
                        k.tt(ea, vv(BNx, cur3), vv(BNx, end3), ALU.subtract)
                        k.act(e, Earg, AF.Exp)
                        k.tt(vv(kd, kd.ap[:, a:b]), vv(gk, gk.ap[:, a:b]), e, ALU.mult)
                        k.tt(ea, vv(BNx, cur3), vv(BNx, prev3), ALU.subtract)
                        k.act(e, Earg, AF.Exp)
                        k.tt(vv(gk, gk.ap[:, a:b]), vv(gk, gk.ap[:, a:b]), e, ALU.mult)
                        k.act(e, Earg, AF.Exp, scale=-1.0)
                        k.stt(vv(gq, gq.ap[:, a:b]), vv(gq, gq.ap[:, a:b]), SCQ, e, ALU.mult, ALU.mult)
                        k.act(vv(DEC, DEC.ap[:, h, 4:4 + NS]), vv(Earg, Earg.ap.rearrange('p (i t) -> p i t', t=4)[:, :, 3]), AF.Exp, scale=-1.0)

            MARKS.append(("mix_proj", j, k.eng["pe"].cnt))
            proj_A()
            proj_gr(1)
            proj_gk()
            prep_mlstm()
            prep_gla(0)
            proj_gr(0)
            proj_mq()
            prep_gla(1)
            proj_mk()
            proj_mv(0)
            prep_gla(2)
            proj_mv(1)
            proj_mo(0)
            prep_gla(3)
            proj_mo(1)
            proj_gv(0)
            proj_gv(1)
            MARKS.append(("mix_prep", j, k.eng["pe"].cnt))

            groups = [dict(g0=c * 128, NP=128, tc=c, owners=[dict(c0=0, n=128, kind="p", dcol=c)],
                           mbig=maskbigPb, m01=mask01P) for c in range(4)]
            if with_s:
                groups.append(dict(g0=512, NP=64, tc=4, owners=[dict(c0=4 * i, n=4, kind="s", i=i, dcol=4 + i)
                                                                  for i in range(NS)],
                                   mbig=maskbigSb, m01=mask01S))
            def prologue(gi_, g, ctx):
                g0, NP, tc = g["g0"], g["NP"], g["tc"]
                gc = slice(g0, g0 + NP)
                GL, MG, AE = GL2[gi_ % 3], MG2[gi_ % 3], AE2[gi_ % 3]
                lc = slice(0, NP)
                if NP == 128:
                    fprev = vv(FNx, FNx.ap[:, g0:g0 + 1])
                    k.ts(vv(GL, GL.ap[:, lc]), vv(GGt, GGt.ap[:, gc]), fprev, ALU.subtract)
                    k.scan(vv(MG, MG.ap[:, lc]), vv(GL, GL.ap[:, lc]), vv(GL, GL.ap[:, lc]), mcar, ALU.max, ALU.max)
                    k.act(vv(AE, AE.ap[:, 0, lc]), vv(MG, MG.ap[:, lc]), AF.Exp, bias=mcar, scale=-1.0)
                    k.stt(vv(TMPm, TMPm.ap[:, lc]), vv(FNx, FNx.ap[:, g0 + 1:g0 + 1 + NP]), fprev,
                          vv(MG, MG.ap[:, lc]), ALU.subtract, ALU.subtract)
                    k.act(vv(AE, AE.ap[:, 1, lc]), vv(TMPm, TMPm.ap[:, lc]), AF.Exp)
                    k.ts(mcar, vv(TMPm, TMPm.ap[:, NP - 1:NP]), -1.0, ALU.mult)
                else:
                    def v3(t, off=0):
                        if t is FNx or t is GGt:
                            return t.ap[:, g0 + off:g0 + off + NP].rearrange("p (i t) -> p i t", t=4)
                        return t.ap[:, 0:NP].rearrange("p (i t) -> p i t", t=4)
                    fprev3 = FNx.ap[:, g0:g0 + NP].rearrange("p (i t) -> p i t", t=4)[:, :, 0:1].to_broadcast(
                        [4, NS, 4])
                    k.tt(vv(GL, v3(GL)), vv(GGt, v3(GGt)), vv(FNx, fprev3), ALU.subtract)
                    k.tt(vv(MG, v3(MG)[:, :, 0]), vv(GL, v3(GL)[:, :, 0]), m0T, ALU.max)
                    for t in range(1, 4):
                        k.tt(vv(MG, v3(MG)[:, :, t]), vv(GL, v3(GL)[:, :, t]), vv(MG, v3(MG)[:, :, t - 1]), ALU.max)
                    m03 = m0T.ap.rearrange("p (i o) -> p i o", o=1).to_broadcast([4, NS, 4])
                    k.tt(vv(TMPm, v3(TMPm)), vv(m0T, m03), vv(MG, v3(MG)), ALU.subtract)
                    k.act(vv(AE, AE.ap[:, 0, lc]), vv(TMPm, TMPm.ap[:, lc]), AF.Exp)
                    k.tt(vv(TMPm, v3(TMPm)), vv(FNx, v3(FNx, 1)), vv(FNx, fprev3), ALU.subtract)
                    k.tt(vv(TMPm, TMPm.ap[:, lc]), vv(TMPm, TMPm.ap[:, lc]), vv(MG, MG.ap[:, lc]), ALU.subtract)
                    k.act(vv(AE, AE.ap[:, 1, lc]), vv(TMPm, TMPm.ap[:, lc]), AF.Exp)
                    k.ts(msT, vv(TMPm, v3(TMPm)[:, :, 3]), -1.0, ALU.mult)
                gt = P_GT.get()
                p = ps_try()
                assert p is not None
                k.tr(vv(p, p.ap[0:NP, 0:4]), vv(GL, GL.ap[:, lc]), vv(ident_f, ident_f.ap[0:4, 0:4]))
                k.cp(vv(gt, gt.ap[0:NP, :]), vv(p, p.ap[0:NP, 0:4]), eng="act")
                psfree(p)
                identNP = vv(ident_f, ident_f.ap[0:NP, 0:NP])
                onesNP = vv(ones_b, ones_b.ap[0:NP, :])
                first = (j == 0 and g0 == 0)
                mgh, aeh = MGH[gi_ % 3], AEH[gi_ % 3]
                k.cp(vv(mgh, mgh.ap[:, 0, lc]), vv(MG, MG.ap[:, lc]), eng="dve")
                k.tt(vv(mgh, mgh.ap[:, 1, lc]), vv(MG, MG.ap[:, lc]), vv(mgh, mgh.ap[:, 0, lc]), ALU.subtract)
                k.cp(vv(aeh, aeh.ap[:, 0, :, lc]), vv(AE, AE.ap[:, :, lc]), eng="dve")
                k.tt(vv(aeh, aeh.ap[:, 1, :, lc]), vv(AE, AE.ap[:, :, lc]), vv(aeh, aeh.ap[:, 0, :, lc]), ALU.subtract)
                ctx.update(dict(g=g, g0=g0, NP=NP, tc=tc, gc=gc, lc=lc, gt=gt, identNP=identNP, onesNP=onesNP,
                                first=first, GL=GL, MG=MG, AE=AE, MGH=mgh, AEH=aeh, ready=True))

            def pro_gen(gi_, g, ctx):
                while len(busy) >= NPS:
                    yield
                prologue(gi_, g, ctx)
                return
                yield

            def wait_ready(ctx, gen_fn, h):
                while not ctx.get("ready"):
                    yield
                yield from gen_fn(ctx, h)

            gens = []
            sample_ctx = None
            for gi_, g in enumerate(groups):
                ctx = {}
                if g["NP"] == 128:
                    gens.append(pro_gen(gi_, g, ctx))
                    for h in range(4):
                        gens.append(wait_ready(ctx, gh_prompt_gen, h))
                else:
                    sample_ctx = (gi_, g, ctx)
            run_interleaved(gens, WH + 1)
            if sample_ctx is not None:
                gi_, g, ctx = sample_ctx
                prologue(gi_, g, ctx)
                CSB[:] = borrow_slot((k.slot_i + 2) % NSLOT, True)
                SSB[:] = borrow_slot((k.slot_i + 3) % NSLOT, False)
                for h in range(4):
                    run_interleaved([gh_sample_gen(ctx, h)], 1)
            MARKS.append(("merge", j, k.eng["pe"].cnt))
            HMc = [Z[ZMQ + hh] if vc == 0 else Z[ZMK + hh] for hh in range(4) for vc in range(2)]
            HGc = [Z[ZGQ + hh] if vc == 0 else Z[ZGK + hh] for hh in range(4) for vc in range(2)]
            yacc_t = Z_t[:, ZMO:ZMO + 8, :].rearrange("p c n -> p (c n)").bitcast(F32).rearrange("p (c n) -> p c n", n=NCOL)
            yacc = [V(yacc_t[:, q, :], [Zb[ZMO + 2 * q], Zb[ZMO + 2 * q + 1]]) for q in range(4)]
            yT = [Z[ZGR + q] for q in range(8)]
            for half in range(2):
                sgm = slab(win, 8, GM + half * 512, 512)
                sbm = slab(W["w_br_m"], 8, half * 512, 512)
                for q in range(4):
                    for (c0, n) in cgs:
                        pg = fm_mm(sgm, 8, q * 128, 128, hT, c0, n)
                        pm = fm_mm(sbm, 8, q * 128, 128, HMc, c0, n)
                        t = mrg[0]
                        k.act(vv(t, t.ap[:, 0:n]), pg, AF.Sigmoid)
                        k.tt(vv(yacc[q], yacc[q].ap[:, c0:c0 + n]), vv(t, t.ap[:, 0:n]), pm, ALU.mult)
                sgg = slab(win, 8, GG + half * 512, 512)
                sbg = slab(W["w_br_g"], 8, half * 512, 512)
                for q in range(4):
                    for (c0, n) in cgs:
                        pg = fm_mm(sgg, 8, q * 128, 128, hT, c0, n)
                        pm = fm_mm(sbg, 8, q * 128, 128, HGc, c0, n)
                        t = mrg[1]
                        k.act(vv(t, t.ap[:, 0:n]), pg, AF.Sigmoid)
                        k.tt(vv(t, t.ap[:, 0:n]), vv(t, t.ap[:, 0:n]), pm, ALU.mult)
                        k.tt(vv(yT[half * 4 + q], yT[half * 4 + q].ap[:, c0:c0 + n]), vv(t, t.ap[:, 0:n]),
                             vv(yacc[q], yacc[q].ap[:, c0:c0 + n]), ALU.add)
            for half in range(2):
                so = slab(W["w_out"], 8, half * 512, 512)
                for q in range(4):
                    dc = half * 4 + q
                    for (c0, n) in cgs:
                        pv = fm_mm(so, 8, q * 128, 128, yT, c0, n)
                        xv = vv(xT[dc], xT[dc].ap[:, c0:c0 + n])
                        k.tt(xv, pv, xv, ALU.add)

        WH = 4
        PEN = "pool"
        busy = set()

        def ps_try():
            for _ in range(NPS):
                idx = k.ps_i % NPS
                k.ps_i += 1
                if idx not in busy and idx not in pinned:
                    busy.add(idx)
                    return PS[idx]
            return None

        def psw():
            while True:
                p = ps_try()
                if p is not None:
                    return p
                yield

        def psfree(p):
            busy.discard(PS.index(p))

        def run_interleaved(gens, width):
            it = iter(gens)
            active = []
            rounds = 0
            while True:
                while len(active) < width:
                    try:
                        active.append(next(it))
                    except StopIteration:
                        break
                if not active:
                    break
                for gn in list(active):
                    try:
                        next(gn)
                    except StopIteration:
                        active.remove(gn)
                rounds += 1
                assert rounds < 100000, "interleave livelock (PSUM banks exhausted?)"

        def inter_gen(gens, width):
            it = iter(gens)
            active = []
            while True:
                while len(active) < width:
                    try:
                        active.append(next(it))
                    except StopIteration:
                        break
                if not active:
                    return
                for gn in list(active):
                    try:
                        next(gn)
                    except StopIteration:
                        active.remove(gn)
                yield

        def hn_gen(HRv, NP, gname, h, gate, dst, g0):
            SQ = P_SQ.get()
            SQv = vv(SQ, SQ.ap[:, :, 0:NP])
            k.act(SQv, HRv, AF.Square)
            p = yield from psw()
            pv = vv(p, p.ap[:, 0:NP])
            for vc in range(2):
                k.mm(pv, ones_b, vv(SQ, SQ.ap[:, vc, 0:NP]), start=(vc == 0), stop=(vc == 1))
            yield
            LR = P_LR.get()
            lv = vv(LR, LR.ap[:, 0, 0:NP])
            rv = vv(LR, LR.ap[:, 1, 0:NP])
            k.act(lv, pv, AF.Ln, bias=EPSC, scale=1.0 / 256.0)
            psfree(p)
            k.act(rv, lv, AF.Exp, scale=-0.5)
            T1 = P_T1.get()
            T1v = vv(T1, T1.ap[:, :, 0:NP])
            k.tt(T1v, HRv, vv(LR, LR.ap[:, 1:2, 0:NP].to_broadcast([128, 2, NP])), ALU.mult)
            dsti, gatei = dst, gate
            dst3 = V(Z_t[:, dsti:dsti + 5:4, g0:g0 + NP], [Zb[dsti], Zb[dsti + 4]])
            gate3 = V(Z_t[:, gatei:gatei + 2, g0:g0 + NP], [Zb[gatei], Zb[gatei + 1]])
            k.tt(dst3, T1v, gate3, ALU.mult)

        def gh_common_front(ctx, h):
            g, g0, NP, tc, gc, lc, gt = ctx["g"], ctx["g0"], ctx["NP"], ctx["tc"], ctx["gc"], ctx["lc"], ctx["gt"]
            GL, MG, AE = ctx["GL"], ctx["MG"], ctx["AE"]
            mq, mkk = Z[ZMQ + h], Z[ZMK + h]
            pST = yield from psw()
            STv = vv(pST, pST.ap[0:NP, 0:NP])
            k.mm(STv, vv(mkk, mkk.ap[:, gc]), vv(mq, mq.ap[:, gc]))
            pMb = yield from psw()
            Mbv = vv(pMb, pMb.ap[0:NP, 0:NP])
            mgh, aeh = ctx["MGH"], ctx["AEH"]
            k.mm(Mbv, vv(sel, sel.ap[:, h, 0:NP]), vv(mgh, mgh.ap[:, 0, lc]), start=True, stop=False)
            k.mm(Mbv, vv(sel, sel.ap[:, h, 0:NP]), vv(mgh, mgh.ap[:, 1, lc]), start=False, stop=False)
            k.mm(Mbv, vv(ident_b, ident_b.ap[0:NP, 0:NP]), g["mbig"], start=False, stop=True)
            pAB = yield from psw()
            ABp = vv(pAB, pAB.ap[:, 0:2 * NP].rearrange("p (a n) -> p a n", n=NP))
            k.mm(ABp, vv(sel, sel.ap[:, h, :]), vv(aeh, aeh.ap[:, 0, :, lc]), start=True, stop=False)
            k.mm(ABp, vv(sel, sel.ap[:, h, :]), vv(aeh, aeh.ap[:, 1, :, lc]), start=False, stop=True)
            yield
            WT = P_WT.get()
            WTv = vv(WT, WT.ap[0:NP, 0:NP])
            k.act(WTv, Mbv, AF.Exp, bias=vv(gt, gt.ap[0:NP, h:h + 1]), scale=-1.0)
            psfree(pMb)
            AB = P_AB.get()
            k.cp(vv(AB, AB.ap[:, :, 0:NP]), ABp, eng="act")
            psfree(pAB)
            PT = P_PT.get()
            PTv = vv(PT, PT.ap[0:NP, 0:NP])
            k.tt(PTv, WTv, STv, ALU.mult)
            psfree(pST)
            QA = P_QA.get()
            k.tt(vv(QA, QA.ap[:, 0:NP]), vv(mq, mq.ap[:, gc]), vv(AB, AB.ap[:, 0, 0:NP]), ALU.mult, eng=PEN)
            return dict(WT=WT, AB=AB, PT=PT, PTv=PTv, QA=QA)

        def gla_front(ctx, h):
            g, g0, NP, gc = ctx["g"], ctx["g0"], ctx["NP"], ctx["gc"]
            gq, gk, kd = Z[ZGQ + h], Z[ZGK + h], Z[ZKD + h]
            pA = yield from psw()
            Av = vv(pA, pA.ap[0:NP, 0:NP])
            k.mm(Av, vv(gk, gk.ap[:, gc]), vv(gq, gq.ap[:, gc]))
            pK32 = yield from psw()
            pK = V(pK32.ap.bitcast(BF16), pK32.bufs)
            k.tr(vv(pK, pK.ap[0:NP, 0:128]), vv(kd, kd.ap[:, gc]), ident_b)
            yield
            PT2 = P_PT.get()
            PT2v = vv(PT2, PT2.ap[0:NP, 0:NP])
            k.tt(PT2v, Av, g["m01"], ALU.mult)
            psfree(pA)
            KD = P_KD.get()
            k.cp(vv(KD, KD.ap[0:NP, :]), vv(pK, pK.ap[0:NP, 0:128]), eng="act")
            psfree(pK32)
            return dict(PT2v=PT2v, KD=KD)

        def gh_prompt_gen(ctx, h):
            g, g0, NP, tc, gc, lc = ctx["g"], ctx["g0"], ctx["NP"], ctx["tc"], ctx["gc"], ctx["lc"]
            first = ctx["first"]
            mq = Z[ZMQ + h]
            o = g["owners"][0]
            last = NP - 1
            fr = yield from gh_common_front(ctx, h)
            WT, AB, PTv, QA = fr["WT"], fr["AB"], fr["PTv"], fr["QA"]
            QAv = vv(QA, QA.ap[:, 0:NP])
            C = Cst[h]
            nT = vv(nTp, nTp.ap[:, h:h + 1])
            QN = P_QN.get()
            if not first:
                pC = yield from psw()
                for vc in range(2):
                    k.tr(vv(pC, pC.ap[:, vc * 128:(vc + 1) * 128]), vv(C, C.ap[:, vc, :]), ident_f)
                k.ts(vv(QN, QN.ap[:, 0:NP]), QAv, nT, ALU.mult, eng=PEN)
                yield
                ctb = P_CTB.get()
                k.cp(ctb, vv(pC, pC.ap[:, 0:256]), eng="act")
                psfree(pC)
            KW = P_KW.get()
            KWv = vv(KW, KW.ap[0:NP, :])
            k.ts(KWv, vv(mk_tm[tc], mk_tm[tc].ap[0:NP, h * 128:(h + 1) * 128]), vv(WT, WT.ap[0:NP, last:last + 1]),
                 ALU.mult, eng=PEN)
            pN = yield from psw()
            for vc in range(2):
                nv = vv(pN, pN.ap[:, vc * NP:(vc + 1) * NP])
                k.mm(nv, vv(mv_tm[tc], mv_tm[tc].ap[0:NP, h, vc * 128:(vc + 1) * 128]), PTv, start=True, stop=first)
                if not first:
                    k.mm(nv, vv(ctb, ctb.ap[:, vc * 128:(vc + 1) * 128]), QAv, start=False, stop=True)
            Dv = vv(pN, pN.ap[:, 2 * NP:3 * NP])
            k.mm(Dv, ctx["onesNP"], PTv, start=True, stop=first)
            if not first:
                k.mm(Dv, ones_b, vv(QN, QN.ap[:, 0:NP]), start=False, stop=True)
            pU = yield from psw()
            for vc in range(2):
                k.mm(vv(pU, pU.ap[:, vc * 128:(vc + 1) * 128]),
                     vv(mv_tm[tc], mv_tm[tc].ap[0:NP, h, vc * 128:(vc + 1) * 128]), KWv)
            k.mm(vv(pU, pU.ap[:, 256:257]), KWv, vv(ones_b, ones_b.ap[0:NP, 0:1]))
            yield
            HD = P_HD.get()
            k.act(vv(HD, HD.ap[:, 0:NP]), Dv, AF.Abs)
            k.tt(vv(HD, HD.ap[:, 0:NP]), vv(HD, HD.ap[:, 0:NP]), vv(AB, AB.ap[:, 1, 0:NP]), ALU.max)
            RD = P_RD.get()
            RDv = vv(RD, RD.ap[:, 0:NP])
            k.act(vv(HD, HD.ap[:, 0:NP]), vv(HD, HD.ap[:, 0:NP]), AF.Ln)
            k.act(RDv, vv(HD, HD.ap[:, 0:NP]), AF.Exp, scale=-1.0)
            HR = P_HR.get()
            HRv = vv(HR, HR.ap[:, :, 0:NP])
            k.tt(HRv, vv(pN, pN.ap[:, 0:2 * NP].rearrange("p (a n) -> p a n", n=NP)),
                 vv(RD, RD.ap[:, 0:NP].rearrange("p (a n) -> p a n", a=1).to_broadcast([128, 2, NP])), ALU.mult)
            psfree(pN)
            aend = vv(AB, AB.ap[:, 0, last:last + 1])
            Cf = vv(C, C.ap.rearrange("p a d -> p (a d)"))
            k.stt(Cf, Cf, aend, vv(pU, pU.ap[:, 0:256]), ALU.mult, ALU.add)
            k.stt(nT, nT, aend, vv(pU, pU.ap[:, 256:257]), ALU.mult, ALU.add)
            psfree(pU)
            yield
            yield from hn_gen(HRv, NP, "mlstm_norm", h, ZMO + 2 * h, ZMQ + h, g0)
            gq = Z[ZGQ + h]
            gf = yield from gla_front(ctx, h)
            PT2v, KD = gf["PT2v"], gf["KD"]
            S = Sst[h]
            if not first:
                sbf = P_SBF.get()
                k.cp(sbf, S, eng="act")
            pO = yield from psw()
            for vc in range(2):
                ov = vv(pO, pO.ap[:, vc * NP:(vc + 1) * NP])
                k.mm(ov, vv(gv_tm[tc], gv_tm[tc].ap[0:NP, h, vc * 128:(vc + 1) * 128]), PT2v, start=True, stop=first)
                if not first:
                    k.mm(ov, vv(sbf, sbf.ap[:, vc * 128:(vc + 1) * 128]), vv(gq, gq.ap[:, gc]), start=False, stop=True)
            pU = yield from psw()
            k.mm(vv(pU, pU.ap[:, 0:256]), vv(KD, KD.ap[0:NP, :]), vv(gv_tm[tc], gv_tm[tc].ap[0:NP, h, :]))
            yield
            HR2 = P_HR.get()
            HR2v = vv(HR2, HR2.ap[:, :, 0:NP])
            k.cp(HR2v, vv(pO, pO.ap[:, 0:2 * NP].rearrange("p (a n) -> p a n", n=NP)), eng="act")
            psfree(pO)
            k.stt(S, S, vv(DEC, DEC.ap[:, h, o["dcol"]:o["dcol"] + 1]), vv(pU, pU.ap[:, 0:256]), ALU.mult, ALU.add)
            psfree(pU)
            yield
            yield from hn_gen(HR2v, NP, "gla_norm", h, ZGR + 2 * h, ZGQ + h, g0)

        sidx = [0]
        CSB = []
        SSB = []
        PFD = 4
        OWW = 4

        def gh_sample_gen(ctx, h):
            g, g0, NP, tc, gc, lc = ctx["g"], ctx["g0"], ctx["NP"], ctx["tc"], ctx["gc"], ctx["lc"]
            owners = g["owners"]
            nown = len(owners)
            fr = yield from gh_common_front(ctx, h)
            WT, AB, PTv, QA = fr["WT"], fr["AB"], fr["PTv"], fr["QA"]
            QN = P_QN.get()
            pN = [(yield from psw()), (yield from psw())]
            pD = yield from psw()
            pDN = yield from psw()
            Dv = vv(pD, pD.ap[:, 0:NP])
            for vc in range(2):
                k.mm(vv(pN[vc], pN[vc].ap[:, 0:NP]), vv(mv_tm[tc], mv_tm[tc].ap[0:NP, h, vc * 128:(vc + 1) * 128]), PTv,
                     start=True, stop=False)
            k.mm(Dv, ctx["onesNP"], PTv, start=True, stop=False)

            def load_C(oj):
                Cj = CSB[oj % 8]
                k.dma("sp", Cj.ap, Cs_in[owners[oj]["i"], h].rearrange("(c p) d -> p c d", p=128), writes=Cj.bufs)

            def load_S(oj):
                Sj = SSB[oj % 8]
                k.dma("sp", Sj.ap, Ss_in[owners[oj]["i"], h], writes=Sj.bufs)

            for oj in range(PFD):
                load_C(oj)

            def own_m(oi, o):
                i = o["i"]
                oc = slice(o["c0"], o["c0"] + o["n"])
                last = o["c0"] + o["n"] - 1
                C = CSB[oi % 8]
                if oi + PFD < nown:
                    load_C(oi + PFD)
                nT = vv(nTs, nTs.ap[:, h * NS + i:h * NS + i + 1])
                pC = yield from psw()
                for vc in range(2):
                    k.tr(vv(pC, pC.ap[:, vc * 128:(vc + 1) * 128]), vv(C, C.ap[:, vc, :]), ident_f)
                k.ts(vv(QN, QN.ap[:, oc]), vv(QA, QA.ap[:, oc]), nT, ALU.mult, eng=PEN)
                KW = P_KW.get()
                KWv = vv(KW, KW.ap[0:NP, :])
                k.ts(KWv, vv(mk_tm[tc], mk_tm[tc].ap[0:NP, h * 128:(h + 1) * 128]), vv(WT, WT.ap[0:NP, last:last + 1]),
                     ALU.mult, eng=PEN)
                yield
                ctb = P_CTB.get()
                k.cp(ctb, vv(pC, pC.ap[:, 0:256]), eng="act")
                for vc in range(2):
                    k.mm(vv(pN[vc], pN[vc].ap[:, oc]), vv(ctb, ctb.ap[:, vc * 128:(vc + 1) * 128]), vv(QA, QA.ap[:, oc]),
                         start=False, stop=(oi == nown - 1))
                for vc in range(2):
                    k.mm(vv(pC, pC.ap[:, 256 + vc * 128:256 + (vc + 1) * 128]),
                         vv(mv_tm[tc], mv_tm[tc].ap[0:NP, h, vc * 128:(vc + 1) * 128]), KWv)
                k.mm(vv(pDN, pDN.ap[:, oi:oi + 1]), KWv, vv(ones_b, ones_b.ap[0:NP, 0:1]))
                yield
                aend = vv(AB, AB.ap[:, 0, last:last + 1])
                Cf = vv(C, C.ap.rearrange("p a d -> p (a d)"))
                k.stt(Cf, Cf, aend, vv(pC, pC.ap[:, 256:512]), ALU.mult, ALU.add)
                psfree(pC)
                k.dma(STQ, Cs_o[i, h].rearrange("(c p) d -> p c d", p=128), C.ap, reads=C.bufs, store=True)

            for oj in range(PFD):
                load_S(oj)
            yield from inter_gen([own_m(oi, o) for oi, o in enumerate(owners)], OWW)
            k.mm(Dv, ones_b, vv(QN, QN.ap[:, 0:NP]), start=False, stop=True)
            nTh = vv(nTs, nTs.ap[:, h * NS:(h + 1) * NS])
            aend16 = vv(AB, AB.ap[:, 0, 0:NP].rearrange("p (i t) -> p i t", t=4)[:, :, 3])
            k.tt(nTh, nTh, aend16, ALU.mult)
            k.tt(nTh, nTh, vv(pDN, pDN.ap[:, 0:NS]), ALU.add)
            psfree(pDN)
            HD = P_HD.get()
            k.act(vv(HD, HD.ap[:, 0:NP]), Dv, AF.Abs)
            psfree(pD)
            k.tt(vv(HD, HD.ap[:, 0:NP]), vv(HD, HD.ap[:, 0:NP]), vv(AB, AB.ap[:, 1, 0:NP]), ALU.max)
            RD = P_RD.get()
            RDv = vv(RD, RD.ap[:, 0:NP])
            k.act(vv(HD, HD.ap[:, 0:NP]), vv(HD, HD.ap[:, 0:NP]), AF.Ln)
            k.act(RDv, vv(HD, HD.ap[:, 0:NP]), AF.Exp, scale=-1.0)
            HR = P_HR.get()
            HRv = vv(HR, HR.ap[:, :, 0:NP])
            for vc in range(2):
                k.tt(vv(HR, HR.ap[:, vc, 0:NP]), vv(pN[vc], pN[vc].ap[:, 0:NP]), RDv, ALU.mult)
                psfree(pN[vc])
            yield from hn_gen(HRv, NP, "mlstm_norm", h, ZMO + 2 * h, ZMQ + h, g0)
            gq = Z[ZGQ + h]
            gf = yield from gla_front(ctx, h)
            PT2v, KD = gf["PT2v"], gf["KD"]
            pO = [(yield from psw()), (yield from psw())]
            for vc in range(2):
                k.mm(vv(pO[vc], pO[vc].ap[:, 0:NP]), vv(gv_tm[tc], gv_tm[tc].ap[0:NP, h, vc * 128:(vc + 1) * 128]),
                     PT2v, start=True, stop=False)

            def own_g(oi, o):
                i = o["i"]
                S = SSB[oi % 8]
                if oi + PFD < nown:
                    load_S(oi + PFD)
                KW = P_KW.get()
                KDo = vv(KW, KW.ap[0:NP, :])
                k.ts(KDo, vv(KD, KD.ap[0:NP, :]), vv(ind, ind.ap[:, i:i + 1]), ALU.mult, eng=PEN)
                pU = yield from psw()
                k.mm(vv(pU, pU.ap[:, 0:256]), KDo, vv(gv_tm[tc], gv_tm[tc].ap[0:NP, h, :]))
                yield
                sbf = P_SBF.get()
                k.cp(sbf, S, eng="act")
                for vc in range(2):
                    k.mm(vv(pO[vc], pO[vc].ap[:, o["c0"]:o["c0"] + o["n"]]), vv(sbf, sbf.ap[:, vc * 128:(vc + 1) * 128]),
                         vv(gq, gq.ap[:, g0 + o["c0"]:g0 + o["c0"] + o["n"]]), start=False, stop=(oi == nown - 1))
                yield
                k.stt(S, S, vv(DEC, DEC.ap[:, h, o["dcol"]:o["dcol"] + 1]), vv(pU, pU.ap[:, 0:256]), ALU.mult, ALU.add)
                psfree(pU)
                k.dma(STQ, Ss_o[i, h], S.ap, reads=S.bufs, store=True)

            yield from inter_gen([own_g(oi, o) for oi, o in enumerate(owners)], OWW)
            HR2 = P_HR.get()
            HR2v = vv(HR2, HR2.ap[:, :, 0:NP])
            for vc in range(2):
                k.cp(vv(HR2, HR2.ap[:, vc, 0:NP]), vv(pO[vc], pO[vc].ap[:, 0:NP]), eng="act")
                psfree(pO[vc])
            yield from hn_gen(HR2v, NP, "gla_norm", h, ZGR + 2 * h, ZGQ + h, g0)

        def head_norm(HRv, NP, gname, h, gate, dst, g0, gla_h=None):
            if dst is None:
                dst = [Z[ZGQ + gla_h], Z[ZGK + gla_h]]
            SQ = P_SQ.get()
            SQv = vv(SQ, SQ.ap[:, :, 0:NP])
            k.act(SQv, HRv, AF.Square)
            p = ps()
            pv = vv(p, p.ap[:, 0:NP])
            for vc in range(2):
                k.mm(pv, ones_b, vv(SQ, SQ.ap[:, vc, 0:NP]), start=(vc == 0), stop=(vc == 1))
            lv = vv(lnv, lnv.ap[:, 0:NP])
            rv = vv(rstd, rstd.ap[:, 0:NP])
            k.act(lv, pv, AF.Ln, bias=EPSC, scale=1.0 / 256.0)
            k.act(rv, lv, AF.Exp, scale=-0.5)
            T1 = P_T1.get()
            for vc in range(2):
                k.stt(vv(T1, T1.ap[:, vc, 0:NP]), vv(HRv, HRv.ap[:, vc, :]), gain(gname, h * 2 + vc), rv, ALU.mult,
                      ALU.mult)
                k.tt(vv(dst[vc], dst[vc].ap[:, g0:g0 + NP]), vv(T1, T1.ap[:, vc, 0:NP]),
                     vv(gate[vc], gate[vc].ap[:, g0:g0 + NP]), ALU.mult)

        def cross_attn(j, cgs, with_s):
            rmsnorm(xT, "ca_norm", hT, cgs)
            qT = Z[0:8]
            aT = Z[8:16]
            for half in range(2):
                sq = slab(W["ca_wq"], 8, half * 512, 512)
                for f in range(4):
                    for (c0, n) in cgs:
                        pv = fm_mm(sq, 8, f * 128, 128, hT, c0, n)
                        k.act(vv(qT[half * 4 + f], qT[half * 4 + f].ap[:, c0:c0 + n]), pv, AF.Identity, scale=1.0 / 16.0)
            for h in range(4):
                ET = P_ET.get()
                for mc in range(2):
                    p = ps()
                    for c in range(2):
                        k.mm(p, vv(mkT_mem, mkT_mem.ap[:, h * 2 + c, mc * 128:(mc + 1) * 128]),
                             vv(qT[h * 2 + c], qT[h * 2 + c].ap[:, 0:512]), start=(c == 0), stop=(c == 1))
                    k.act(vv(ET, ET.ap[:, mc, :]), p, AF.Exp)
                pd = ps()
                for mc in range(2):
                    k.mm(pd, ones_b, vv(ET, ET.ap[:, mc, :]), start=(mc == 0), stop=(mc == 1))
                k.act(lnv, pd, AF.Ln)
                k.act(rstd, lnv, AF.Exp, scale=-1.0)
                for c in range(2):
                    p = ps()
                    for mc in range(2):
                        k.mm(p, vv(mv_mem, mv_mem.ap[:, mc, h * 256 + c * 128:h * 256 + (c + 1) * 128]),
                             vv(ET, ET.ap[:, mc, :]), start=(mc == 0), stop=(mc == 1))
                    k.tt(vv(aT[h * 2 + c], aT[h * 2 + c].ap[:, 0:512]), p, rstd, ALU.mult)
            MARKS.append(("ca_s", j, k.eng["pe"].cnt))
            so_pre = None
            if with_s:
                so_pre = [slab(W["ca_wo"], 8, half * 512, 512) for half in range(2)]
                halves = [V(TM_t[:, q, 0:2048].bitcast(F32), [TMb[q]]) for q in range(4)]
                halves += [V(Z_t[:, c0:c0 + 4, :].rearrange("p c n -> p (c n)")[:, 0:2048].bitcast(F32), Zb[c0:c0 + 4])
                           for c0 in (16, 20, 24, 28)]
                for si_ in (k.slot_i % NSLOT, (k.slot_i + 1) % NSLOT):
                    subs = borrow_slot(si_, False)
                    f = slots[si_][0][:, 0:4096].bitcast(F32)
                    halves.append(V(f[:, 0:1024], [sb_.bufs[0] for sb_ in subs[0:4]]))
                    halves.append(V(f[:, 1024:2048], [sb_.bufs[0] for sb_ in subs[4:8]]))
                sets = [halves[0:4], halves[4:8], halves[8:12]]
                NPRE = 2
                Vbf = [V(TM_t[:, 4, 0:2048].rearrange("p (m n) -> p m n", n=D), [TMb[4]]),
                       V(Z_t[:, 32:36, :].rearrange("p c n -> p (c n)")[:, 0:2048].rearrange("p (m n) -> p m n", n=D),
                         Zb[32:36])]

                def load_kv(i):
                    K0, K1, V0, V1 = sets[i % 3]
                    for mc, (Kh, Vh) in enumerate([(K0, V0), (K1, V1)]):
                        k.dma("sp", Kh.ap, ck_in[i, mc * 128:(mc + 1) * 128, :], writes=Kh.bufs)
                    for mc, (Kh, Vh) in enumerate([(K0, V0), (K1, V1)]):
                        k.dma("sp", Vh.ap, cv_in[i, mc * 128:(mc + 1) * 128, :], writes=Vh.bufs)

                for i in range(NPRE):
                    load_kv(i)
                for i in range(NS):
                    if i + NPRE < NS:
                        load_kv(i + NPRE)
                    K0, K1, V0, V1 = sets[i % 3]
                    Kh = [K0, K1]
                    Vh = [V0, V1]
                    KT = KTs8[i % 2]
                    Vb = Vbf[i % 2]
                    k.cp(vv(Vb, Vb.ap[:, 0, :]), Vh[0], eng="pool")
                    k.cp(vv(Vb, Vb.ap[:, 1, 0:512]), vv(Vh[1], Vh[1].ap[:, 0:512]), eng="dve")
                    k.cp(vv(Vb, Vb.ap[:, 1, 512:1024]), vv(Vh[1], Vh[1].ap[:, 512:1024]), eng="act")
                    for hc in range(8):
                        pK = ps()
                        for mc in range(2):
                            k.tr(vv(pK, pK.ap[:, mc * 128:(mc + 1) * 128]), vv(Kh[mc], Kh[mc].ap[:, hc * 128:(hc + 1) * 128]),
                                 ident_f)
                        k.cpx(KT[hc], vv(pK, pK.ap[:, 0:256]))
                    sc = slice(512 + 4 * i, 512 + 4 * i + 4)
                    p = ps()
                    for h in range(4):
                        for mc in range(2):
                            col = (h * 2 + mc) * 4
                            for c in range(2):
                                kt = KT[h * 2 + c]
                                k.mm(vv(p, p.ap[:, col:col + 4]), vv(kt, kt.ap[:, mc * 128:(mc + 1) * 128]),
                                     vv(qT[h * 2 + c], qT[h * 2 + c].ap[:, sc]), start=(c == 0), stop=(c == 1))
                    k.act(ETs, vv(p, p.ap[:, 0:32]), AF.Exp)
                    pd = ps()
                    e4 = ETs.ap.rearrange("p (h m t) -> p h m t", h=4, m=2)
                    for mc in range(2):
                        k.mm(vv(pd, pd.ap[:, 0:16].rearrange("p (h t) -> p h t", t=4)), ones_b, vv(ETs, e4[:, :, mc, :]),
                             start=(mc == 0), stop=(mc == 1))
                    k.act(LNs, vv(pd, pd.ap[:, 0:16]), AF.Ln)
                    k.act(RDs, LNs, AF.Exp, scale=-1.0)
                    po = ps()
                    for h in range(4):
                        for c in range(2):
                            col = (h * 2 + c) * 4
                            for mc in range(2):
                                k.mm(vv(po, po.ap[:, col:col + 4]),
                                     vv(Vb, Vb.ap[:, mc, h * 256 + c * 128:h * 256 + (c + 1) * 128]),
                                     vv(ETs, e4[:, h, mc, :]), start=(mc == 0), stop=(mc == 1))
                    for h in range(4):
                        for c in range(2):
                            col = (h * 2 + c) * 4
                            k.tt(vv(aT[h * 2 + c], aT[h * 2 + c].ap[:, sc]), vv(po, po.ap[:, col:col + 4]),
                                 vv(RDs, RDs.ap[:, h * 4:h * 4 + 4]), ALU.mult)
            for half in range(2):
                so = so_pre[half] if so_pre is not None else slab(W["ca_wo"], 8, half * 512, 512)
                for q in range(4):
                    dc = half * 4 + q
                    for (c0, n) in cgs:
                        pv = fm_mm(so, 8, q * 128, 128, aT, c0, n)
                        xv = vv(xT[dc], xT[dc].ap[:, c0:c0 + n])
                        k.tt(xv, pv, xv, ALU.add)

        ONEC = k.sbv("onec", [128, 1], F32)
        k.memset(ONEC, 1.0)
        ONEC4 = vv(ONEC, ONEC.ap[0:4, :])
        ONES4 = vv(ones_c, ones_c.ap[0:4, :])

        STAGE = 99
        TILES = [0, 1, 2, 3]
        MARKS = []
        for j in TILES:
            with_s = (j == NTILE - 1)
            cgs = [(0, 512)] + ([(512, 64)] if with_s else [])
            def mark(nm):
                MARKS.append((nm, j, k.eng["pe"].cnt))
            mark("start")
            if j == TILES[0] and STAGE >= 1:
                mem_kv()
            mark("load_x")
            load_x(j, with_s, fetched=(j != TILES[0]))
            mark("ffn1")
            if STAGE >= 2:
                ffn("ffn1", cgs)
            mark("mix")
            if STAGE >= 3:
                mix(j, cgs, with_s)
            mark("ca")
            if STAGE >= 4:
                cross_attn(j, cgs, with_s)
            mark("ffn2")
            if j + 1 < NTILE:
                fetch_x(j + 1, j + 1 == NTILE - 1)
            if STAGE >= 5:
                ffn("ffn2", cgs)
            mark("final")
            final_out(j, cgs, with_s)
            mark("end")

        if STAGE >= 3:
            for h in range(4):
                k.dma("sp", Cp_o[h].rearrange("(c p) d -> p c d", p=128), Cst[h].ap, reads=Cst[h].bufs, store=True)
                k.dma("sp", Sp_o[h], Sst[h].ap, reads=Sst[h].bufs, store=True)
            p = ps()
            k.tr(vv(p, p.ap[0:4, 0:128]), nTp, ident_f)
            npst = V(n_tm.ap[0:4, :], n_tm.bufs)
            k.cp(npst, vv(p, p.ap[0:4, 0:128]), eng="act")
            k.dma("sp", np_o, npst.ap, reads=npst.bufs, store=True)
            k.dma("sp", mp_o, mcar.ap, reads=mcar.bufs, store=True)
            if 3 in TILES:
              p = ps()
              k.tr(vv(p, p.ap[0:64, 0:128]), nTs, ident_f)
              k.cp(n_tm, vv(p, p.ap[0:64, 0:128]), eng="act")
              k.dma("sp", ns_o, n_tm.ap, reads=n_tm.bufs, store=True)
              k.dma("sp", ms_o, msT.ap, reads=msT.bufs, store=True)

        fin = {}
        for (s, v) in k.final:
            key = id(s)
            if key not in fin or fin[key][1] < v:
                fin[key] = (s, v)

        def replay(h, e):
            for waits, fn, inc in e.prog:
                for (s, v) in waits:
                    h.wait_ge(s, v)
                fn(h).then_inc(inc[0], inc[1])

        print("waits emitted:", k.nwaits)
        print("SBUF bytes remaining/partition:", nc.sbuf_bytes_remaining() if callable(nc.sbuf_bytes_remaining)
              else nc.sbuf_bytes_remaining, "instr:", {n: len(e.prog) for n, e in k.eng.items()})
        with nc.Block() as block:
            @block.tensor
            def _(h):
                replay(h, k.eng["pe"])

            @block.scalar
            def _(h):
                replay(h, k.eng["act"])

            @block.vector
            def _(h):
                replay(h, k.eng["dve"])

            @block.gpsimd
            def _(h):
                replay(h, k.eng["pool"])

            @block.sync
            def _(h):
                replay(h, k.eng["sp"])
                for (s, v) in fin.values():
                    h.wait_ge(s, v)
    return nc


_NC_CACHE = {}


def kernel(**inputs):
    f = lambda a: np.ascontiguousarray(np.asarray(a, dtype=np.float32))
    if "nc" not in _NC_CACHE:
        _NC_CACHE["nc"] = build_nc()
    nc = _NC_CACHE["nc"]
    wnames = ["ffn1_norm", "ffn1_wg", "ffn1_wu", "ffn1_wd", "mix_norm", "w_in", "b_if", "gla_wa2", "gla_ba",
              "mlstm_norm", "gla_norm", "w_br_m", "w_br_g", "w_out", "ca_norm", "mem_norm", "ca_wq", "ca_wk", "ca_wv",
              "ca_wo", "ffn2_norm", "ffn2_wg", "ffn2_wu", "ffn2_wd"]
    wd = {n: f(inputs[n][0]) for n in wnames}
    wd["final_norm"] = f(inputs["final_norm"])
    in_maps = []
    for c in range(NCORES):
        s0, s1 = c * NS, (c + 1) * NS
        m = dict(wd)
        m["x_p"] = f(inputs["x_prompt"][c])
        m["x_s"] = f(inputs["x_sample"][s0:s1].reshape(NS * STK, D))
        m["mem"] = f(inputs["mem_prompt"][c])
        m["C_s"] = f(inputs["state_mlstm_C"][0, s0:s1])
        m["n_s"] = f(inputs["state_mlstm_n"][0, s0:s1].reshape(NS * 4, 128))
        m["m_s"] = f(inputs["state_mlstm_m"][0, s0:s1])
        m["S_s"] = f(inputs["state_gla_S"][0, s0:s1])
        m["ck"] = f(inputs["cache_mem_k"][0, s0:s1].reshape(NS, 256, D))
        m["cv"] = f(inputs["cache_mem_v"][0, s0:s1].reshape(NS, 256, D))
        in_maps.append(m)
    res = run_bass_kernel_spmd(nc, in_maps, core_ids=list(range(NCORES)))
    R = res.results
    y_prompt = np.stack([R[c]["y_p"] for c in range(NCORES)]).astype(np.float32)
    y_sample = np.concatenate([R[c]["y_s"].reshape(NS, STK, D) for c in range(NCORES)]).astype(np.float32)
    Cp = np.stack([R[c]["Cp"] for c in range(NCORES)])[None].astype(np.float32)
    npp = np.stack([R[c]["np"] for c in range(NCORES)])[None].astype(np.float32)
    mp = np.stack([R[c]["mp"].reshape(4) for c in range(NCORES)])[None].astype(np.float32)
    Sp = np.stack([R[c]["Sp"] for c in range(NCORES)])[None].astype(np.float32)
    mkp = np.stack([R[c]["mkp"].reshape(256, 4, 256) for c in range(NCORES)])[None].astype(np.float32)
    mvp = np.stack([R[c]["mvp"].reshape(256, 4, 256) for c in range(NCORES)])[None].astype(np.float32)
    Cs = np.concatenate([R[c]["Cs"] for c in range(NCORES)])[None].astype(np.float32)
    ns = np.concatenate([R[c]["ns"].reshape(4, NS, 128).transpose(1, 0, 2) for c in range(NCORES)])[None].astype(
        np.float32)
    ms = np.concatenate([R[c]["ms"].reshape(4, NS).T for c in range(NCORES)])[None].astype(np.float32)
    Ss = np.concatenate([R[c]["Ss"] for c in range(NCORES)])[None].astype(np.float32)
    return (y_prompt, y_sample, Cp, npp, mp, Sp, mkp, mvp, Cs, np.ascontiguousarray(ns), np.ascontiguousarray(ms), Ss)
```

```python
import numpy as np
from contextlib import ExitStack
import concourse.bass as bass
import concourse.mybir as mybir
from concourse.bass_utils import run_bass_kernel_spmd

F32 = mybir.dt.float32
BF16 = mybir.dt.bfloat16
AF = mybir.ActivationFunctionType
ALU = mybir.AluOpType

NCORES = 8
D = 1024
DFF = 2816
SEQ = 2048
TT = 512
NTILE = 4
NS = 16
STK = 4
NCOL = 576
INW = 8216
EPS = 1e-6
MQ, MK, MV, MO, MI, MF, GQ, GK, GV, GR, GA, GM, GG = (0, 512, 1024, 2048, 3072, 3076, 3080, 3592, 4104, 5128,
                                                      6152, 6168, 7192)
NSLOT = 4
SLAB_ELEMS = 8 * 528
NZ = 36
BIG = 1.0e30
ZMQ, ZMK, ZMO, ZGQ, ZGK, ZGR, ZKD = 0, 4, 8, 16, 20, 24, 32


class Buf:
    __slots__ = ("name", "w", "r", "dsem", "dcnt", "excl")

    def __init__(self, name):
        self.name = name
        self.excl = name.startswith("ps")
        self.w = {}
        self.r = {}
        self.dsem = None
        self.dcnt = 0


class V:
    __slots__ = ("ap", "bufs")

    def __init__(self, ap, bufs):
        self.ap = ap
        self.bufs = bufs


class Eng:
    def __init__(self, name, sem):
        self.name = name
        self.sem = sem
        self.cnt = 0
        self.seen = {}
        self.prog = []


class KB:
    def __init__(self, nc, stack):
        self.nc = nc
        self.stack = stack
        self.eng = {n: Eng(n, stack.enter_context(nc.semaphore("s_" + n))) for n in ("pe", "act", "dve", "pool", "sp")}
        self.final = []
        self.nbuf = 0
        self.ps_i = 0
        self.slot_i = 0
        self.tog = 0

    def buf(self, name=None):
        self.nbuf += 1
        return Buf(name or ("b%d" % self.nbuf))

    def sb(self, name, shape, dt=F32):
        return self.stack.enter_context(self.nc.sbuf_tensor(name, shape, dt))

    def sbv(self, name, shape, dt=F32):
        t = self.sb(name, shape, dt)
        return V(t[:], [self.buf(name)])

    def _waits(self, e, reads, writes):
        need = {}

        def add(d, war):
            for k, (s, v) in d.items():
                if k == e.name and e.name in ("pe", "sp"):
                    continue
                if k not in need or need[k][1] < v:
                    need[k] = (s, v)
        for b in reads:
            add(b.w, False)
            if b.excl:
                add(b.r, True)
        for b in writes:
            add(b.w, False)
            add(b.r, True)
        out = []
        for k, (s, v) in need.items():
            if e.seen.get(k, 0) >= v:
                continue
            e.seen[k] = v
            out.append((s, v))
        return out

    def op(self, en, fn, reads=(), writes=()):
        e = self.eng[en]
        waits = self._waits(e, reads, writes)
        e.cnt += 1
        ev = (e.sem, e.cnt)
        e.prog.append((waits, fn, (e.sem, 1)))
        for b in reads:
            b.r[en] = ev
        for b in writes:
            b.w[en] = ev
            b.r = {}

    def dma(self, qn, out, in_, reads=(), writes=(), store=False, **kw):
        e = self.eng[qn]
        waits = self._waits(e, reads, writes)
        tb = writes[0] if writes else reads[0]
        if tb.dsem is None:
            tb.dsem = {}
        if qn not in tb.dsem:
            tb.dsem[qn] = [self.stack.enter_context(self.nc.semaphore("d_%s_%s" % (tb.name, qn))), 0]
        ds = tb.dsem[qn]
        ds[1] += 16
        dsem = ds[0]
        ev = (dsem, ds[1])
        key = "d_%s_%s" % (tb.name, qn)
        e.prog.append((waits, (lambda h: h.dma_start(out=out, in_=in_, **kw)), (dsem, 16)))
        for b in reads:
            b.r[key] = ev
        for b in writes:
            b.w[key] = ev
            b.r = {}
        self.final.append(ev)

    @staticmethod
    def _bufs(*vs):
        out = []
        for v in vs:
            if isinstance(v, V):
                out.extend(v.bufs)
        return out

    @staticmethod
    def _ap(v):
        return v.ap if isinstance(v, V) else v

    def mm(self, out, lhsT, rhs, start=True, stop=True):
        o, l, r = out.ap, lhsT.ap, rhs.ap
        self.op("pe", lambda h: h.matmul(o, lhsT=l, rhs=r, start=start, stop=stop),
                reads=self._bufs(lhsT, rhs), writes=out.bufs)

    def tr(self, out, in_, ident):
        o, i, d = out.ap, in_.ap, ident.ap
        self.op("pe", lambda h: h.transpose(o, i, d), reads=self._bufs(in_, ident), writes=out.bufs)

    def act(self, out, in_, func, bias=None, scale=None, accum=None, eng="act"):
        o, i = out.ap, in_.ap
        kw = {}
        if bias is not None:
            kw["bias"] = self._ap(bias)
        if scale is not None:
            kw["scale"] = self._ap(scale)
        if accum is not None:
            kw["accum_out"] = accum.ap
        w = out.bufs + (accum.bufs if accum is not None else [])
        self.op("act", lambda h: h.activation(out=o, in_=i, func=func, **kw),
                reads=self._bufs(in_, bias, scale), writes=w)

    def tt(self, out, in0, in1, op, eng="dve"):
        o, a, b = out.ap, in0.ap, in1.ap
        self.op(eng, lambda h: h.tensor_tensor(out=o, in0=a, in1=b, op=op), reads=self._bufs(in0, in1), writes=out.bufs)

    def ts(self, out, in0, s1, op0, s2=None, op1=None, eng="dve"):
        o, a = out.ap, in0.ap
        if eng == "pool" and op1 is None and op0 == ALU.mult:
            op1, s2 = ALU.mult, 1.0
        x1, x2 = self._ap(s1), self._ap(s2)
        if op1 is None:
            fn = lambda h: h.tensor_scalar(out=o, in0=a, scalar1=x1, scalar2=None, op0=op0)
        else:
            fn = lambda h: h.tensor_scalar(out=o, in0=a, scalar1=x1, scalar2=x2, op0=op0, op1=op1)
        self.op(eng, fn, reads=self._bufs(in0, s1, s2), writes=out.bufs)

    def stt(self, out, in0, scalar, in1, op0, op1):
        o, a, b = out.ap, in0.ap, in1.ap
        s = self._ap(scalar)
        self.op("dve", lambda h: h.scalar_tensor_tensor(out=o, in0=a, scalar=s, in1=b, op0=op0, op1=op1),
                reads=self._bufs(in0, scalar, in1), writes=out.bufs)

    def cp(self, out, in_, eng="dve"):
        o, i = out.ap, in_.ap
        if eng == "act":
            self.op("act", lambda h: h.activation(out=o, in_=i, func=AF.Identity), reads=in_.bufs, writes=out.bufs)
        else:
            self.op(eng, lambda h: h.tensor_copy(out=o, in_=i), reads=in_.bufs, writes=out.bufs)

    def cpx(self, out, in_):
        self.tog ^= 1
        self.cp(out, in_, eng="act" if self.tog else "dve")

    def scan(self, out, d0, d1, initial, op0, op1):
        o, a, b = out.ap, d0.ap, d1.ap
        ini = self._ap(initial)
        self.op("dve", lambda h: h.tensor_tensor_scan(out=o, data0=a, data1=b, initial=ini, op0=op0, op1=op1),
                reads=self._bufs(d0, d1, initial), writes=out.bufs)

    def memset(self, out, val, eng="pool"):
        o = out.ap
        self.op(eng, lambda h: h.memset(o, val), writes=out.bufs)

    def asel(self, out, in_, pattern, cmp, fill, base, cm):
        o, i = out.ap, in_.ap
        self.op("pool", lambda h: h.affine_select(out=o, in_=i, pattern=pattern, compare_op=cmp, fill=fill, base=base,
                                                  channel_multiplier=cm), reads=in_.bufs, writes=out.bufs)


def vv(v, ap):
    return V(ap, v.bufs)


def build_nc():
    nc = bass.Bass("TRN2", target_bir_lowering=False)

    def di(n, s):
        return nc.dram_tensor(n, s, F32, kind="ExternalInput").ap()

    def do(n, s):
        return nc.dram_tensor(n, s, F32, kind="ExternalOutput").ap()

    x_p = di("x_p", [SEQ, D])
    x_s = di("x_s", [NS * STK, D])
    mem = di("mem", [256, D])
    Cs_in = di("C_s", [NS, 4, 256, 128])
    ns_in = di("n_s", [NS * 4, 128])
    ms_in = di("m_s", [NS, 4])
    Ss_in = di("S_s", [NS, 4, 128, 256])
    ck_in = di("ck", [NS, 256, D])
    cv_in = di("cv", [NS, 256, D])
    W = {}
    for n, s in [("ffn1_norm", [D]), ("ffn1_wg", [D, DFF]), ("ffn1_wu", [D, DFF]), ("ffn1_wd", [DFF, D]),
                 ("mix_norm", [D]), ("w_in", [D, INW]), ("b_if", [8]), ("gla_wa2", [16, 512]), ("gla_ba", [512]),
                 ("mlstm_norm", [D]), ("gla_norm", [D]), ("w_br_m", [D, D]), ("w_br_g", [D, D]), ("w_out", [D, D]),
                 ("ca_norm", [D]), ("mem_norm", [D]), ("ca_wq", [D, D]), ("ca_wk", [D, D]), ("ca_wv", [D, D]),
                 ("ca_wo", [D, D]), ("ffn2_norm", [D]), ("ffn2_wg", [D, DFF]), ("ffn2_wu", [D, DFF]),
                 ("ffn2_wd", [DFF, D]), ("final_norm", [D])]:
        W[n] = di(n, s)
    y_p = do("y_p", [SEQ, D])
    y_s = do("y_s", [NS * STK, D])
    Cp_o = do("Cp", [4, 256, 128])
    np_o = do("np", [4, 128])
    mp_o = do("mp", [4, 1])
    Sp_o = do("Sp", [4, 128, 256])
    mkp_o = do("mkp", [256, D])
    mvp_o = do("mvp", [256, D])
    Cs_o = do("Cs", [NS, 4, 256, 128])
    ns_o = do("ns", [4 * NS, 128])
    ms_o = do("ms", [4, NS])
    Ss_o = do("Ss", [NS, 4, 128, 256])

    with ExitStack() as stack:
        k = KB(nc, stack)

        xT_t = k.sb("xT", [128, 8, NCOL], F32)
        xT = [V(xT_t[:, i, :], [k.buf("xT%d" % i)]) for i in range(8)]
        hT_t = k.sb("hT", [128, 8, NCOL], BF16)
        hT = [V(hT_t[:, i, :], [k.buf("hT%d" % i)]) for i in range(8)]
        Z_t = k.sb("Z", [128, NZ, NCOL], BF16)
        Zb = [k.buf("Z%d" % i) for i in range(NZ)]
        Z = [V(Z_t[:, i, :], [Zb[i]]) for i in range(NZ)]
        TMW = 2564
        TM_t = k.sb("TM", [128, 5, TMW], BF16)
        TMb = [k.buf("TM%d" % i) for i in range(5)]
        mk_tm = [V(TM_t[:, p, 0:512], [TMb[p]]) for p in range(5)]
        mv_tm = [V(TM_t[:, p, 512:512 + 1028].rearrange("p (h v) -> p h v", v=257), [TMb[p]]) for p in range(5)]
        gv_tm = [V(TM_t[:, p, 1540:2564].rearrange("p (h v) -> p h v", v=256), [TMb[p]]) for p in range(5)]
        stg = [V(TM_t[:, p, 0:2048].bitcast(F32), [TMb[p]]) for p in range(5)]
        slots = []
        for i in range(NSLOT):
            t = k.sb("slab%d" % i, [128, SLAB_ELEMS], BF16)
            slots.append((t, k.buf("slab%d" % i)))
        NPS = 8
        PS = []
        for i in range(NPS):
            t = stack.enter_context(nc.psum_tensor("ps%d" % i, [128, 512], F32))
            PS.append(V(t[:], [k.buf("ps%d" % i)]))

        def psb():
            p = ps()
            return V(p.ap.bitcast(BF16), p.bufs)

        pinned = set()

        def ps(pin=False):
            while (k.ps_i % NPS) in pinned:
                k.ps_i += 1
            idx = k.ps_i % NPS
            k.ps_i += 1
            if pin:
                pinned.add(idx)
            return PS[idx]

        def unpin(p):
            pinned.discard(PS.index(p))


        ident_f = k.sbv("ident_f", [128, 128], F32)
        ident_b = k.sbv("ident_b", [128, 128], BF16)
        ones_b = k.sbv("ones_b", [128, 128], BF16)
        ones_c = k.sbv("ones_c", [128, 1], F32)
        sel = k.sbv("sel", [4, 4, 128], BF16)
        maskbigP = k.sbv("maskbigP", [128, 128], F32)
        mask01P = k.sbv("mask01P", [128, 128], F32)
        maskbigS = k.sbv("maskbigS", [64, 64], F32)
        mask01S = k.sbv("mask01S", [64, 64], F32)
        ind = k.sbv("ind", [64, NS], F32)
        CT = k.sbv("consts", [128, 72], F32)
        gcol = {"ffn1_norm": 0, "mix_norm": 8, "ca_norm": 16, "mem_norm": 24, "ffn2_norm": 32, "final_norm": 40,
                "mlstm_norm": 48, "gla_norm": 56}
        negba = k.sbv("negba", [128, 4], F32)
        bi = k.sbv("bi", [4, 1], F32)
        negbf = k.sbv("negbf", [4, 1], F32)
        wa2_b = k.sbv("wa2_b", [16, 512], BF16)

        mkT_mem = k.sbv("mkT_mem", [128, 8, 256], BF16)
        mv_mem = k.sbv("mv_mem", [128, 2, D], BF16)
        Cst = [k.sbv("Cst%d" % h, [128, 2, 128], F32) for h in range(4)]
        Sst = [k.sbv("Sst%d" % h, [128, 256], F32) for h in range(4)]
        nTp = k.sbv("nTp", [128, 4], F32)
        mcar = k.sbv("mcar", [4, 1], F32)
        IG = k.sbv("IG", [4, NCOL], F32)
        E1 = None
        SPm = None
        FNx = k.sbv("FNx", [4, NCOL + 1], F32)
        GGt = IG
        GL2 = [k.sbv("GL0", [4, 128], F32)] * 3
        MG2 = [k.sbv("MG0", [4, 128], F32)] * 3
        AE2 = [k.sbv("AE0", [4, 2, 128], F32)] * 3
        MGH = [k.sbv("MGH%d" % i, [4, 2, 128], BF16) for i in range(3)]
        AEH = [k.sbv("AEH%d" % i, [4, 2, 2, 128], BF16) for i in range(3)]
        TMPm = k.sbv("TMPm", [4, 128], F32)
        gaT = k.sbv("gaT", [16, NCOL], BF16)
        Ge1 = k.sbv("Ge1", [128, NCOL], F32)
        E1 = V(Ge1.ap[0:4, :], Ge1.bufs)
        SPm = E1
        BNx = k.sbv("BNx", [128, NCOL + 1], F32)
        NB5 = k.sbv("NB5", [128, 8], F32)
        carB = k.sbv("carB", [128, 4], F32)
        Earg = k.sbv("Earg", [128, 64], F32)
        Eq3 = [k.sbv("Eq%d" % i, [128, 128], F32) for i in range(3)]
        DEC = k.sbv("DEC", [128, 4, 4 + NS], F32)
        def zf32(c0):
            t = Z_t[:, c0:c0 + 2, :].rearrange("p c n -> p (c n)").bitcast(F32)
            return V(t[:, 0:512], [Zb[c0], Zb[c0 + 1]])
        lnv = zf32(32)
        rstd = zf32(34)
        sgt = [V(Z_t[:, 22 + i, 0:512], [Zb[22 + i]]) for i in range(2)]
        mrg = [lnv, rstd]
        NSTG = 4
        STQ = "sp"
        STG_t = k.sb("STG", [128, 2 * NSTG, 256], F32)
        STGb = [k.buf("STG%d" % i) for i in range(2 * NSTG)]
        Cstg = [V(STG_t[:, i, :].rearrange("p (a d) -> p a d", d=128), [STGb[i]]) for i in range(NSTG)]
        Sstg = [V(STG_t[:, NSTG + i, :], [STGb[NSTG + i]]) for i in range(NSTG)]
        nTs = k.sbv("nTs", [128, 64], F32)
        n_tm = k.sbv("n_tm", [64, 128], F32)
        m0T = k.sbv("m0T", [4, NS], F32)
        msT = k.sbv("msT", [4, NS], F32)

        class Pool2:
            def __init__(self, name, shape, dt, n=2):
                self.items = [k.sbv("%s_%d" % (name, i), shape, dt) for i in range(n)]
                self.i = 0

            def get(self):
                v = self.items[self.i % len(self.items)]
                self.i += 1
                return v
        P_WT = Pool2("WT", [128, 128], F32, 5)
        P_PT = Pool2("PT", [128, 128], BF16, 6)
        P_AB = Pool2("AB", [128, 2, 128], F32, 5)
        P_QA = Pool2("QA", [128, 128], BF16, 5)
        P_QN = Pool2("QN", [128, 128], BF16, 5)
        P_CTB = Pool2("CTB", [128, 256], BF16, 3)
        P_HD = Pool2("HD", [128, 128], F32)
        P_RD = Pool2("RD", [128, 128], F32)
        P_HR = Pool2("HR", [128, 2, 128], F32, 6)
        P_KW = Pool2("KW", [128, 128], BF16, 6)
        P_GT = Pool2("GT", [128, 4], F32, 3)
        P_SQ = Pool2("SQh", [128, 2, 128], BF16)
        P_T1 = Pool2("T1", [128, 2, 128], F32)
        P_LR = Pool2("LR", [128, 2, 128], F32)
        P_SBF = Pool2("SBF", [128, 256], BF16, 3)
        P_KD = Pool2("KD", [128, 128], BF16, 5)
        class PoolV:
            def __init__(self, items):
                self.items = items
                self.i = 0

            def get(self):
                v = self.items[self.i % len(self.items)]
                self.i += 1
                return v
        P_ET = PoolV([V(TM_t[:, 4, q * 1024:(q + 1) * 1024].rearrange("p (m n) -> p m n", n=512), [TMb[4]])
                      for q in range(2)])
        P_KB = PoolV([V(TM_t[:, q, 0:2048].rearrange("p (m n) -> p m n", n=D), [TMb[q]]) for q in range(4)])
        KTs8 = []
        for q in range(2):
            kt3 = STG_t[:, q * 4:q * 4 + 4, :].rearrange("p a d -> p (a d)").bitcast(BF16).rearrange("p (c m) -> p c m", m=256)
            KTs8.append([V(kt3[:, c, :], [k.buf("KT%d_%d" % (q, c))]) for c in range(8)])
        maskbigPb = k.sbv("maskbigPb", [128, 128], BF16)
        maskbigSb = k.sbv("maskbigSb", [64, 64], BF16)
        ETs = k.sbv("ETs", [128, 32], BF16)
        RDs = k.sbv("RDs", [128, 16], F32)
        LNs = k.sbv("LNs", [128, 16], F32)

        k.memset(ident_f, 0.0)
        k.asel(ident_f, ident_f, [[-1, 128]], ALU.not_equal, 1.0, 0, 1)
        k.cp(ident_b, ident_f, eng="pool")
        k.memset(ones_b, 1.0)
        k.memset(ones_c, 1.0)
        k.memset(sel, 0.0)
        k.asel(sel, sel, [[-1, 4], [0, 128]], ALU.not_equal, 1.0, 0, 1)
        k.memset(maskbigP, 0.0)
        k.asel(maskbigP, maskbigP, [[1, 128]], ALU.is_ge, BIG, 0, -1)
        k.memset(mask01P, 1.0)
        k.asel(mask01P, mask01P, [[1, 128]], ALU.is_ge, 0.0, 0, -1)
        k.memset(maskbigS, 0.0)
        k.asel(maskbigS, maskbigS, [[4, 16], [1, 4]], ALU.is_ge, BIG, 0, -1)
        k.asel(maskbigS, maskbigS, [[-4, 16], [0, 4]], ALU.is_ge, BIG, 0, 1)
        k.cp(maskbigPb, maskbigP, eng="pool")
        k.cp(maskbigSb, maskbigS, eng="pool")
        k.memset(mask01S, 1.0)
        k.asel(mask01S, mask01S, [[4, 16], [1, 4]], ALU.is_ge, 0.0, 0, -1)
        k.asel(mask01S, mask01S, [[-4, 16], [0, 4]], ALU.is_ge, 0.0, 0, 1)
        k.memset(ind, 1.0)
        k.asel(ind, ind, [[-4, NS]], ALU.is_ge, 0.0, 0, 1)
        k.asel(ind, ind, [[4, NS]], ALU.is_ge, 0.0, 3, -1)
        for h in range(4):
            k.memset(Cst[h], 0.0)
            k.memset(Sst[h], 0.0)
        k.memset(nTp, 0.0)
        k.memset(mcar, 0.0)
        k.memset(carB, 0.0)
        k.memset(vv(FNx, FNx.ap[:, 0:1]), 0.0)
        for p in range(5):
            k.memset(vv(mv_tm[p], mv_tm[p].ap[:, :, 256:257]), 1.0)
        for n, c in gcol.items():
            k.dma("sp", CT.ap[:, c:c + 8], W[n].rearrange("(c p) -> p c", p=128), writes=CT.bufs,
                  allow_slow_non_contiguous=True)
        k.dma("sp", CT.ap[:, 64:68], W["gla_ba"].rearrange("(c p) -> p c", p=128), writes=CT.bufs,
              allow_slow_non_contiguous=True)
        k.dma("sp", bi.ap, W["b_if"][0:4].rearrange("(p o) -> p o", o=1), writes=bi.bufs)
        k.dma("sp", negbf.ap, W["b_if"][4:8].rearrange("(p o) -> p o", o=1), writes=negbf.bufs)
        k.dma("pool", wa2_b.ap, W["gla_wa2"], writes=wa2_b.bufs)
        k.dma("sp", n_tm.ap, ns_in, writes=n_tm.bufs)
        k.dma("sp", m0T.ap, ms_in.rearrange("i h -> h i"), writes=m0T.bufs, allow_slow_non_contiguous=True)
        k.ts(negbf, negbf, -1.0, ALU.mult)
        k.ts(negba, vv(CT, CT.ap[:, 64:68]), -1.0, ALU.mult)

        def gain(name, c):
            col = gcol[name] + c
            return vv(CT, CT.ap[:, col:col + 1])

        pt = ps()
        k.tr(vv(pt, pt.ap[:, 0:64]), n_tm, vv(ident_f, ident_f.ap[0:64, 0:64]))
        k.cp(vv(nTs, nTs.ap.rearrange("p (h i) -> p h i", i=NS)),
             vv(pt, pt.ap[:, 0:64].rearrange("p (i h) -> p h i", h=4)))

        def slab(wap, KC, c0, w):
            si_ = k.slot_i % NSLOT
            t, b = slots[si_]
            k.slot_i += 1
            view = t[:, 0:KC * w].rearrange("p (c n) -> p c n", n=w)
            src = wap.rearrange("(c p) n -> p c n", p=128)[:, :, c0:c0 + w]
            k.dma("pool", view, src, writes=[b] + slot_subs[si_])
            return V(view, [b])

        slot_subs = [[] for _ in range(NSLOT)]

        def borrow_slot(si_, shape3):
            t, b = slots[si_]
            if not slot_subs[si_]:
                slot_subs[si_] = [k.buf("sub%d_%d" % (si_, q)) for q in range(8)]
            f = t[:, 0:4096].bitcast(F32)
            k.memset(V(f[:, 0:1], [b] + slot_subs[si_]), 0.0, eng="dve")
            out = []
            for q in range(8):
                v = f[:, q * 256:(q + 1) * 256]
                if shape3:
                    v = v.rearrange("p (a d) -> p a d", d=128)
                out.append(V(v, [slot_subs[si_][q]]))
            return out

        def fm_mm(sl, KC, s0, M, ins, c0, n):
            p = ps()
            pv = vv(p, p.ap[0:M, 0:n])
            for c in range(KC):
                k.mm(pv, vv(sl, sl.ap[:, c, s0:s0 + M]), vv(ins[c], ins[c].ap[:, c0:c0 + n]), start=(c == 0),
                     stop=(c == KC - 1))
            return pv

        def tm_mm(sl, KC, s0, w, ins, t0, M):
            p = ps()
            pv = vv(p, p.ap[0:M, 0:w])
            for c in range(KC):
                k.mm(pv, vv(ins[c], ins[c].ap[:, t0:t0 + M]), vv(sl, sl.ap[:, c, s0:s0 + w]), start=(c == 0),
                     stop=(c == KC - 1))
            return pv

        def rmsnorm(src, gname, dst, cgs, nch=8, div=float(D), sq=None):
            if sq is None:
                sq = Z[24:24 + nch]
            for (c0, n) in cgs:
                for c in range(nch):
                    k.act(vv(sq[c], sq[c].ap[:, c0:c0 + n]), vv(src[c], src[c].ap[:, c0:c0 + n]), AF.Square)
                p = ps()
                pv = vv(p, p.ap[:, 0:n])
                for c in range(nch):
                    k.mm(pv, ones_b, vv(sq[c], sq[c].ap[:, c0:c0 + n]), start=(c == 0), stop=(c == nch - 1))
                lv = vv(lnv, lnv.ap[:, 0:n])
                rv = vv(rstd, rstd.ap[:, 0:n])
                k.act(lv, pv, AF.Ln, bias=EPSC, scale=1.0 / div)
                k.act(rv, lv, AF.Exp, scale=-0.5)
                for c in range(nch):
                    k.stt(vv(dst[c], dst[c].ap[:, c0:c0 + n]), vv(src[c], src[c].ap[:, c0:c0 + n]), gain(gname, c), rv,
                          ALU.mult, ALU.mult)

        EPSC = k.sbv("epsc", [128, 1], F32)
        k.memset(EPSC, EPS)

        def ffn(pref, cgs):
            rmsnorm(xT, pref + "_norm", hT, cgs)
            wg, wu, wd = W[pref + "_wg"], W[pref + "_wu"], W[pref + "_wd"]
            for s0 in range(0, DFF, 512):
                w = min(512, DFF - s0)
                sg = slab(wg, 8, s0, w)
                su = slab(wu, 8, s0, w)
                for f in range(w // 128):
                    fc = s0 // 128 + f
                    for (c0, n) in cgs:
                        pg = fm_mm(sg, 8, f * 128, 128, hT, c0, n)
                        pu = fm_mm(su, 8, f * 128, 128, hT, c0, n)
                        st = sgt[fc % 2]
                        sv = vv(st, st.ap[:, 0:n])
                        k.act(sv, pg, AF.Silu)
                        k.tt(vv(Z[fc], Z[fc].ap[:, c0:c0 + n]), sv, pu, ALU.mult)
            for dc in range(8):
                sd = slab(wd, 22, dc * 128, 128)
                for (c0, n) in cgs:
                    pv = fm_mm(sd, 22, 0, 128, Z, c0, n)
                    xv = vv(xT[dc], xT[dc].ap[:, c0:c0 + n])
                    k.stt(xv, pv, 0.5, xv, ALU.mult, ALU.add)

        def fetch_x(j, with_s):
            for tc in range(4):
                r0 = (j * 4 + tc) * 128
                k.dma("sp", stg[tc].ap, x_p[r0:r0 + 128, :], writes=stg[tc].bufs)
            if with_s:
                k.dma("sp", stg[4].ap[0:64, :], x_s, writes=stg[4].bufs)

        def load_x(j, with_s, fetched):
            if not fetched:
                fetch_x(j, with_s)
            for c in range(8):
                p = ps()
                for tc in range(4):
                    k.tr(vv(p, p.ap[:, tc * 128:(tc + 1) * 128]), vv(stg[tc], stg[tc].ap[:, c * 128:(c + 1) * 128]),
                         ident_f)
                k.cpx(vv(xT[c], xT[c].ap[:, 0:512]), p)
            if with_s:
                for c in range(8):
                    p = ps()
                    k.tr(vv(p, p.ap[:, 0:64]), vv(stg[4], stg[4].ap[0:64, c * 128:(c + 1) * 128]),
                         vv(ident_f, ident_f.ap[0:64, 0:64]))
                    k.cpx(vv(xT[c], xT[c].ap[:, 512:576]), vv(p, p.ap[:, 0:64]))

        def final_out(j, cgs, with_s):
            yf_t = Z_t[:, 0:16, :].rearrange("p c n -> p (c n)").bitcast(F32).rearrange("p (c n) -> p c n", n=NCOL)
            yf = [V(yf_t[:, c, :], [Zb[2 * c], Zb[2 * c + 1]]) for c in range(8)]
            rmsnorm(xT, "final_norm", yf, cgs)
            nchunks = 5 if with_s else 4
            ystg = [V(Z_t[:, c0:c0 + 4, :].rearrange("p c n -> p (c n)")[:, 0:2048].bitcast(F32), Zb[c0:c0 + 4])
                    for c0 in (16, 20)]
            for tc in range(nchunks):
                M = 128 if tc < 4 else 64
                t0 = tc * 128
                yst = ystg[tc % 2]
                for half in range(2):
                    p = ps()
                    for q in range(4):
                        c = half * 4 + q
                        k.tr(vv(p, p.ap[0:M, q * 128:(q + 1) * 128]), vv(yf[c], yf[c].ap[:, t0:t0 + M]), ident_f)
                    k.cpx(vv(yst, yst.ap[0:M, half * 512:(half + 1) * 512]), vv(p, p.ap[0:M, :]))
                if tc < 4:
                    r0 = (j * 4 + tc) * 128
                    k.dma("sp", y_p[r0:r0 + 128, :], yst.ap, reads=yst.bufs, store=True)
                else:
                    k.dma("sp", y_s, yst.ap[0:64, :], reads=yst.bufs, store=True)

        def mem_kv():
            for mc in range(2):
                k.dma("sp", stg[mc].ap, mem[mc * 128:(mc + 1) * 128, :], writes=stg[mc].bufs)
            mT = [V(Z_t[:, c, 0:512].bitcast(F32), [Zb[c]]) for c in range(8)]
            for c in range(8):
                p = ps()
                for mc in range(2):
                    k.tr(vv(p, p.ap[:, mc * 128:(mc + 1) * 128]), vv(stg[mc], stg[mc].ap[:, c * 128:(c + 1) * 128]),
                         ident_f)
                k.cpx(mT[c], vv(p, p.ap[:, 0:256]))
            KSUB = 99
            if KSUB < 1:
                return
            mn = [vv(hT[c], hT[c].ap[:, 0:256]) for c in range(8)]
            rmsnorm(mT, "mem_norm", mn, [(0, 256)], sq=Z[24:32])
            if KSUB < 2:
                return
            for half in range(2):
                sk = slab(W["ca_wk"], 8, half * 512, 512)
                if KSUB < 3:
                    continue
                KMM = 4
                KCP = 1
                for f in range(min(4, KMM)):
                    pv = fm_mm(sk, 8, f * 128, 128, mn, 0, 256)
                    if KCP == 1:
                        k.cpx(vv(mkT_mem, mkT_mem.ap[:, half * 4 + f, :]), pv)
                    elif KCP == 2:
                        k.cp(vv(mkT_mem, mkT_mem.ap[:, half * 4 + f, :]), pv, eng="dve")
                    elif KCP == 3:
                        k.cp(vv(lnv, lnv.ap[:, 0:256]), pv, eng="act")
                if KSUB < 4:
                    continue
                for mc in range(2):
                    pv = tm_mm(sk, 8, 0, 512, mn, mc * 128, 128)
                    k.cpx(vv(stg[2 + mc], stg[2 + mc].ap[:, half * 512:(half + 1) * 512]), pv)
            if KSUB < 5:
                return
            for mc in range(2):
                k.dma("sp", mkp_o[mc * 128:(mc + 1) * 128, :], stg[2 + mc].ap, reads=stg[2 + mc].bufs, store=True)
            if KSUB < 6:
                return
            for half in range(2):
                sv = slab(W["ca_wv"], 8, half * 512, 512)
                for mc in range(2):
                    pv = tm_mm(sv, 8, 0, 512, mn, mc * 128, 128)
                    k.cp(vv(stg[mc], stg[mc].ap[:, half * 512:(half + 1) * 512]), pv, eng="act")
                    k.cp(vv(mv_mem, mv_mem.ap[:, mc, half * 512:(half + 1) * 512]), pv, eng="dve")
            for mc in range(2):
                k.dma("sp", mvp_o[mc * 128:(mc + 1) * 128, :], stg[mc].ap, reads=stg[mc].bufs, store=True)

        def mix(j, cgs, with_s):
            rmsnorm(xT, "mix_norm", hT, cgs)
            win = W["w_in"]
            ntc = 5 if with_s else 4
            tcs = [(tc, tc * 128, 128 if tc < 4 else 64) for tc in range(ntc)]
            SC = 128.0 ** -0.5
            for p_ in range(ntc):
                k.memset(vv(mv_tm[p_], mv_tm[p_].ap[:, :, 256:257]), 1.0, eng="dve")
            ncol = NCOL if with_s else 512

            def proj_mq():
                sl = slab(win, 8, MQ, 512)
                for h in range(4):
                    for (c0, n) in cgs:
                        pv = fm_mm(sl, 8, h * 128, 128, hT, c0, n)
                        k.cpx(vv(Z[ZMQ + h], Z[ZMQ + h].ap[:, c0:c0 + n]), pv)

            def proj_mk():
                sl = slab(win, 8, MK, 512)
                for h in range(4):
                    for (c0, n) in cgs:
                        pv = fm_mm(sl, 8, h * 128, 128, hT, c0, n)
                        k.act(vv(Z[ZMK + h], Z[ZMK + h].ap[:, c0:c0 + n]), pv, AF.Identity, scale=SC)
                for (tc, t0, M) in tcs:
                    pv = tm_mm(sl, 8, 0, 512, hT, t0, M)
                    k.ts(vv(mk_tm[tc], mk_tm[tc].ap[0:M, :]), pv, SC, ALU.mult)

            def proj_mv(half):
                sl = slab(win, 8, MV + half * 512, 512)
                for (tc, t0, M) in tcs:
                    pv = tm_mm(sl, 8, 0, 512, hT, t0, M)
                    k.cpx(vv(mv_tm[tc], mv_tm[tc].ap[0:M, half * 2:half * 2 + 2, 0:256]),
                          vv(pv, pv.ap.rearrange("p (h v) -> p h v", v=256)))

            def proj_mo(half):
                sl = slab(win, 8, MO + half * 512, 512)
                for f in range(4):
                    zc = Z[ZMO + half * 4 + f]
                    for (c0, n) in cgs:
                        pv = fm_mm(sl, 8, f * 128, 128, hT, c0, n)
                        k.act(vv(zc, zc.ap[:, c0:c0 + n]), pv, AF.Sigmoid)
                    k.ts(vv(zc, zc.ap[:, 0:ncol]), vv(zc, zc.ap[:, 0:ncol]), gain("mlstm_norm", half * 4 + f), ALU.mult)

            def proj_A():
                sl = slab(win, 8, MI, 520)
                for (c0, n) in cgs:
                    pv = fm_mm(sl, 8, 0, 4, hT, c0, n)
                    k.ts(vv(IG, IG.ap[:, c0:c0 + n]), pv, bi, ALU.add)
                    pv = fm_mm(sl, 8, 4, 4, hT, c0, n)
                    k.act(vv(E1, E1.ap[:, c0:c0 + n]), pv, AF.Exp, bias=negbf, scale=-1.0)
                for h in range(4):
                    for (c0, n) in cgs:
                        pv = fm_mm(sl, 8, 8 + h * 128, 128, hT, c0, n)
                        k.cpx(vv(Z[ZGQ + h], Z[ZGQ + h].ap[:, c0:c0 + n]), pv)

            def proj_gk():
                sl = slab(win, 8, GK, 512)
                for h in range(4):
                    for (c0, n) in cgs:
                        pv = fm_mm(sl, 8, h * 128, 128, hT, c0, n)
                        k.cpx(vv(Z[ZGK + h], Z[ZGK + h].ap[:, c0:c0 + n]), pv)

            def proj_gv(half):
                sl = slab(win, 8, GV + half * 512, 512)
                for (tc, t0, M) in tcs:
                    pv = tm_mm(sl, 8, 0, 512, hT, t0, M)
                    k.cpx(vv(gv_tm[tc], gv_tm[tc].ap[0:M, half * 2:half * 2 + 2, :]),
                          vv(pv, pv.ap.rearrange("p (h v) -> p h v", v=256)))

            def proj_gr(half):
                wdt = 512 if half == 0 else 528
                sl = slab(win, 8, GR + half * 512, wdt)
                if half == 1:
                    for (c0, n) in cgs:
                        pv = fm_mm(sl, 8, 512, 16, hT, c0, n)
                        k.cp(vv(gaT, gaT.ap[:, c0:c0 + n]), pv, eng="dve")
                for f in range(4):
                    zc = Z[ZGR + half * 4 + f]
                    for (c0, n) in cgs:
                        pv = fm_mm(sl, 8, f * 128, 128, hT, c0, n)
                        k.act(vv(zc, zc.ap[:, c0:c0 + n]), pv, AF.Silu)
                    k.ts(vv(zc, zc.ap[:, 0:ncol]), vv(zc, zc.ap[:, 0:ncol]), gain("gla_norm", half * 4 + f), ALU.mult)

            def prep_mlstm():
                k.act(vv(SPm, SPm.ap[:, 0:ncol]), vv(E1, E1.ap[:, 0:ncol]), AF.Ln, bias=ONEC4, scale=1.0)
                k.scan(vv(FNx, FNx.ap[:, 1:1 + ncol]), vv(ONES4, ONES4.ap[:, 0:1].to_broadcast([4, ncol])),
                       vv(SPm, SPm.ap[:, 0:ncol]), vv(FNx, FNx.ap[:, 0:1]), ALU.mult, ALU.add)
                k.tt(vv(GGt, GGt.ap[:, 0:ncol]), vv(IG, IG.ap[:, 0:ncol]), vv(FNx, FNx.ap[:, 1:1 + ncol]), ALU.add)

            def prep_gla(h):
                    for (c0, n) in cgs:
                        p = ps()
                        pv = vv(p, p.ap[:, 0:n])
                        k.mm(pv, vv(wa2_b, wa2_b.ap[:, h * 128:(h + 1) * 128]), vv(gaT, gaT.ap[:, c0:c0 + n]))
                        k.act(vv(Ge1, Ge1.ap[:, c0:c0 + n]), pv, AF.Exp, bias=vv(negba, negba.ap[:, h:h + 1]), scale=-1.0)
                    k.act(vv(Ge1, Ge1.ap[:, 0:ncol]), vv(Ge1, Ge1.ap[:, 0:ncol]), AF.Ln, bias=ONEC, scale=1.0)
                    k.ts(vv(Ge1, Ge1.ap[:, 0:ncol]), vv(Ge1, Ge1.ap[:, 0:ncol]), 1.0 / 16.0, ALU.mult)
                    k.cp(vv(BNx, BNx.ap[:, 0:1]), vv(carB, carB.ap[:, h:h + 1]), eng="dve")
                    k.scan(vv(BNx, BNx.ap[:, 1:1 + ncol]), vv(ones_c, ones_c.ap[:, 0:1].to_broadcast([128, ncol])),
                           vv(Ge1, Ge1.ap[:, 0:ncol]), vv(BNx, BNx.ap[:, 0:1]), ALU.mult, ALU.add)
                    k.cp(vv(carB, carB.ap[:, h:h + 1]), vv(BNx, BNx.ap[:, 512:513]), eng="dve")
                    k.ts(vv(NB5, NB5.ap[:, 0:5]), vv(BNx, BNx.ap[:, 0:513:128]), -1.0, ALU.mult)
                    gq, gk, kd = Z[ZGQ + h], Z[ZGK + h], Z[ZKD + h]
                    SCQ = 128.0 ** -0.5
                    for c in range(4):
                        a, b = c * 128, (c + 1) * 128
                        prevP = vv(BNx, BNx.ap[:, a:a + 1])
                        prevN = vv(NB5, NB5.ap[:, c:c + 1])
                        endN = vv(NB5, NB5.ap[:, c + 1:c + 2])
                        cur = vv(BNx, BNx.ap[:, a + 1:b + 1])
                        e0, e1, e2 = Eq3
                        k.act(e0, cur, AF.Exp, bias=endN, scale=1.0)
                        k.act(e1, cur, AF.Exp, bias=prevN, scale=1.0)
                        k.act(e2, cur, AF.Exp, bias=prevP, scale=-1.0)
                        k.tt(vv(kd, kd.ap[:, a:b]), vv(gk, gk.ap[:, a:b]), e0, ALU.mult)
                        k.tt(vv(gk, gk.ap[:, a:b]), vv(gk, gk.ap[:, a:b]), e1, ALU.mult)
                        k.stt(vv(gq, gq.ap[:, a:b]), vv(gq, gq.ap[:, a:b]), SCQ, e2, ALU.mult, ALU.mult)
                        k.act(vv(DEC, DEC.ap[:, h, c:c + 1]), vv(BNx, BNx.ap[:, b:b + 1]), AF.Exp, bias=prevP, scale=-1.0)
                    if with_s:
                        a, b = 512, 576
                        cur3 = BNx.ap[:, a + 1:b + 1].rearrange("p (i t) -> p i t", t=4)
                        prev3 = BNx.ap[:, a:b].rearrange("p (i t) -> p i t", t=4)[:, :, 0:1].to_broadcast([128, NS, 4])
                        end3 = BNx.ap[:, a + 1:b + 1].rearrange("p (i t) -> p i t", t=4)[:, :, 3:4].to_broadcast(
                            [128, NS, 4])
                        ea = vv(Earg, Earg.ap.rearrange("p (i t) -> p i t", t=4))
                        e = vv(Eq3[0], Eq3[0].ap[:, 0:64])
                        k.tt(ea, vv(BNx, cur3), vv(BNx, end3), ALU.subtract)
                        k.act(e, Earg, AF.Exp)
                        k.tt(vv(kd, kd.ap[:, a:b]), vv(gk, gk.ap[:, a:b]), e, ALU.mult)
                        k.tt(ea, vv(BNx, cur3), vv(BNx, prev3), ALU.subtract)
                        k.act(e, Earg, AF.Exp)
                        k.tt(vv(gk, gk.ap[:, a:b]), vv(gk, gk.ap[:, a:b]), e, ALU.mult)
                        k.act(e, Earg, AF.Exp, scale=-1.0)
                        k.stt(vv(gq, gq.ap[:, a:b]), vv(gq, gq.ap[:, a:b]), SCQ, e, ALU.mult, ALU.mult)
                        k.act(vv(DEC, DEC.ap[:, h, 4:4 + NS]), vv(Earg, Earg.ap.rearrange('p (i t) -> p i t', t=4)[:, :, 3]), AF.Exp, scale=-1.0)

            MARKS.append(("mix_proj", j, k.eng["pe"].cnt))
            proj_A()
            proj_gr(1)
            proj_gk()
            prep_mlstm()
            prep_gla(0)
            proj_gr(0)
            proj_mq()
            prep_gla(1)
            proj_mk()
            proj_mv(0)
            prep_gla(2)
            proj_mv(1)
            proj_mo(0)
            prep_gla(3)
            proj_mo(1)
            proj_gv(0)
            proj_gv(1)
            MARKS.append(("mix_prep", j, k.eng["pe"].cnt))

            groups = [dict(g0=c * 128, NP=128, tc=c, owners=[dict(c0=0, n=128, kind="p", dcol=c)],
                           mbig=maskbigPb, m01=mask01P) for c in range(4)]
            if with_s:
                groups.append(dict(g0=512, NP=64, tc=4, owners=[dict(c0=4 * i, n=4, kind="s", i=i, dcol=4 + i)
                                                                  for i in range(NS)],
                                   mbig=maskbigSb, m01=mask01S))
            def prologue(gi_, g, ctx):
                g0, NP, tc = g["g0"], g["NP"], g["tc"]
                gc = slice(g0, g0 + NP)
                GL, MG, AE = GL2[gi_ % 3], MG2[gi_ % 3], AE2[gi_ % 3]
                lc = slice(0, NP)
                if NP == 128:
                    fprev = vv(FNx, FNx.ap[:, g0:g0 + 1])
                    k.ts(vv(GL, GL.ap[:, lc]), vv(GGt, GGt.ap[:, gc]), fprev, ALU.subtract)
                    k.scan(vv(MG, MG.ap[:, lc]), vv(GL, GL.ap[:, lc]), vv(GL, GL.ap[:, lc]), mcar, ALU.max, ALU.max)
                    k.act(vv(AE, AE.ap[:, 0, lc]), vv(MG, MG.ap[:, lc]), AF.Exp, bias=mcar, scale=-1.0)
                    k.stt(vv(TMPm, TMPm.ap[:, lc]), vv(FNx, FNx.ap[:, g0 + 1:g0 + 1 + NP]), fprev,
                          vv(MG, MG.ap[:, lc]), ALU.subtract, ALU.subtract)
                    k.act(vv(AE, AE.ap[:, 1, lc]), vv(TMPm, TMPm.ap[:, lc]), AF.Exp)
                    k.ts(mcar, vv(TMPm, TMPm.ap[:, NP - 1:NP]), -1.0, ALU.mult)
                else:
                    def v3(t, off=0):
                        if t is FNx or t is GGt:
                            return t.ap[:, g0 + off:g0 + off + NP].rearrange("p (i t) -> p i t", t=4)
                        return t.ap[:, 0:NP].rearrange("p (i t) -> p i t", t=4)
                    fprev3 = FNx.ap[:, g0:g0 + NP].rearrange("p (i t) -> p i t", t=4)[:, :, 0:1].to_broadcast(
                        [4, NS, 4])
                    k.tt(vv(GL, v3(GL)), vv(GGt, v3(GGt)), vv(FNx, fprev3), ALU.subtract)
                    k.tt(vv(MG, v3(MG)[:, :, 0]), vv(GL, v3(GL)[:, :, 0]), m0T, ALU.max)
                    for t in range(1, 4):
                        k.tt(vv(MG, v3(MG)[:, :, t]), vv(GL, v3(GL)[:, :, t]), vv(MG, v3(MG)[:, :, t - 1]), ALU.max)
                    m03 = m0T.ap.rearrange("p (i o) -> p i o", o=1).to_broadcast([4, NS, 4])
                    k.tt(vv(TMPm, v3(TMPm)), vv(m0T, m03), vv(MG, v3(MG)), ALU.subtract)
                    k.act(vv(AE, AE.ap[:, 0, lc]), vv(TMPm, TMPm.ap[:, lc]), AF.Exp)
                    k.tt(vv(TMPm, v3(TMPm)), vv(FNx, v3(FNx, 1)), vv(FNx, fprev3), ALU.subtract)
                    k.tt(vv(TMPm, TMPm.ap[:, lc]), vv(TMPm, TMPm.ap[:, lc]), vv(MG, MG.ap[:, lc]), ALU.subtract)
                    k.act(vv(AE, AE.ap[:, 1, lc]), vv(TMPm, TMPm.ap[:, lc]), AF.Exp)
                    k.ts(msT, vv(TMPm, v3(TMPm)[:, :, 3]), -1.0, ALU.mult)
                gt = P_GT.get()
                p = ps_try()
                assert p is not None
                k.tr(vv(p, p.ap[0:NP, 0:4]), vv(GL, GL.ap[:, lc]), vv(ident_f, ident_f.ap[0:4, 0:4]))
                k.cp(vv(gt, gt.ap[0:NP, :]), vv(p, p.ap[0:NP, 0:4]), eng="act")
                psfree(p)
                identNP = vv(ident_f, ident_f.ap[0:NP, 0:NP])
                onesNP = vv(ones_b, ones_b.ap[0:NP, :])
                first = (j == 0 and g0 == 0)
                mgh, aeh = MGH[gi_ % 3], AEH[gi_ % 3]
                k.cp(vv(mgh, mgh.ap[:, 0, lc]), vv(MG, MG.ap[:, lc]), eng="dve")
                k.tt(vv(mgh, mgh.ap[:, 1, lc]), vv(MG, MG.ap[:, lc]), vv(mgh, mgh.ap[:, 0, lc]), ALU.subtract)
                k.cp(vv(aeh, aeh.ap[:, 0, :, lc]), vv(AE, AE.ap[:, :, lc]), eng="dve")
                k.tt(vv(aeh, aeh.ap[:, 1, :, lc]), vv(AE, AE.ap[:, :, lc]), vv(aeh, aeh.ap[:, 0, :, lc]), ALU.subtract)
                ctx.update(dict(g=g, g0=g0, NP=NP, tc=tc, gc=gc, lc=lc, gt=gt, identNP=identNP, onesNP=onesNP,
                                first=first, GL=GL, MG=MG, AE=AE, MGH=mgh, AEH=aeh, ready=True))

            def pro_gen(gi_, g, ctx):
                while len(busy) >= NPS:
                    yield
                prologue(gi_, g, ctx)
                return
                yield

            def wait_ready(ctx, gen_fn, h):
                while not ctx.get("ready"):
                    yield
                yield from gen_fn(ctx, h)

            gens = []
            sample_ctx = None
            for gi_, g in enumerate(groups):
                ctx = {}
                if g["NP"] == 128:
                    gens.append(pro_gen(gi_, g, ctx))
                    for h in range(4):
                        gens.append(wait_ready(ctx, gh_prompt_gen, h))
                else:
                    sample_ctx = (gi_, g, ctx)
            run_interleaved(gens, WH + 1)
            if sample_ctx is not None:
                gi_, g, ctx = sample_ctx
                prologue(gi_, g, ctx)
                CSB[:] = borrow_slot((k.slot_i + 2) % NSLOT, True)
                SSB[:] = borrow_slot((k.slot_i + 3) % NSLOT, False)
                for h in range(4):
                    run_interleaved([gh_sample_gen(ctx, h)], 1)
            MARKS.append(("merge", j, k.eng["pe"].cnt))
            HMc = [Z[ZMQ + hh] if vc == 0 else Z[ZMK + hh] for hh in range(4) for vc in range(2)]
            HGc = [Z[ZGQ + hh] if vc == 0 else Z[ZGK + hh] for hh in range(4) for vc in range(2)]
            yacc_t = Z_t[:, ZMO:ZMO + 8, :].rearrange("p c n -> p (c n)").bitcast(F32).rearrange("p (c n) -> p c n", n=NCOL)
            yacc = [V(yacc_t[:, q, :], [Zb[ZMO + 2 * q], Zb[ZMO + 2 * q + 1]]) for q in range(4)]
            yT = [Z[ZGR + q] for q in range(8)]
            for half in range(2):
                sgm = slab(win, 8, GM + half * 512, 512)
                sbm = slab(W["w_br_m"], 8, half * 512, 512)
                for q in range(4):
                    for (c0, n) in cgs:
                        pg = fm_mm(sgm, 8, q * 128, 128, hT, c0, n)
                        pm = fm_mm(sbm, 8, q * 128, 128, HMc, c0, n)
                        t = mrg[0]
                        k.act(vv(t, t.ap[:, 0:n]), pg, AF.Sigmoid)
                        k.tt(vv(yacc[q], yacc[q].ap[:, c0:c0 + n]), vv(t, t.ap[:, 0:n]), pm, ALU.mult)
                sgg = slab(win, 8, GG + half * 512, 512)
                sbg = slab(W["w_br_g"], 8, half * 512, 512)
                for q in range(4):
                    for (c0, n) in cgs:
                        pg = fm_mm(sgg, 8, q * 128, 128, hT, c0, n)
                        pm = fm_mm(sbg, 8, q * 128, 128, HGc, c0, n)
                        t = mrg[1]
                        k.act(vv(t, t.ap[:, 0:n]), pg, AF.Sigmoid)
                        k.tt(vv(t, t.ap[:, 0:n]), vv(t, t.ap[:, 0:n]), pm, ALU.mult)
                        k.tt(vv(yT[half * 4 + q], yT[half * 4 + q].ap[:, c0:c0 + n]), vv(t, t.ap[:, 0:n]),
                             vv(yacc[q], yacc[q].ap[:, c0:c0 + n]), ALU.add)
            for half in range(2):
                so = slab(W["w_out"], 8, half * 512, 512)
                for q in range(4):
                    dc = half * 4 + q
                    for (c0, n) in cgs:
                        pv = fm_mm(so, 8, q * 128, 128, yT, c0, n)
                        xv = vv(xT[dc], xT[dc].ap[:, c0:c0 + n])
                        k.tt(xv, pv, xv, ALU.add)

        WH = 4
        PEN = "pool"
        busy = set()

        def ps_try():
            for _ in range(NPS):
                idx = k.ps_i % NPS
                k.ps_i += 1
                if idx not in busy and idx not in pinned:
                    busy.add(idx)
                    return PS[idx]
            return None

        def psw():
            while True:
                p = ps_try()
                if p is not None:
                    return p
                yield

        def psfree(p):
            busy.discard(PS.index(p))

        def run_interleaved(gens, width):
            it = iter(gens)
            active = []
            rounds = 0
            while True:
                while len(active) < width:
                    try:
                        active.append(next(it))
                    except StopIteration:
                        break
                if not active:
                    break
                for gn in list(active):
                    try:
                        next(gn)
                    except StopIteration:
                        active.remove(gn)
                rounds += 1
                assert rounds < 100000, "interleave livelock (PSUM banks exhausted?)"

        def inter_gen(gens, width):
            it = iter(gens)
            active = []
            while True:
                while len(active) < width:
                    try:
                        active.append(next(it))
                    except StopIteration:
                        break
                if not active:
                    return
                for gn in list(active):
                    try:
                        next(gn)
                    except StopIteration:
                        active.remove(gn)
                yield

        def hn_gen(HRv, NP, gname, h, gate, dst, g0):
            SQ = P_SQ.get()
            SQv = vv(SQ, SQ.ap[:, :, 0:NP])
            k.act(SQv, HRv, AF.Square)
            p = yield from psw()
            pv = vv(p, p.ap[:, 0:NP])
            for vc in range(2):
                k.mm(pv, ones_b, vv(SQ, SQ.ap[:, vc, 0:NP]), start=(vc == 0), stop=(vc == 1))
            yield
            LR = P_LR.get()
            lv = vv(LR, LR.ap[:, 0, 0:NP])
            rv = vv(LR, LR.ap[:, 1, 0:NP])
            k.act(lv, pv, AF.Ln, bias=EPSC, scale=1.0 / 256.0)
            psfree(p)
            k.act(rv, lv, AF.Exp, scale=-0.5)
            T1 = P_T1.get()
            T1v = vv(T1, T1.ap[:, :, 0:NP])
            k.tt(T1v, HRv, vv(LR, LR.ap[:, 1:2, 0:NP].to_broadcast([128, 2, NP])), ALU.mult)
            dsti, gatei = dst, gate
            dst3 = V(Z_t[:, dsti:dsti + 5:4, g0:g0 + NP], [Zb[dsti], Zb[dsti + 4]])
            gate3 = V(Z_t[:, gatei:gatei + 2, g0:g0 + NP], [Zb[gatei], Zb[gatei + 1]])
            k.tt(dst3, T1v, gate3, ALU.mult)

        def gh_common_front(ctx, h):
            g, g0, NP, tc, gc, lc, gt = ctx["g"], ctx["g0"], ctx["NP"], ctx["tc"], ctx["gc"], ctx["lc"], ctx["gt"]
            GL, MG, AE = ctx["GL"], ctx["MG"], ctx["AE"]
            mq, mkk = Z[ZMQ + h], Z[ZMK + h]
            pST = yield from psw()
            STv = vv(pST, pST.ap[0:NP, 0:NP])
            k.mm(STv, vv(mkk, mkk.ap[:, gc]), vv(mq, mq.ap[:, gc]))
            pMb = yield from psw()
            Mbv = vv(pMb, pMb.ap[0:NP, 0:NP])
            mgh, aeh = ctx["MGH"], ctx["AEH"]
            k.mm(Mbv, vv(sel, sel.ap[:, h, 0:NP]), vv(mgh, mgh.ap[:, 0, lc]), start=True, stop=False)
            k.mm(Mbv, vv(sel, sel.ap[:, h, 0:NP]), vv(mgh, mgh.ap[:, 1, lc]), start=False, stop=False)
            k.mm(Mbv, vv(ident_b, ident_b.ap[0:NP, 0:NP]), g["mbig"], start=False, stop=True)
            pAB = yield from psw()
            ABp = vv(pAB, pAB.ap[:, 0:2 * NP].rearrange("p (a n) -> p a n", n=NP))
            k.mm(ABp, vv(sel, sel.ap[:, h, :]), vv(aeh, aeh.ap[:, 0, :, lc]), start=True, stop=False)
            k.mm(ABp, vv(sel, sel.ap[:, h, :]), vv(aeh, aeh.ap[:, 1, :, lc]), start=False, stop=True)
            yield
            WT = P_WT.get()
            WTv = vv(WT, WT.ap[0:NP, 0:NP])
            k.act(WTv, Mbv, AF.Exp, bias=vv(gt, gt.ap[0:NP, h:h + 1]), scale=-1.0)
            psfree(pMb)
            AB = P_AB.get()
            k.cp(vv(AB, AB.ap[:, :, 0:NP]), ABp, eng="act")
            psfree(pAB)
            PT = P_PT.get()
            PTv = vv(PT, PT.ap[0:NP, 0:NP])
            k.tt(PTv, WTv, STv, ALU.mult)
            psfree(pST)
            QA = P_QA.get()
            k.tt(vv(QA, QA.ap[:, 0:NP]), vv(mq, mq.ap[:, gc]), vv(AB, AB.ap[:, 0, 0:NP]), ALU.mult, eng=PEN)
            return dict(WT=WT, AB=AB, PT=PT, PTv=PTv, QA=QA)

        def gla_front(ctx, h):
            g, g0, NP, gc = ctx["g"], ctx["g0"], ctx["NP"], ctx["gc"]
            gq, gk, kd = Z[ZGQ + h], Z[ZGK + h], Z[ZKD + h]
            pA = yield from psw()
            Av = vv(pA, pA.ap[0:NP, 0:NP])
            k.mm(Av, vv(gk, gk.ap[:, gc]), vv(gq, gq.ap[:, gc]))
            pK32 = yield from psw()
            pK = V(pK32.ap.bitcast(BF16), pK32.bufs)
            k.tr(vv(pK, pK.ap[0:NP, 0:128]), vv(kd, kd.ap[:, gc]), ident_b)
            yield
            PT2 = P_PT.get()
            PT2v = vv(PT2, PT2.ap[0:NP, 0:NP])
            k.tt(PT2v, Av, g["m01"], ALU.mult)
            psfree(pA)
            KD = P_KD.get()
            k.cp(vv(KD, KD.ap[0:NP, :]), vv(pK, pK.ap[0:NP, 0:128]), eng="act")
            psfree(pK32)
            return dict(PT2v=PT2v, KD=KD)

        def gh_prompt_gen(ctx, h):
            g, g0, NP, tc, gc, lc = ctx["g"], ctx["g0"], ctx["NP"], ctx["tc"], ctx["gc"], ctx["lc"]
            first = ctx["first"]
            mq = Z[ZMQ + h]
            o = g["owners"][0]
            last = NP - 1
            fr = yield from gh_common_front(ctx, h)
            WT, AB, PTv, QA = fr["WT"], fr["AB"], fr["PTv"], fr["QA"]
            QAv = vv(QA, QA.ap[:, 0:NP])
            C = Cst[h]
            nT = vv(nTp, nTp.ap[:, h:h + 1])
            QN = P_QN.get()
            if not first:
                pC = yield from psw()
                for vc in range(2):
                    k.tr(vv(pC, pC.ap[:, vc * 128:(vc + 1) * 128]), vv(C, C.ap[:, vc, :]), ident_f)
                k.ts(vv(QN, QN.ap[:, 0:NP]), QAv, nT, ALU.mult, eng=PEN)
                yield
                ctb = P_CTB.get()
                k.cp(ctb, vv(pC, pC.ap[:, 0:256]), eng="act")
                psfree(pC)
            KW = P_KW.get()
            KWv = vv(KW, KW.ap[0:NP, :])
            k.ts(KWv, vv(mk_tm[tc], mk_tm[tc].ap[0:NP, h * 128:(h + 1) * 128]), vv(WT, WT.ap[0:NP, last:last + 1]),
                 ALU.mult, eng=PEN)
            pN = yield from psw()
            for vc in range(2):
                nv = vv(pN, pN.ap[:, vc * NP:(vc + 1) * NP])
                k.mm(nv, vv(mv_tm[tc], mv_tm[tc].ap[0:NP, h, vc * 128:(vc + 1) * 128]), PTv, start=True, stop=first)
                if not first:
                    k.mm(nv, vv(ctb, ctb.ap[:, vc * 128:(vc + 1) * 128]), QAv, start=False, stop=True)
            Dv = vv(pN, pN.ap[:, 2 * NP:3 * NP])
            k.mm(Dv, ctx["onesNP"], PTv, start=True, stop=first)
            if not first:
                k.mm(Dv, ones_b, vv(QN, QN.ap[:, 0:NP]), start=False, stop=True)
            pU = yield from psw()
            for vc in range(2):
                k.mm(vv(pU, pU.ap[:, vc * 128:(vc + 1) * 128]),
                     vv(mv_tm[tc], mv_tm[tc].ap[0:NP, h, vc * 128:(vc + 1) * 128]), KWv)
            k.mm(vv(pU, pU.ap[:, 256:257]), KWv, vv(ones_b, ones_b.ap[0:NP, 0:1]))
            yield
            HD = P_HD.get()
            k.act(vv(HD, HD.ap[:, 0:NP]), Dv, AF.Abs)
            k.tt(vv(HD, HD.ap[:, 0:NP]), vv(HD, HD.ap[:, 0:NP]), vv(AB, AB.ap[:, 1, 0:NP]), ALU.max)
            RD = P_RD.get()
            RDv = vv(RD, RD.ap[:, 0:NP])
            k.act(vv(HD, HD.ap[:, 0:NP]), vv(HD, HD.ap[:, 0:NP]), AF.Ln)
            k.act(RDv, vv(HD, HD.ap[:, 0:NP]), AF.Exp, scale=-1.0)
            HR = P_HR.get()
            HRv = vv(HR, HR.ap[:, :, 0:NP])
            k.tt(HRv, vv(pN, pN.ap[:, 0:2 * NP].rearrange("p (a n) -> p a n", n=NP)),
                 vv(RD, RD.ap[:, 0:NP].rearrange("p (a n) -> p a n", a=1).to_broadcast([128, 2, NP])), ALU.mult)
            psfree(pN)
            aend = vv(AB, AB.ap[:, 0, last:last + 1])
            Cf = vv(C, C.ap.rearrange("p a d -> p (a d)"))
            k.stt(Cf, Cf, aend, vv(pU, pU.ap[:, 0:256]), ALU.mult, ALU.add)
            k.stt(nT, nT, aend, vv(pU, pU.ap[:, 256:257]), ALU.mult, ALU.add)
            psfree(pU)
            yield
            yield from hn_gen(HRv, NP, "mlstm_norm", h, ZMO + 2 * h, ZMQ + h, g0)
            gq = Z[ZGQ + h]
            gf = yield from gla_front(ctx, h)
            PT2v, KD = gf["PT2v"], gf["KD"]
            S = Sst[h]
            if not first:
                sbf = P_SBF.get()
                k.cp(sbf, S, eng="act")
            pO = yield from psw()
            for vc in range(2):
                ov = vv(pO, pO.ap[:, vc * NP:(vc + 1) * NP])
                k.mm(ov, vv(gv_tm[tc], gv_tm[tc].ap[0:NP, h, vc * 128:(vc + 1) * 128]), PT2v, start=True, stop=first)
                if not first:
                    k.mm(ov, vv(sbf, sbf.ap[:, vc * 128:(vc + 1) * 128]), vv(gq, gq.ap[:, gc]), start=False, stop=True)
            pU = yield from psw()
            k.mm(vv(pU, pU.ap[:, 0:256]), vv(KD, KD.ap[0:NP, :]), vv(gv_tm[tc], gv_tm[tc].ap[0:NP, h, :]))
            yield
            HR2 = P_HR.get()
            HR2v = vv(HR2, HR2.ap[:, :, 0:NP])
            k.cp(HR2v, vv(pO, pO.ap[:, 0:2 * NP].rearrange("p (a n) -> p a n", n=NP)), eng="act")
            psfree(pO)
            k.stt(S, S, vv(DEC, DEC.ap[:, h, o["dcol"]:o["dcol"] + 1]), vv(pU, pU.ap[:, 0:256]), ALU.mult, ALU.add)
            psfree(pU)
            yield
            yield from hn_gen(HR2v, NP, "gla_norm", h, ZGR + 2 * h, ZGQ + h, g0)

        sidx = [0]
        CSB = []
        SSB = []
        PFD = 4
        OWW = 4

        def gh_sample_gen(ctx, h):
            g, g0, NP, tc, gc, lc = ctx["g"], ctx["g0"], ctx["NP"], ctx["tc"], ctx["gc"], ctx["lc"]
            owners = g["owners"]
            nown = len(owners)
            fr = yield from gh_common_front(ctx, h)
            WT, AB, PTv, QA = fr["WT"], fr["AB"], fr["PTv"], fr["QA"]
            gf = yield from gla_front(ctx, h)
            QN = P_QN.get()
            pN = [(yield from psw()), (yield from psw())]
            pD = yield from psw()
            pDN = yield from psw()
            Dv = vv(pD, pD.ap[:, 0:NP])
            for vc in range(2):
                k.mm(vv(pN[vc], pN[vc].ap[:, 0:NP]), vv(mv_tm[tc], mv_tm[tc].ap[0:NP, h, vc * 128:(vc + 1) * 128]), PTv,
                     start=True, stop=False)
            k.mm(Dv, ctx["onesNP"], PTv, start=True, stop=False)

            def load_C(oj):
                Cj = CSB[oj % 8]
                k.dma("sp", Cj.ap, Cs_in[owners[oj]["i"], h].rearrange("(c p) d -> p c d", p=128), writes=Cj.bufs)

            def load_S(oj):
                Sj = SSB[oj % 8]
                k.dma("sp", Sj.ap, Ss_in[owners[oj]["i"], h], writes=Sj.bufs)

            for oj in range(PFD):
                load_C(oj)

            def own_m(oi, o):
                i = o["i"]
                oc = slice(o["c0"], o["c0"] + o["n"])
                last = o["c0"] + o["n"] - 1
                C = CSB[oi % 8]
                if oi + PFD < nown:
                    load_C(oi + PFD)
                nT = vv(nTs, nTs.ap[:, h * NS + i:h * NS + i + 1])
                pC = yield from psw()
                for vc in range(2):
                    k.tr(vv(pC, pC.ap[:, vc * 128:(vc + 1) * 128]), vv(C, C.ap[:, vc, :]), ident_f)
                k.ts(vv(QN, QN.ap[:, oc]), vv(QA, QA.ap[:, oc]), nT, ALU.mult, eng=PEN)
                KW = P_KW.get()
                KWv = vv(KW, KW.ap[0:NP, :])
                k.ts(KWv, vv(mk_tm[tc], mk_tm[tc].ap[0:NP, h * 128:(h + 1) * 128]), vv(WT, WT.ap[0:NP, last:last + 1]),
                     ALU.mult, eng=PEN)
                yield
                ctb = P_CTB.get()
                k.cp(ctb, vv(pC, pC.ap[:, 0:256]), eng="act")
                for vc in range(2):
                    k.mm(vv(pN[vc], pN[vc].ap[:, oc]), vv(ctb, ctb.ap[:, vc * 128:(vc + 1) * 128]), vv(QA, QA.ap[:, oc]),
                         start=False, stop=(oi == nown - 1))
                for vc in range(2):
                    k.mm(vv(pC, pC.ap[:, 256 + vc * 128:256 + (vc + 1) * 128]),
                         vv(mv_tm[tc], mv_tm[tc].ap[0:NP, h, vc * 128:(vc + 1) * 128]), KWv)
                k.mm(vv(pDN, pDN.ap[:, oi:oi + 1]), KWv, vv(ones_b, ones_b.ap[0:NP, 0:1]))
                yield
                aend = vv(AB, AB.ap[:, 0, last:last + 1])
                Cf = vv(C, C.ap.rearrange("p a d -> p (a d)"))
                k.stt(Cf, Cf, aend, vv(pC, pC.ap[:, 256:512]), ALU.mult, ALU.add)
                psfree(pC)
                k.dma(STQ, Cs_o[i, h].rearrange("(c p) d -> p c d", p=128), C.ap, reads=C.bufs, store=True)

            for oj in range(PFD - 1):
                load_S(oj)
            yield from inter_gen([own_m(oi, o) for oi, o in enumerate(owners)], OWW)
            k.mm(Dv, ones_b, vv(QN, QN.ap[:, 0:NP]), start=False, stop=True)
            nTh = vv(nTs, nTs.ap[:, h * NS:(h + 1) * NS])
            aend16 = vv(AB, AB.ap[:, 0, 0:NP].rearrange("p (i t) -> p i t", t=4)[:, :, 3])
            k.tt(nTh, nTh, aend16, ALU.mult)
            k.tt(nTh, nTh, vv(pDN, pDN.ap[:, 0:NS]), ALU.add)
            psfree(pDN)
            HD = P_HD.get()
            k.act(vv(HD, HD.ap[:, 0:NP]), Dv, AF.Abs)
            psfree(pD)
            k.tt(vv(HD, HD.ap[:, 0:NP]), vv(HD, HD.ap[:, 0:NP]), vv(AB, AB.ap[:, 1, 0:NP]), ALU.max)
            RD = P_RD.get()
            RDv = vv(RD, RD.ap[:, 0:NP])
            k.act(vv(HD, HD.ap[:, 0:NP]), vv(HD, HD.ap[:, 0:NP]), AF.Ln)
            k.act(RDv, vv(HD, HD.ap[:, 0:NP]), AF.Exp, scale=-1.0)
            HR = P_HR.get()
            HRv = vv(HR, HR.ap[:, :, 0:NP])
            for vc in range(2):
                k.tt(vv(HR, HR.ap[:, vc, 0:NP]), vv(pN[vc], pN[vc].ap[:, 0:NP]), RDv, ALU.mult)
                psfree(pN[vc])
            hn_m = hn_gen(HRv, NP, "mlstm_norm", h, ZMO + 2 * h, ZMQ + h, g0)
            gq = Z[ZGQ + h]
            PT2v, KD = gf["PT2v"], gf["KD"]
            pO = [(yield from psw()), (yield from psw())]
            for vc in range(2):
                k.mm(vv(pO[vc], pO[vc].ap[:, 0:NP]), vv(gv_tm[tc], gv_tm[tc].ap[0:NP, h, vc * 128:(vc + 1) * 128]),
                     PT2v, start=True, stop=False)

            def own_g(oi, o):
                i = o["i"]
                S = SSB[oi % 8]
                if oi + PFD - 1 < nown:
                    load_S(oi + PFD - 1)
                KW = P_KW.get()
                KDo = vv(KW, KW.ap[0:NP, :])
                k.ts(KDo, vv(KD, KD.ap[0:NP, :]), vv(ind, ind.ap[:, i:i + 1]), ALU.mult, eng=PEN)
                pU = yield from psw()
                k.mm(vv(pU, pU.ap[:, 0:256]), KDo, vv(gv_tm[tc], gv_tm[tc].ap[0:NP, h, :]))
                yield
                sbf = P_SBF.get()
                k.cp(sbf, S, eng="act")
                for vc in range(2):
                    k.mm(vv(pO[vc], pO[vc].ap[:, o["c0"]:o["c0"] + o["n"]]), vv(sbf, sbf.ap[:, vc * 128:(vc + 1) * 128]),
                         vv(gq, gq.ap[:, g0 + o["c0"]:g0 + o["c0"] + o["n"]]), start=False, stop=(oi == nown - 1))
                yield
                k.stt(S, S, vv(DEC, DEC.ap[:, h, o["dcol"]:o["dcol"] + 1]), vv(pU, pU.ap[:, 0:256]), ALU.mult, ALU.add)
                psfree(pU)
                k.dma(STQ, Ss_o[i, h], S.ap, reads=S.bufs, store=True)

            yield from inter_gen([hn_m] + [own_g(oi, o) for oi, o in enumerate(owners)], OWW + 1)
            HR2 = P_HR.get()
            HR2v = vv(HR2, HR2.ap[:, :, 0:NP])
            for vc in range(2):
                k.cp(vv(HR2, HR2.ap[:, vc, 0:NP]), vv(pO[vc], pO[vc].ap[:, 0:NP]), eng="act")
                psfree(pO[vc])
            yield from hn_gen(HR2v, NP, "gla_norm", h, ZGR + 2 * h, ZGQ + h, g0)

        def head_norm(HRv, NP, gname, h, gate, dst, g0, gla_h=None):
            if dst is None:
                dst = [Z[ZGQ + gla_h], Z[ZGK + gla_h]]
            SQ = P_SQ.get()
            SQv = vv(SQ, SQ.ap[:, :, 0:NP])
            k.act(SQv, HRv, AF.Square)
            p = ps()
            pv = vv(p, p.ap[:, 0:NP])
            for vc in range(2):
                k.mm(pv, ones_b, vv(SQ, SQ.ap[:, vc, 0:NP]), start=(vc == 0), stop=(vc == 1))
            lv = vv(lnv, lnv.ap[:, 0:NP])
            rv = vv(rstd, rstd.ap[:, 0:NP])
            k.act(lv, pv, AF.Ln, bias=EPSC, scale=1.0 / 256.0)
            k.act(rv, lv, AF.Exp, scale=-0.5)
            T1 = P_T1.get()
            for vc in range(2):
                k.stt(vv(T1, T1.ap[:, vc, 0:NP]), vv(HRv, HRv.ap[:, vc, :]), gain(gname, h * 2 + vc), rv, ALU.mult,
                      ALU.mult)
                k.tt(vv(dst[vc], dst[vc].ap[:, g0:g0 + NP]), vv(T1, T1.ap[:, vc, 0:NP]),
                     vv(gate[vc], gate[vc].ap[:, g0:g0 + NP]), ALU.mult)

        def cross_attn(j, cgs, with_s):
            rmsnorm(xT, "ca_norm", hT, cgs)
            qT = Z[0:8]
            aT = Z[8:16]
            for half in range(2):
                sq = slab(W["ca_wq"], 8, half * 512, 512)
                for f in range(4):
                    for (c0, n) in cgs:
                        pv = fm_mm(sq, 8, f * 128, 128, hT, c0, n)
                        k.act(vv(qT[half * 4 + f], qT[half * 4 + f].ap[:, c0:c0 + n]), pv, AF.Identity, scale=1.0 / 16.0)
            for h in range(4):
                ET = P_ET.get()
                for mc in range(2):
                    p = ps()
                    for c in range(2):
                        k.mm(p, vv(mkT_mem, mkT_mem.ap[:, h * 2 + c, mc * 128:(mc + 1) * 128]),
                             vv(qT[h * 2 + c], qT[h * 2 + c].ap[:, 0:512]), start=(c == 0), stop=(c == 1))
                    k.act(vv(ET, ET.ap[:, mc, :]), p, AF.Exp)
                pd = ps()
                for mc in range(2):
                    k.mm(pd, ones_b, vv(ET, ET.ap[:, mc, :]), start=(mc == 0), stop=(mc == 1))
                k.act(lnv, pd, AF.Ln)
                k.act(rstd, lnv, AF.Exp, scale=-1.0)
                for c in range(2):
                    p = ps()
                    for mc in range(2):
                        k.mm(p, vv(mv_mem, mv_mem.ap[:, mc, h * 256 + c * 128:h * 256 + (c + 1) * 128]),
                             vv(ET, ET.ap[:, mc, :]), start=(mc == 0), stop=(mc == 1))
                    k.tt(vv(aT[h * 2 + c], aT[h * 2 + c].ap[:, 0:512]), p, rstd, ALU.mult)
            MARKS.append(("ca_s", j, k.eng["pe"].cnt))
            so_pre = None
            if with_s:
                so_pre = [slab(W["ca_wo"], 8, half * 512, 512) for half in range(2)]
                halves = [V(TM_t[:, q, 0:2048].bitcast(F32), [TMb[q]]) for q in range(4)]
                halves += [V(Z_t[:, c0:c0 + 4, :].rearrange("p c n -> p (c n)")[:, 0:2048].bitcast(F32), Zb[c0:c0 + 4])
                           for c0 in (16, 20, 24, 28)]
                for si_ in (k.slot_i % NSLOT, (k.slot_i + 1) % NSLOT):
                    subs = borrow_slot(si_, False)
                    f = slots[si_][0][:, 0:4096].bitcast(F32)
                    halves.append(V(f[:, 0:1024], [sb_.bufs[0] for sb_ in subs[0:4]]))
                    halves.append(V(f[:, 1024:2048], [sb_.bufs[0] for sb_ in subs[4:8]]))
                sets = [halves[0:4], halves[4:8], halves[8:12]]
                NPRE = 2
                Vbf = [V(TM_t[:, 4, 0:2048].rearrange("p (m n) -> p m n", n=D), [TMb[4]]),
                       V(Z_t[:, 32:36, :].rearrange("p c n -> p (c n)")[:, 0:2048].rearrange("p (m n) -> p m n", n=D),
                         Zb[32:36])]

                def load_kv(i):
                    K0, K1, V0, V1 = sets[i % 3]
                    for mc, (Kh, Vh) in enumerate([(K0, V0), (K1, V1)]):
                        k.dma("sp", Kh.ap, ck_in[i, mc * 128:(mc + 1) * 128, :], writes=Kh.bufs)
                    for mc, (Kh, Vh) in enumerate([(K0, V0), (K1, V1)]):
                        k.dma("sp", Vh.ap, cv_in[i, mc * 128:(mc + 1) * 128, :], writes=Vh.bufs)

                for i in range(NPRE):
                    load_kv(i)
                for i in range(NS):
                    if i + NPRE < NS:
                        load_kv(i + NPRE)
                    K0, K1, V0, V1 = sets[i % 3]
                    Kh = [K0, K1]
                    Vh = [V0, V1]
                    KT = KTs8[i % 2]
                    Vb = Vbf[i % 2]
                    k.cp(vv(Vb, Vb.ap[:, 0, :]), Vh[0], eng="pool")
                    k.cp(vv(Vb, Vb.ap[:, 1, 0:512]), vv(Vh[1], Vh[1].ap[:, 0:512]), eng="dve")
                    k.cp(vv(Vb, Vb.ap[:, 1, 512:1024]), vv(Vh[1], Vh[1].ap[:, 512:1024]), eng="act")
                    for hc in range(8):
                        pK = ps()
                        for mc in range(2):
                            k.tr(vv(pK, pK.ap[:, mc * 128:(mc + 1) * 128]), vv(Kh[mc], Kh[mc].ap[:, hc * 128:(hc + 1) * 128]),
                                 ident_f)
                        k.cpx(KT[hc], vv(pK, pK.ap[:, 0:256]))
                    sc = slice(512 + 4 * i, 512 + 4 * i + 4)
                    p = ps()
                    for h in range(4):
                        for mc in range(2):
                            col = (h * 2 + mc) * 4
                            for c in range(2):
                                kt = KT[h * 2 + c]
                                k.mm(vv(p, p.ap[:, col:col + 4]), vv(kt, kt.ap[:, mc * 128:(mc + 1) * 128]),
                                     vv(qT[h * 2 + c], qT[h * 2 + c].ap[:, sc]), start=(c == 0), stop=(c == 1))
                    k.act(ETs, vv(p, p.ap[:, 0:32]), AF.Exp)
                    pd = ps()
                    e4 = ETs.ap.rearrange("p (h m t) -> p h m t", h=4, m=2)
                    for mc in range(2):
                        k.mm(vv(pd, pd.ap[:, 0:16].rearrange("p (h t) -> p h t", t=4)), ones_b, vv(ETs, e4[:, :, mc, :]),
                             start=(mc == 0), stop=(mc == 1))
                    k.act(LNs, vv(pd, pd.ap[:, 0:16]), AF.Ln)
                    k.act(RDs, LNs, AF.Exp, scale=-1.0)
                    po = ps()
                    for h in range(4):
                        for c in range(2):
                            col = (h * 2 + c) * 4
                            for mc in range(2):
                                k.mm(vv(po, po.ap[:, col:col + 4]),
                                     vv(Vb, Vb.ap[:, mc, h * 256 + c * 128:h * 256 + (c + 1) * 128]),
                                     vv(ETs, e4[:, h, mc, :]), start=(mc == 0), stop=(mc == 1))
                    for h in range(4):
                        for c in range(2):
                            col = (h * 2 + c) * 4
                            k.tt(vv(aT[h * 2 + c], aT[h * 2 + c].ap[:, sc]), vv(po, po.ap[:, col:col + 4]),
                                 vv(RDs, RDs.ap[:, h * 4:h * 4 + 4]), ALU.mult)
            for half in range(2):
                so = so_pre[half] if so_pre is not None else slab(W["ca_wo"], 8, half * 512, 512)
                for q in range(4):
                    dc = half * 4 + q
                    for (c0, n) in cgs:
                        pv = fm_mm(so, 8, q * 128, 128, aT, c0, n)
                        xv = vv(xT[dc], xT[dc].ap[:, c0:c0 + n])
                        k.tt(xv, pv, xv, ALU.add)

        ONEC = k.sbv("onec", [128, 1], F32)
        k.memset(ONEC, 1.0)
        ONEC4 = vv(ONEC, ONEC.ap[0:4, :])
        ONES4 = vv(ones_c, ones_c.ap[0:4, :])

        STAGE = 99
        TILES = [0, 1, 2, 3]
        MARKS = []
        for j in TILES:
            with_s = (j == NTILE - 1)
            cgs = [(0, 512)] + ([(512, 64)] if with_s else [])
            def mark(nm):
                MARKS.append((nm, j, k.eng["pe"].cnt))
            mark("start")
            if j == TILES[0] and STAGE >= 1:
                mem_kv()
            mark("load_x")
            load_x(j, with_s, fetched=(j != TILES[0]))
            mark("ffn1")
            if STAGE >= 2:
                ffn("ffn1", cgs)
            mark("mix")
            if STAGE >= 3:
                mix(j, cgs, with_s)
            mark("ca")
            if STAGE >= 4:
                cross_attn(j, cgs, with_s)
            mark("ffn2")
            if j + 1 < NTILE:
                fetch_x(j + 1, j + 1 == NTILE - 1)
            if STAGE >= 5:
                ffn("ffn2", cgs)
            mark("final")
            final_out(j, cgs, with_s)
            mark("end")

        if STAGE >= 3:
            for h in range(4):
                k.dma("sp", Cp_o[h].rearrange("(c p) d -> p c d", p=128), Cst[h].ap, reads=Cst[h].bufs, store=True)
                k.dma("sp", Sp_o[h], Sst[h].ap, reads=Sst[h].bufs, store=True)
            p = ps()
            k.tr(vv(p, p.ap[0:4, 0:128]), nTp, ident_f)
            npst = V(n_tm.ap[0:4, :], n_tm.bufs)
            k.cp(npst, vv(p, p.ap[0:4, 0:128]), eng="act")
            k.dma("sp", np_o, npst.ap, reads=npst.bufs, store=True)
            k.dma("sp", mp_o, mcar.ap, reads=mcar.bufs, store=True)
            if 3 in TILES:
              p = ps()
              k.tr(vv(p, p.ap[0:64, 0:128]), nTs, ident_f)
              k.cp(n_tm, vv(p, p.ap[0:64, 0:128]), eng="act")
              k.dma("sp", ns_o, n_tm.ap, reads=n_tm.bufs, store=True)
              k.dma("sp", ms_o, msT.ap, reads=msT.bufs, store=True)

        fin = {}
        for (s, v) in k.final:
            key = id(s)
            if key not in fin or fin[key][1] < v:
                fin[key] = (s, v)

        def replay(h, e):
            for waits, fn, inc in e.prog:
                for (s, v) in waits:
                    h.wait_ge(s, v)
                fn(h).then_inc(inc[0], inc[1])

        print("SBUF bytes remaining/partition:", nc.sbuf_bytes_remaining() if callable(nc.sbuf_bytes_remaining)
              else nc.sbuf_bytes_remaining, "instr:", {n: len(e.prog) for n, e in k.eng.items()})
        with nc.Block() as block:
            @block.tensor
            def _(h):
                replay(h, k.eng["pe"])

            @block.scalar
            def _(h):
                replay(h, k.eng["act"])

            @block.vector
            def _(h):
                replay(h, k.eng["dve"])

            @block.gpsimd
            def _(h):
                replay(h, k.eng["pool"])

            @block.sync
            def _(h):
                replay(h, k.eng["sp"])
                for (s, v) in fin.values():
                    h.wait_ge(s, v)
    return nc


_NC_CACHE = {}


def kernel(**inputs):
    f = lambda a: np.ascontiguousarray(np.asarray(a, dtype=np.float32))
    if "nc" not in _NC_CACHE:
        _NC_CACHE["nc"] = build_nc()
    nc = _NC_CACHE["nc"]
    wnames = ["ffn1_norm", "ffn1_wg", "ffn1_wu", "ffn1_wd", "mix_norm", "w_in", "b_if", "gla_wa2", "gla_ba",
              "mlstm_norm", "gla_norm", "w_br_m", "w_br_g", "w_out", "ca_norm", "mem_norm", "ca_wq", "ca_wk", "ca_wv",
              "ca_wo", "ffn2_norm", "ffn2_wg", "ffn2_wu", "ffn2_wd"]
    wd = {n: f(inputs[n][0]) for n in wnames}
    wd["final_norm"] = f(inputs["final_norm"])
    in_maps = []
    for c in range(NCORES):
        s0, s1 = c * NS, (c + 1) * NS
        m = dict(wd)
        m["x_p"] = f(inputs["x_prompt"][c])
        m["x_s"] = f(inputs["x_sample"][s0:s1].reshape(NS * STK, D))
        m["mem"] = f(inputs["mem_prompt"][c])
        m["C_s"] = f(inputs["state_mlstm_C"][0, s0:s1])
        m["n_s"] = f(inputs["state_mlstm_n"][0, s0:s1].reshape(NS * 4, 128))
        m["m_s"] = f(inputs["state_mlstm_m"][0, s0:s1])
        m["S_s"] = f(inputs["state_gla_S"][0, s0:s1])
        m["ck"] = f(inputs["cache_mem_k"][0, s0:s1].reshape(NS, 256, D))
        m["cv"] = f(inputs["cache_mem_v"][0, s0:s1].reshape(NS, 256, D))
        in_maps.append(m)
    res = run_bass_kernel_spmd(nc, in_maps, core_ids=list(range(NCORES)))
    R = res.results
    y_prompt = np.stack([R[c]["y_p"] for c in range(NCORES)]).astype(np.float32)
    y_sample = np.concatenate([R[c]["y_s"].reshape(NS, STK, D) for c in range(NCORES)]).astype(np.float32)
    Cp = np.stack([R[c]["Cp"] for c in range(NCORES)])[None].astype(np.float32)
    npp = np.stack([R[c]["np"] for c in range(NCORES)])[None].astype(np.float32)
    mp = np.stack([R[c]["mp"].reshape(4) for c in range(NCORES)])[None].astype(np.float32)
    Sp = np.stack([R[c]["Sp"] for c in range(NCORES)])[None].astype(np.float32)
    mkp = np.stack([R[c]["mkp"].reshape(256, 4, 256) for c in range(NCORES)])[None].astype(np.float32)
    mvp = np.stack([R[c]["mvp"].reshape(256, 4, 256) for c in range(NCORES)])[None].astype(np.float32)
    Cs = np.concatenate([R[c]["Cs"] for c in range(NCORES)])[None].astype(np.float32)
    ns = np.concatenate([R[c]["ns"].reshape(4, NS, 128).transpose(1, 0, 2) for c in range(NCORES)])[None].astype(
        np.float32)
    ms = np.concatenate([R[c]["ms"].reshape(4, NS).T for c in range(NCORES)])[None].astype(np.float32)
    Ss = np.concatenate([R[c]["Ss"] for c in range(NCORES)])[None].astype(np.float32)
    return (y_prompt, y_sample, Cp, npp, mp, Sp, mkp, mvp, Cs, np.ascontiguousarray(ns), np.ascontiguousarray(ms), Ss)
```

```python
import numpy as np
from contextlib import ExitStack
import concourse.bass as bass
import concourse.mybir as mybir
from concourse.bass_utils import run_bass_kernel_spmd

F32 = mybir.dt.float32
BF16 = mybir.dt.bfloat16
AF = mybir.ActivationFunctionType
ALU = mybir.AluOpType

NCORES = 8
D = 1024
DFF = 2816
SEQ = 2048
TT = 512
NTILE = 4
NS = 16
STK = 4
NCOL = 576
INW = 8216
EPS = 1e-6
MQ, MK, MV, MO, MI, MF, GQ, GK, GV, GR, GA, GM, GG = (0, 512, 1024, 2048, 3072, 3076, 3080, 3592, 4104, 5128,
                                                      6152, 6168, 7192)
NSLOT = 4
SLAB_ELEMS = 8 * 528
NZ = 36
BIG = 1.0e30
ZMQ, ZMK, ZMO, ZGQ, ZGK, ZGR, ZKD = 0, 4, 8, 16, 20, 24, 32


class Buf:
    __slots__ = ("name", "w", "r", "dsem", "dcnt", "excl")

    def __init__(self, name):
        self.name = name
        self.excl = name.startswith("ps")
        self.w = {}
        self.r = {}
        self.dsem = None
        self.dcnt = 0


class V:
    __slots__ = ("ap", "bufs")

    def __init__(self, ap, bufs):
        self.ap = ap
        self.bufs = bufs


class Eng:
    def __init__(self, name, sem):
        self.name = name
        self.sem = sem
        self.cnt = 0
        self.seen = {}
        self.prog = []


class KB:
    def __init__(self, nc, stack):
        self.nc = nc
        self.stack = stack
        self.eng = {n: Eng(n, stack.enter_context(nc.semaphore("s_" + n))) for n in ("pe", "act", "dve", "pool", "sp")}
        self.final = []
        self.nbuf = 0
        self.ps_i = 0
        self.slot_i = 0
        self.tog = 0

    def buf(self, name=None):
        self.nbuf += 1
        return Buf(name or ("b%d" % self.nbuf))

    def sb(self, name, shape, dt=F32):
        return self.stack.enter_context(self.nc.sbuf_tensor(name, shape, dt))

    def sbv(self, name, shape, dt=F32):
        t = self.sb(name, shape, dt)
        return V(t[:], [self.buf(name)])

    def _waits(self, e, reads, writes):
        need = {}

        def add(d, war):
            for k, (s, v) in d.items():
                if k == e.name and e.name in ("pe", "sp"):
                    continue
                if k not in need or need[k][1] < v:
                    need[k] = (s, v)
        for b in reads:
            add(b.w, False)
            if b.excl:
                add(b.r, True)
        for b in writes:
            add(b.w, False)
            add(b.r, True)
        out = []
        for k, (s, v) in need.items():
            if e.seen.get(k, 0) >= v:
                continue
            e.seen[k] = v
            out.append((s, v))
        return out

    def op(self, en, fn, reads=(), writes=()):
        e = self.eng[en]
        waits = self._waits(e, reads, writes)
        e.cnt += 1
        ev = (e.sem, e.cnt)
        e.prog.append((waits, fn, (e.sem, 1)))
        for b in reads:
            b.r[en] = ev
        for b in writes:
            b.w[en] = ev
            b.r = {}

    def dma(self, qn, out, in_, reads=(), writes=(), store=False, **kw):
        e = self.eng[qn]
        waits = self._waits(e, reads, writes)
        tb = writes[0] if writes else reads[0]
        if tb.dsem is None:
            tb.dsem = {}
        if qn not in tb.dsem:
            tb.dsem[qn] = [self.stack.enter_context(self.nc.semaphore("d_%s_%s" % (tb.name, qn))), 0]
        ds = tb.dsem[qn]
        ds[1] += 16
        dsem = ds[0]
        ev = (dsem, ds[1])
        key = "d_%s_%s" % (tb.name, qn)
        e.prog.append((waits, (lambda h: h.dma_start(out=out, in_=in_, **kw)), (dsem, 16)))
        for b in reads:
            b.r[key] = ev
        for b in writes:
            b.w[key] = ev
            b.r = {}
        self.final.append(ev)

    @staticmethod
    def _bufs(*vs):
        out = []
        for v in vs:
            if isinstance(v, V):
                out.extend(v.bufs)
        return out

    @staticmethod
    def _ap(v):
        return v.ap if isinstance(v, V) else v

    def mm(self, out, lhsT, rhs, start=True, stop=True):
        o, l, r = out.ap, lhsT.ap, rhs.ap
        self.op("pe", lambda h: h.matmul(o, lhsT=l, rhs=r, start=start, stop=stop),
                reads=self._bufs(lhsT, rhs), writes=out.bufs)

    def tr(self, out, in_, ident):
        o, i, d = out.ap, in_.ap, ident.ap
        self.op("pe", lambda h: h.transpose(o, i, d), reads=self._bufs(in_, ident), writes=out.bufs)

    def act(self, out, in_, func, bias=None, scale=None, accum=None, eng="act"):
        o, i = out.ap, in_.ap
        kw = {}
        if bias is not None:
            kw["bias"] = self._ap(bias)
        if scale is not None:
            kw["scale"] = self._ap(scale)
        if accum is not None:
            kw["accum_out"] = accum.ap
        w = out.bufs + (accum.bufs if accum is not None else [])
        self.op("act", lambda h: h.activation(out=o, in_=i, func=func, **kw),
                reads=self._bufs(in_, bias, scale), writes=w)

    def tt(self, out, in0, in1, op, eng="dve"):
        o, a, b = out.ap, in0.ap, in1.ap
        self.op(eng, lambda h: h.tensor_tensor(out=o, in0=a, in1=b, op=op), reads=self._bufs(in0, in1), writes=out.bufs)

    def ts(self, out, in0, s1, op0, s2=None, op1=None, eng="dve"):
        o, a = out.ap, in0.ap
        if eng == "pool" and op1 is None and op0 == ALU.mult:
            op1, s2 = ALU.mult, 1.0
        x1, x2 = self._ap(s1), self._ap(s2)
        if op1 is None:
            fn = lambda h: h.tensor_scalar(out=o, in0=a, scalar1=x1, scalar2=None, op0=op0)
        else:
            fn = lambda h: h.tensor_scalar(out=o, in0=a, scalar1=x1, scalar2=x2, op0=op0, op1=op1)
        self.op(eng, fn, reads=self._bufs(in0, s1, s2), writes=out.bufs)

    def stt(self, out, in0, scalar, in1, op0, op1):
        o, a, b = out.ap, in0.ap, in1.ap
        s = self._ap(scalar)
        self.op("dve", lambda h: h.scalar_tensor_tensor(out=o, in0=a, scalar=s, in1=b, op0=op0, op1=op1),
                reads=self._bufs(in0, scalar, in1), writes=out.bufs)

    def cp(self, out, in_, eng="dve"):
        o, i = out.ap, in_.ap
        if eng == "act":
            self.op("act", lambda h: h.activation(out=o, in_=i, func=AF.Identity), reads=in_.bufs, writes=out.bufs)
        else:
            self.op(eng, lambda h: h.tensor_copy(out=o, in_=i), reads=in_.bufs, writes=out.bufs)

    def cpx(self, out, in_):
        self.tog ^= 1
        self.cp(out, in_, eng="act" if self.tog else "dve")

    def scan(self, out, d0, d1, initial, op0, op1):
        o, a, b = out.ap, d0.ap, d1.ap
        ini = self._ap(initial)
        self.op("dve", lambda h: h.tensor_tensor_scan(out=o, data0=a, data1=b, initial=ini, op0=op0, op1=op1),
                reads=self._bufs(d0, d1, initial), writes=out.bufs)

    def memset(self, out, val, eng="pool"):
        o = out.ap
        self.op(eng, lambda h: h.memset(o, val), writes=out.bufs)

    def asel(self, out, in_, pattern, cmp, fill, base, cm):
        o, i = out.ap, in_.ap
        self.op("pool", lambda h: h.affine_select(out=o, in_=i, pattern=pattern, compare_op=cmp, fill=fill, base=base,
                                                  channel_multiplier=cm), reads=in_.bufs, writes=out.bufs)


def vv(v, ap):
    return V(ap, v.bufs)


def build_nc():
    nc = bass.Bass("TRN2", target_bir_lowering=False)

    def di(n, s):
        return nc.dram_tensor(n, s, F32, kind="ExternalInput").ap()

    def do(n, s):
        return nc.dram_tensor(n, s, F32, kind="ExternalOutput").ap()

    x_p = di("x_p", [SEQ, D])
    x_s = di("x_s", [NS * STK, D])
    mem = di("mem", [256, D])
    Cs_in = di("C_s", [NS, 4, 256, 128])
    ns_in = di("n_s", [NS * 4, 128])
    ms_in = di("m_s", [NS, 4])
    Ss_in = di("S_s", [NS, 4, 128, 256])
    ck_in = di("ck", [NS, 256, D])
    cv_in = di("cv", [NS, 256, D])
    W = {}
    for n, s in [("ffn1_norm", [D]), ("ffn1_wg", [D, DFF]), ("ffn1_wu", [D, DFF]), ("ffn1_wd", [DFF, D]),
                 ("mix_norm", [D]), ("w_in", [D, INW]), ("b_if", [8]), ("gla_wa2", [16, 512]), ("gla_ba", [512]),
                 ("mlstm_norm", [D]), ("gla_norm", [D]), ("w_br_m", [D, D]), ("w_br_g", [D, D]), ("w_out", [D, D]),
                 ("ca_norm", [D]), ("mem_norm", [D]), ("ca_wq", [D, D]), ("ca_wk", [D, D]), ("ca_wv", [D, D]),
                 ("ca_wo", [D, D]), ("ffn2_norm", [D]), ("ffn2_wg", [D, DFF]), ("ffn2_wu", [D, DFF]),
                 ("ffn2_wd", [DFF, D]), ("final_norm", [D])]:
        W[n] = di(n, s)
    y_p = do("y_p", [SEQ, D])
    y_s = do("y_s", [NS * STK, D])
    Cp_o = do("Cp", [4, 256, 128])
    np_o = do("np", [4, 128])
    mp_o = do("mp", [4, 1])
    Sp_o = do("Sp", [4, 128, 256])
    mkp_o = do("mkp", [256, D])
    mvp_o = do("mvp", [256, D])
    Cs_o = do("Cs", [NS, 4, 256, 128])
    ns_o = do("ns", [4 * NS, 128])
    ms_o = do("ms", [4, NS])
    Ss_o = do("Ss", [NS, 4, 128, 256])

    with ExitStack() as stack:
        k = KB(nc, stack)

        xT_t = k.sb("xT", [128, 8, NCOL], F32)
        xT = [V(xT_t[:, i, :], [k.buf("xT%d" % i)]) for i in range(8)]
        hT_t = k.sb("hT", [128, 8, NCOL], BF16)
        hT = [V(hT_t[:, i, :], [k.buf("hT%d" % i)]) for i in range(8)]
        Z_t = k.sb("Z", [128, NZ, NCOL], BF16)
        Zb = [k.buf("Z%d" % i) for i in range(NZ)]
        Z = [V(Z_t[:, i, :], [Zb[i]]) for i in range(NZ)]
        TMW = 2564
        TM_t = k.sb("TM", [128, 5, TMW], BF16)
        TMb = [k.buf("TM%d" % i) for i in range(5)]
        mk_tm = [V(TM_t[:, p, 0:512], [TMb[p]]) for p in range(5)]
        mv_tm = [V(TM_t[:, p, 512:512 + 1028].rearrange("p (h v) -> p h v", v=257), [TMb[p]]) for p in range(5)]
        gv_tm = [V(TM_t[:, p, 1540:2564].rearrange("p (h v) -> p h v", v=256), [TMb[p]]) for p in range(5)]
        stg = [V(TM_t[:, p, 0:2048].bitcast(F32), [TMb[p]]) for p in range(5)]
        slots = []
        for i in range(NSLOT):
            t = k.sb("slab%d" % i, [128, SLAB_ELEMS], BF16)
            slots.append((t, k.buf("slab%d" % i)))
        NPS = 8
        PS = []
        for i in range(NPS):
            t = stack.enter_context(nc.psum_tensor("ps%d" % i, [128, 512], F32))
            PS.append(V(t[:], [k.buf("ps%d" % i)]))

        def psb():
            p = ps()
            return V(p.ap.bitcast(BF16), p.bufs)

        pinned = set()

        def ps(pin=False):
            while (k.ps_i % NPS) in pinned:
                k.ps_i += 1
            idx = k.ps_i % NPS
            k.ps_i += 1
            if pin:
                pinned.add(idx)
            return PS[idx]

        def unpin(p):
            pinned.discard(PS.index(p))


        ident_f = k.sbv("ident_f", [128, 128], F32)
        ident_b = k.sbv("ident_b", [128, 128], BF16)
        ones_b = k.sbv("ones_b", [128, 128], BF16)
        ones_c = k.sbv("ones_c", [128, 1], F32)
        sel = k.sbv("sel", [4, 4, 128], BF16)
        maskbigP = k.sbv("maskbigP", [128, 128], F32)
        mask01P = k.sbv("mask01P", [128, 128], F32)
        maskbigS = k.sbv("maskbigS", [64, 64], F32)
        mask01S = k.sbv("mask01S", [64, 64], F32)
        ind = k.sbv("ind", [64, NS], F32)
        CT = k.sbv("consts", [128, 72], F32)
        gcol = {"ffn1_norm": 0, "mix_norm": 8, "ca_norm": 16, "mem_norm": 24, "ffn2_norm": 32, "final_norm": 40,
                "mlstm_norm": 48, "gla_norm": 56}
        negba = k.sbv("negba", [128, 4], F32)
        bi = k.sbv("bi", [4, 1], F32)
        negbf = k.sbv("negbf", [4, 1], F32)
        wa2_b = k.sbv("wa2_b", [16, 512], BF16)

        mkT_mem = k.sbv("mkT_mem", [128, 8, 256], BF16)
        mv_mem = k.sbv("mv_mem", [128, 2, D], BF16)
        Cst = [k.sbv("Cst%d" % h, [128, 2, 128], F32) for h in range(4)]
        Sst = [k.sbv("Sst%d" % h, [128, 256], F32) for h in range(4)]
        nTp = k.sbv("nTp", [128, 4], F32)
        mcar = k.sbv("mcar", [4, 1], F32)
        IG = k.sbv("IG", [4, NCOL], F32)
        E1 = None
        SPm = None
        FNx = k.sbv("FNx", [4, NCOL + 1], F32)
        GGt = IG
        GL2 = [k.sbv("GL0", [4, 128], F32)] * 3
        MG2 = [k.sbv("MG0", [4, 128], F32)] * 3
        AE2 = [k.sbv("AE0", [4, 2, 128], F32)] * 3
        MGH = [k.sbv("MGH%d" % i, [4, 2, 128], BF16) for i in range(3)]
        AEH = [k.sbv("AEH%d" % i, [4, 2, 2, 128], BF16) for i in range(3)]
        TMPm = k.sbv("TMPm", [4, 128], F32)
        gaT = k.sbv("gaT", [16, NCOL], BF16)
        Ge1 = k.sbv("Ge1", [128, NCOL], F32)
        E1 = V(Ge1.ap[0:4, :], Ge1.bufs)
        SPm = E1
        BNx = k.sbv("BNx", [128, NCOL + 1], F32)
        NB5 = k.sbv("NB5", [128, 8], F32)
        carB = k.sbv("carB", [128, 4], F32)
        Earg = k.sbv("Earg", [128, 64], F32)
        Eq3 = [k.sbv("Eq%d" % i, [128, 128], F32) for i in range(3)]
        DEC = k.sbv("DEC", [128, 4, 4 + NS], F32)
        def zf32(c0):
            t = Z_t[:, c0:c0 + 2, :].rearrange("p c n -> p (c n)").bitcast(F32)
            return V(t[:, 0:512], [Zb[c0], Zb[c0 + 1]])
        lnv = zf32(32)
        rstd = zf32(34)
        sgt = [V(Z_t[:, 22 + i, 0:512], [Zb[22 + i]]) for i in range(2)]
        mrg = [lnv, rstd]
        NSTG = 4
        STQ = "sp"
        STG_t = k.sb("STG", [128, 2 * NSTG, 256], F32)
        STGb = [k.buf("STG%d" % i) for i in range(2 * NSTG)]
        Cstg = [V(STG_t[:, i, :].rearrange("p (a d) -> p a d", d=128), [STGb[i]]) for i in range(NSTG)]
        Sstg = [V(STG_t[:, NSTG + i, :], [STGb[NSTG + i]]) for i in range(NSTG)]
        nTs = k.sbv("nTs", [128, 64], F32)
        n_tm = k.sbv("n_tm", [64, 128], F32)
        m0T = k.sbv("m0T", [4, NS], F32)
        msT = k.sbv("msT", [4, NS], F32)

        class Pool2:
            def __init__(self, name, shape, dt, n=2):
                self.items = [k.sbv("%s_%d" % (name, i), shape, dt) for i in range(n)]
                self.i = 0

            def get(self):
                v = self.items[self.i % len(self.items)]
                self.i += 1
                return v
        P_WT = Pool2("WT", [128, 128], F32, 5)
        P_PT = Pool2("PT", [128, 128], BF16, 6)
        P_AB = Pool2("AB", [128, 2, 128], F32, 5)
        P_QA = Pool2("QA", [128, 128], BF16, 5)
        P_QN = Pool2("QN", [128, 128], BF16, 5)
        P_CTB = Pool2("CTB", [128, 256], BF16, 3)
        P_HD = Pool2("HD", [128, 128], F32)
        P_RD = Pool2("RD", [128, 128], F32)
        P_HR = Pool2("HR", [128, 2, 128], F32, 6)
        P_KW = Pool2("KW", [128, 128], BF16, 6)
        P_GT = Pool2("GT", [128, 4], F32, 3)
        P_SQ = Pool2("SQh", [128, 2, 128], BF16)
        P_T1 = Pool2("T1", [128, 2, 128], F32)
        P_LR = Pool2("LR", [128, 2, 128], F32)
        P_SBF = Pool2("SBF", [128, 256], BF16, 3)
        P_KD = Pool2("KD", [128, 128], BF16, 5)
        class PoolV:
            def __init__(self, items):
                self.items = items
                self.i = 0

            def get(self):
                v = self.items[self.i % len(self.items)]
                self.i += 1
                return v
        P_ET = PoolV([V(TM_t[:, 4, q * 1024:(q + 1) * 1024].rearrange("p (m n) -> p m n", n=512), [TMb[4]])
                      for q in range(2)])
        P_KB = PoolV([V(TM_t[:, q, 0:2048].rearrange("p (m n) -> p m n", n=D), [TMb[q]]) for q in range(4)])
        KTs8 = []
        for q in range(2):
            kt3 = STG_t[:, q * 4:q * 4 + 4, :].rearrange("p a d -> p (a d)").bitcast(BF16).rearrange("p (c m) -> p c m", m=256)
            KTs8.append([V(kt3[:, c, :], [k.buf("KT%d_%d" % (q, c))]) for c in range(8)])
        maskbigPb = k.sbv("maskbigPb", [128, 128], BF16)
        maskbigSb = k.sbv("maskbigSb", [64, 64], BF16)
        ETs = k.sbv("ETs", [128, 32], BF16)
        RDs = k.sbv("RDs", [128, 16], F32)
        LNs = k.sbv("LNs", [128, 16], F32)

        k.memset(ident_f, 0.0)
        k.asel(ident_f, ident_f, [[-1, 128]], ALU.not_equal, 1.0, 0, 1)
        k.cp(ident_b, ident_f, eng="pool")
        k.memset(ones_b, 1.0)
        k.memset(ones_c, 1.0)
        k.memset(sel, 0.0)
        k.asel(sel, sel, [[-1, 4], [0, 128]], ALU.not_equal, 1.0, 0, 1)
        k.memset(maskbigP, 0.0)
        k.asel(maskbigP, maskbigP, [[1, 128]], ALU.is_ge, BIG, 0, -1)
        k.memset(mask01P, 1.0)
        k.asel(mask01P, mask01P, [[1, 128]], ALU.is_ge, 0.0, 0, -1)
        k.memset(maskbigS, 0.0)
        k.asel(maskbigS, maskbigS, [[4, 16], [1, 4]], ALU.is_ge, BIG, 0, -1)
        k.asel(maskbigS, maskbigS, [[-4, 16], [0, 4]], ALU.is_ge, BIG, 0, 1)
        k.cp(maskbigPb, maskbigP, eng="pool")
        k.cp(maskbigSb, maskbigS, eng="pool")
        k.memset(mask01S, 1.0)
        k.asel(mask01S, mask01S, [[4, 16], [1, 4]], ALU.is_ge, 0.0, 0, -1)
        k.asel(mask01S, mask01S, [[-4, 16], [0, 4]], ALU.is_ge, 0.0, 0, 1)
        k.memset(ind, 1.0)
        k.asel(ind, ind, [[-4, NS]], ALU.is_ge, 0.0, 0, 1)
        k.asel(ind, ind, [[4, NS]], ALU.is_ge, 0.0, 3, -1)
        for h in range(4):
            k.memset(Cst[h], 0.0)
            k.memset(Sst[h], 0.0)
        k.memset(nTp, 0.0)
        k.memset(mcar, 0.0)
        k.memset(carB, 0.0)
        k.memset(vv(FNx, FNx.ap[:, 0:1]), 0.0)
        for p in range(5):
            k.memset(vv(mv_tm[p], mv_tm[p].ap[:, :, 256:257]), 1.0)
        for n, c in gcol.items():
            k.dma("sp", CT.ap[:, c:c + 8], W[n].rearrange("(c p) -> p c", p=128), writes=CT.bufs,
                  allow_slow_non_contiguous=True)
        k.dma("sp", CT.ap[:, 64:68], W["gla_ba"].rearrange("(c p) -> p c", p=128), writes=CT.bufs,
              allow_slow_non_contiguous=True)
        k.dma("sp", bi.ap, W["b_if"][0:4].rearrange("(p o) -> p o", o=1), writes=bi.bufs)
        k.dma("sp", negbf.ap, W["b_if"][4:8].rearrange("(p o) -> p o", o=1), writes=negbf.bufs)
        k.dma("pool", wa2_b.ap, W["gla_wa2"], writes=wa2_b.bufs)
        k.dma("sp", n_tm.ap, ns_in, writes=n_tm.bufs)
        k.dma("sp", m0T.ap, ms_in.rearrange("i h -> h i"), writes=m0T.bufs, allow_slow_non_contiguous=True)
        k.ts(negbf, negbf, -1.0, ALU.mult)
        k.ts(negba, vv(CT, CT.ap[:, 64:68]), -1.0, ALU.mult)

        def gain(name, c):
            col = gcol[name] + c
            return vv(CT, CT.ap[:, col:col + 1])

        pt = ps()
        k.tr(vv(pt, pt.ap[:, 0:64]), n_tm, vv(ident_f, ident_f.ap[0:64, 0:64]))
        k.cp(vv(nTs, nTs.ap.rearrange("p (h i) -> p h i", i=NS)),
             vv(pt, pt.ap[:, 0:64].rearrange("p (i h) -> p h i", h=4)))

        def slab(wap, KC, c0, w):
            si_ = k.slot_i % NSLOT
            t, b = slots[si_]
            k.slot_i += 1
            view = t[:, 0:KC * w].rearrange("p (c n) -> p c n", n=w)
            src = wap.rearrange("(c p) n -> p c n", p=128)[:, :, c0:c0 + w]
            k.dma("pool", view, src, writes=[b] + slot_subs[si_])
            return V(view, [b])

        slot_subs = [[] for _ in range(NSLOT)]

        def borrow_slot(si_, shape3):
            t, b = slots[si_]
            if not slot_subs[si_]:
                slot_subs[si_] = [k.buf("sub%d_%d" % (si_, q)) for q in range(8)]
            f = t[:, 0:4096].bitcast(F32)
            k.memset(V(f[:, 0:1], [b] + slot_subs[si_]), 0.0, eng="dve")
            out = []
            for q in range(8):
                v = f[:, q * 256:(q + 1) * 256]
                if shape3:
                    v = v.rearrange("p (a d) -> p a d", d=128)
                out.append(V(v, [slot_subs[si_][q]]))
            return out

        def fm_mm(sl, KC, s0, M, ins, c0, n):
            p = ps()
            pv = vv(p, p.ap[0:M, 0:n])
            for c in range(KC):
                k.mm(pv, vv(sl, sl.ap[:, c, s0:s0 + M]), vv(ins[c], ins[c].ap[:, c0:c0 + n]), start=(c == 0),
                     stop=(c == KC - 1))
            return pv

        def tm_mm(sl, KC, s0, w, ins, t0, M):
            p = ps()
            pv = vv(p, p.ap[0:M, 0:w])
            for c in range(KC):
                k.mm(pv, vv(ins[c], ins[c].ap[:, t0:t0 + M]), vv(sl, sl.ap[:, c, s0:s0 + w]), start=(c == 0),
                     stop=(c == KC - 1))
            return pv

        def rmsnorm(src, gname, dst, cgs, nch=8, div=float(D), sq=None):
            if sq is None:
                sq = Z[24:24 + nch]
            for (c0, n) in cgs:
                for c in range(nch):
                    k.act(vv(sq[c], sq[c].ap[:, c0:c0 + n]), vv(src[c], src[c].ap[:, c0:c0 + n]), AF.Square)
                p = ps()
                pv = vv(p, p.ap[:, 0:n])
                for c in range(nch):
                    k.mm(pv, ones_b, vv(sq[c], sq[c].ap[:, c0:c0 + n]), start=(c == 0), stop=(c == nch - 1))
                lv = vv(lnv, lnv.ap[:, 0:n])
                rv = vv(rstd, rstd.ap[:, 0:n])
                k.act(lv, pv, AF.Ln, bias=EPSC, scale=1.0 / div)
                k.act(rv, lv, AF.Exp, scale=-0.5)
                for c in range(nch):
                    k.stt(vv(dst[c], dst[c].ap[:, c0:c0 + n]), vv(src[c], src[c].ap[:, c0:c0 + n]), gain(gname, c), rv,
                          ALU.mult, ALU.mult)

        EPSC = k.sbv("epsc", [128, 1], F32)
        k.memset(EPSC, EPS)

        def ffn(pref, cgs):
            rmsnorm(xT, pref + "_norm", hT, cgs)
            wg, wu, wd = W[pref + "_wg"], W[pref + "_wu"], W[pref + "_wd"]
            for s0 in range(0, DFF, 512):
                w = min(512, DFF - s0)
                sg = slab(wg, 8, s0, w)
                su = slab(wu, 8, s0, w)
                for f in range(w // 128):
                    fc = s0 // 128 + f
                    for (c0, n) in cgs:
                        pg = fm_mm(sg, 8, f * 128, 128, hT, c0, n)
                        pu = fm_mm(su, 8, f * 128, 128, hT, c0, n)
                        st = sgt[fc % 2]
                        sv = vv(st, st.ap[:, 0:n])
                        k.act(sv, pg, AF.Silu)
                        k.tt(vv(Z[fc], Z[fc].ap[:, c0:c0 + n]), sv, pu, ALU.mult)
            for dc in range(8):
                sd = slab(wd, 22, dc * 128, 128)
                for (c0, n) in cgs:
                    pv = fm_mm(sd, 22, 0, 128, Z, c0, n)
                    xv = vv(xT[dc], xT[dc].ap[:, c0:c0 + n])
                    k.stt(xv, pv, 0.5, xv, ALU.mult, ALU.add)

        def fetch_x(j, with_s):
            for tc in range(4):
                r0 = (j * 4 + tc) * 128
                k.dma("sp", stg[tc].ap, x_p[r0:r0 + 128, :], writes=stg[tc].bufs)
            if with_s:
                k.dma("sp", stg[4].ap[0:64, :], x_s, writes=stg[4].bufs)

        def load_x(j, with_s, fetched):
            if not fetched:
                fetch_x(j, with_s)
            for c in range(8):
                p = ps()
                for tc in range(4):
                    k.tr(vv(p, p.ap[:, tc * 128:(tc + 1) * 128]), vv(stg[tc], stg[tc].ap[:, c * 128:(c + 1) * 128]),
                         ident_f)
                k.cpx(vv(xT[c], xT[c].ap[:, 0:512]), p)
            if with_s:
                for c in range(8):
                    p = ps()
                    k.tr(vv(p, p.ap[:, 0:64]), vv(stg[4], stg[4].ap[0:64, c * 128:(c + 1) * 128]),
                         vv(ident_f, ident_f.ap[0:64, 0:64]))
                    k.cpx(vv(xT[c], xT[c].ap[:, 512:576]), vv(p, p.ap[:, 0:64]))

        def final_out(j, cgs, with_s):
            yf_t = Z_t[:, 0:16, :].rearrange("p c n -> p (c n)").bitcast(F32).rearrange("p (c n) -> p c n", n=NCOL)
            yf = [V(yf_t[:, c, :], [Zb[2 * c], Zb[2 * c + 1]]) for c in range(8)]
            rmsnorm(xT, "final_norm", yf, cgs)
            nchunks = 5 if with_s else 4
            ystg = [V(Z_t[:, c0:c0 + 4, :].rearrange("p c n -> p (c n)")[:, 0:2048].bitcast(F32), Zb[c0:c0 + 4])
                    for c0 in (16, 20)]
            for tc in range(nchunks):
                M = 128 if tc < 4 else 64
                t0 = tc * 128
                yst = ystg[tc % 2]
                for half in range(2):
                    p = ps()
                    for q in range(4):
                        c = half * 4 + q
                        k.tr(vv(p, p.ap[0:M, q * 128:(q + 1) * 128]), vv(yf[c], yf[c].ap[:, t0:t0 + M]), ident_f)
                    k.cpx(vv(yst, yst.ap[0:M, half * 512:(half + 1) * 512]), vv(p, p.ap[0:M, :]))
                if tc < 4:
                    r0 = (j * 4 + tc) * 128
                    k.dma("sp", y_p[r0:r0 + 128, :], yst.ap, reads=yst.bufs, store=True)
                else:
                    k.dma("sp", y_s, yst.ap[0:64, :], reads=yst.bufs, store=True)

        def mem_kv():
            for mc in range(2):
                k.dma("sp", stg[mc].ap, mem[mc * 128:(mc + 1) * 128, :], writes=stg[mc].bufs)
            mT = [V(Z_t[:, c, 0:512].bitcast(F32), [Zb[c]]) for c in range(8)]
            for c in range(8):
                p = ps()
                for mc in range(2):
                    k.tr(vv(p, p.ap[:, mc * 128:(mc + 1) * 128]), vv(stg[mc], stg[mc].ap[:, c * 128:(c + 1) * 128]),
                         ident_f)
                k.cpx(mT[c], vv(p, p.ap[:, 0:256]))
            KSUB = 99
            if KSUB < 1:
                return
            mn = [vv(hT[c], hT[c].ap[:, 0:256]) for c in range(8)]
            rmsnorm(mT, "mem_norm", mn, [(0, 256)], sq=Z[24:32])
            if KSUB < 2:
                return
            for half in range(2):
                sk = slab(W["ca_wk"], 8, half * 512, 512)
                if KSUB < 3:
                    continue
                KMM = 4
                KCP = 1
                for f in range(min(4, KMM)):
                    pv = fm_mm(sk, 8, f * 128, 128, mn, 0, 256)
                    if KCP == 1:
                        k.cpx(vv(mkT_mem, mkT_mem.ap[:, half * 4 + f, :]), pv)
                    elif KCP == 2:
                        k.cp(vv(mkT_mem, mkT_mem.ap[:, half * 4 + f, :]), pv, eng="dve")
                    elif KCP == 3:
                        k.cp(vv(lnv, lnv.ap[:, 0:256]), pv, eng="act")
                if KSUB < 4:
                    continue
                for mc in range(2):
                    pv = tm_mm(sk, 8, 0, 512, mn, mc * 128, 128)
                    k.cpx(vv(stg[2 + mc], stg[2 + mc].ap[:, half * 512:(half + 1) * 512]), pv)
            if KSUB < 5:
                return
            for mc in range(2):
                k.dma("sp", mkp_o[mc * 128:(mc + 1) * 128, :], stg[2 + mc].ap, reads=stg[2 + mc].bufs, store=True)
            if KSUB < 6:
                return
            for half in range(2):
                sv = slab(W["ca_wv"], 8, half * 512, 512)
                for mc in range(2):
                    pv = tm_mm(sv, 8, 0, 512, mn, mc * 128, 128)
                    k.cp(vv(stg[mc], stg[mc].ap[:, half * 512:(half + 1) * 512]), pv, eng="act")
                    k.cp(vv(mv_mem, mv_mem.ap[:, mc, half * 512:(half + 1) * 512]), pv, eng="dve")
            for mc in range(2):
                k.dma("sp", mvp_o[mc * 128:(mc + 1) * 128, :], stg[mc].ap, reads=stg[mc].bufs, store=True)

        def mix(j, cgs, with_s):
            rmsnorm(xT, "mix_norm", hT, cgs)
            win = W["w_in"]
            ntc = 5 if with_s else 4
            tcs = [(tc, tc * 128, 128 if tc < 4 else 64) for tc in range(ntc)]
            SC = 128.0 ** -0.5
            for p_ in range(ntc):
                k.memset(vv(mv_tm[p_], mv_tm[p_].ap[:, :, 256:257]), 1.0, eng="dve")
            ncol = NCOL if with_s else 512

            def proj_mq():
                sl = slab(win, 8, MQ, 512)
                for h in range(4):
                    for (c0, n) in cgs:
                        pv = fm_mm(sl, 8, h * 128, 128, hT, c0, n)
                        k.cpx(vv(Z[ZMQ + h], Z[ZMQ + h].ap[:, c0:c0 + n]), pv)

            def proj_mk():
                sl = slab(win, 8, MK, 512)
                for h in range(4):
                    for (c0, n) in cgs:
                        pv = fm_mm(sl, 8, h * 128, 128, hT, c0, n)
                        k.act(vv(Z[ZMK + h], Z[ZMK + h].ap[:, c0:c0 + n]), pv, AF.Identity, scale=SC)
                for (tc, t0, M) in tcs:
                    pv = tm_mm(sl, 8, 0, 512, hT, t0, M)
                    k.ts(vv(mk_tm[tc], mk_tm[tc].ap[0:M, :]), pv, SC, ALU.mult)

            def proj_mv(half):
                sl = slab(win, 8, MV + half * 512, 512)
                for (tc, t0, M) in tcs:
                    pv = tm_mm(sl, 8, 0, 512, hT, t0, M)
                    k.cpx(vv(mv_tm[tc], mv_tm[tc].ap[0:M, half * 2:half * 2 + 2, 0:256]),
                          vv(pv, pv.ap.rearrange("p (h v) -> p h v", v=256)))

            def proj_mo(half):
                sl = slab(win, 8, MO + half * 512, 512)
                for f in range(4):
                    zc = Z[ZMO + half * 4 + f]
                    for (c0, n) in cgs:
                        pv = fm_mm(sl, 8, f * 128, 128, hT, c0, n)
                        k.act(vv(zc, zc.ap[:, c0:c0 + n]), pv, AF.Sigmoid)
                    k.ts(vv(zc, zc.ap[:, 0:ncol]), vv(zc, zc.ap[:, 0:ncol]), gain("mlstm_norm", half * 4 + f), ALU.mult)

            def proj_A():
                sl = slab(win, 8, MI, 520)
                for (c0, n) in cgs:
                    pv = fm_mm(sl, 8, 0, 4, hT, c0, n)
                    k.ts(vv(IG, IG.ap[:, c0:c0 + n]), pv, bi, ALU.add)
                    pv = fm_mm(sl, 8, 4, 4, hT, c0, n)
                    k.act(vv(E1, E1.ap[:, c0:c0 + n]), pv, AF.Exp, bias=negbf, scale=-1.0)
                for h in range(4):
                    for (c0, n) in cgs:
                        pv = fm_mm(sl, 8, 8 + h * 128, 128, hT, c0, n)
                        k.cpx(vv(Z[ZGQ + h], Z[ZGQ + h].ap[:, c0:c0 + n]), pv)

            def proj_gk():
                sl = slab(win, 8, GK, 512)
                for h in range(4):
                    for (c0, n) in cgs:
                        pv = fm_mm(sl, 8, h * 128, 128, hT, c0, n)
                        k.cpx(vv(Z[ZGK + h], Z[ZGK + h].ap[:, c0:c0 + n]), pv)

            def proj_gv(half):
                sl = slab(win, 8, GV + half * 512, 512)
                for (tc, t0, M) in tcs:
                    pv = tm_mm(sl, 8, 0, 512, hT, t0, M)
                    k.cpx(vv(gv_tm[tc], gv_tm[tc].ap[0:M, half * 2:half * 2 + 2, :]),
                          vv(pv, pv.ap.rearrange("p (h v) -> p h v", v=256)))

            def proj_gr(half):
                wdt = 512 if half == 0 else 528
                sl = slab(win, 8, GR + half * 512, wdt)
                if half == 1:
                    for (c0, n) in cgs:
                        pv = fm_mm(sl, 8, 512, 16, hT, c0, n)
                        k.cp(vv(gaT, gaT.ap[:, c0:c0 + n]), pv, eng="dve")
                for f in range(4):
                    zc = Z[ZGR + half * 4 + f]
                    for (c0, n) in cgs:
                        pv = fm_mm(sl, 8, f * 128, 128, hT, c0, n)
                        k.act(vv(zc, zc.ap[:, c0:c0 + n]), pv, AF.Silu)
                    k.ts(vv(zc, zc.ap[:, 0:ncol]), vv(zc, zc.ap[:, 0:ncol]), gain("gla_norm", half * 4 + f), ALU.mult)

            def prep_mlstm():
                k.act(vv(SPm, SPm.ap[:, 0:ncol]), vv(E1, E1.ap[:, 0:ncol]), AF.Ln, bias=ONEC4, scale=1.0)
                k.scan(vv(FNx, FNx.ap[:, 1:1 + ncol]), vv(ONES4, ONES4.ap[:, 0:1].to_broadcast([4, ncol])),
                       vv(SPm, SPm.ap[:, 0:ncol]), vv(FNx, FNx.ap[:, 0:1]), ALU.mult, ALU.add)
                k.tt(vv(GGt, GGt.ap[:, 0:ncol]), vv(IG, IG.ap[:, 0:ncol]), vv(FNx, FNx.ap[:, 1:1 + ncol]), ALU.add)

            def prep_gla(h):
                    for (c0, n) in cgs:
                        p = ps()
                        pv = vv(p, p.ap[:, 0:n])
                        k.mm(pv, vv(wa2_b, wa2_b.ap[:, h * 128:(h + 1) * 128]), vv(gaT, gaT.ap[:, c0:c0 + n]))
                        k.act(vv(Ge1, Ge1.ap[:, c0:c0 + n]), pv, AF.Exp, bias=vv(negba, negba.ap[:, h:h + 1]), scale=-1.0)
                    k.act(vv(Ge1, Ge1.ap[:, 0:ncol]), vv(Ge1, Ge1.ap[:, 0:ncol]), AF.Ln, bias=ONEC, scale=1.0)
                    k.ts(vv(Ge1, Ge1.ap[:, 0:ncol]), vv(Ge1, Ge1.ap[:, 0:ncol]), 1.0 / 16.0, ALU.mult)
                    k.cp(vv(BNx, BNx.ap[:, 0:1]), vv(carB, carB.ap[:, h:h + 1]), eng="dve")
                    k.scan(vv(BNx, BNx.ap[:, 1:1 + ncol]), vv(ones_c, ones_c.ap[:, 0:1].to_broadcast([128, ncol])),
                           vv(Ge1, Ge1.ap[:, 0:ncol]), vv(BNx, BNx.ap[:, 0:1]), ALU.mult, ALU.add)
                    k.cp(vv(carB, carB.ap[:, h:h + 1]), vv(BNx, BNx.ap[:, 512:513]), eng="dve")
                    k.ts(vv(NB5, NB5.ap[:, 0:5]), vv(BNx, BNx.ap[:, 0:513:128]), -1.0, ALU.mult)
                    gq, gk, kd = Z[ZGQ + h], Z[ZGK + h], Z[ZKD + h]
                    SCQ = 128.0 ** -0.5
                    for c in range(4):
                        a, b = c * 128, (c + 1) * 128
                        prevP = vv(BNx, BNx.ap[:, a:a + 1])
                        prevN = vv(NB5, NB5.ap[:, c:c + 1])
                        endN = vv(NB5, NB5.ap[:, c + 1:c + 2])
                        cur = vv(BNx, BNx.ap[:, a + 1:b + 1])
                        e0, e1, e2 = Eq3
                        k.act(e0, cur, AF.Exp, bias=endN, scale=1.0)
                        k.act(e1, cur, AF.Exp, bias=prevN, scale=1.0)
                        k.act(e2, cur, AF.Exp, bias=prevP, scale=-1.0)
                        k.tt(vv(kd, kd.ap[:, a:b]), vv(gk, gk.ap[:, a:b]), e0, ALU.mult)
                        k.tt(vv(gk, gk.ap[:, a:b]), vv(gk, gk.ap[:, a:b]), e1, ALU.mult)
                        k.stt(vv(gq, gq.ap[:, a:b]), vv(gq, gq.ap[:, a:b]), SCQ, e2, ALU.mult, ALU.mult)
                        k.act(vv(DEC, DEC.ap[:, h, c:c + 1]), vv(BNx, BNx.ap[:, b:b + 1]), AF.Exp, bias=prevP, scale=-1.0)
                    if with_s:
                        a, b = 512, 576
                        cur3 = BNx.ap[:, a + 1:b + 1].rearrange("p (i t) -> p i t", t=4)
                        prev3 = BNx.ap[:, a:b].rearrange("p (i t) -> p i t", t=4)[:, :, 0:1].to_broadcast([128, NS, 4])
                        end3 = BNx.ap[:, a + 1:b + 1].rearrange("p (i t) -> p i t", t=4)[:, :, 3:4].to_broadcast(
                            [128, NS, 4])
                        ea = vv(Earg, Earg.ap.rearrange("p (i t) -> p i t", t=4))
                        e = vv(Eq3[0], Eq3[0].ap[:, 0:64])
                        k.tt(ea, vv(BNx, cur3), vv(BNx, end3), ALU.subtract)
                        k.act(e, Earg, AF.Exp)
                        k.tt(vv(kd, kd.ap[:, a:b]), vv(gk, gk.ap[:, a:b]), e, ALU.mult)
                        k.tt(ea, vv(BNx, cur3), vv(BNx, prev3), ALU.subtract)
                        k.act(e, Earg, AF.Exp)
                        k.tt(vv(gk, gk.ap[:, a:b]), vv(gk, gk.ap[:, a:b]), e, ALU.mult)
                        k.act(e, Earg, AF.Exp, scale=-1.0)
                        k.stt(vv(gq, gq.ap[:, a:b]), vv(gq, gq.ap[:, a:b]), SCQ, e, ALU.mult, ALU.mult)
                        k.act(vv(DEC, DEC.ap[:, h, 4:4 + NS]), vv(Earg, Earg.ap.rearrange('p (i t) -> p i t', t=4)[:, :, 3]), AF.Exp, scale=-1.0)

            MARKS.append(("mix_proj", j, k.eng["pe"].cnt))
            proj_A()
            proj_gr(1)
            proj_gk()
            prep_mlstm()
            prep_gla(0)
            proj_gr(0)
            proj_mq()
            prep_gla(1)
            proj_mk()
            proj_mv(0)
            prep_gla(2)
            proj_mv(1)
            proj_mo(0)
            prep_gla(3)
            proj_mo(1)
            proj_gv(0)
            proj_gv(1)
            MARKS.append(("mix_prep", j, k.eng["pe"].cnt))

            groups = [dict(g0=c * 128, NP=128, tc=c, owners=[dict(c0=0, n=128, kind="p", dcol=c)],
                           mbig=maskbigPb, m01=mask01P) for c in range(4)]
            if with_s:
                groups.append(dict(g0=512, NP=64, tc=4, owners=[dict(c0=4 * i, n=4, kind="s", i=i, dcol=4 + i)
                                                                  for i in range(NS)],
                                   mbig=maskbigSb, m01=mask01S))
            def prologue(gi_, g, ctx):
                g0, NP, tc = g["g0"], g["NP"], g["tc"]
                gc = slice(g0, g0 + NP)
                GL, MG, AE = GL2[gi_ % 3], MG2[gi_ % 3], AE2[gi_ % 3]
                lc = slice(0, NP)
                if NP == 128:
                    fprev = vv(FNx, FNx.ap[:, g0:g0 + 1])
                    k.ts(vv(GL, GL.ap[:, lc]), vv(GGt, GGt.ap[:, gc]), fprev, ALU.subtract)
                    k.scan(vv(MG, MG.ap[:, lc]), vv(GL, GL.ap[:, lc]), vv(GL, GL.ap[:, lc]), mcar, ALU.max, ALU.max)
                    k.act(vv(AE, AE.ap[:, 0, lc]), vv(MG, MG.ap[:, lc]), AF.Exp, bias=mcar, scale=-1.0)
                    k.stt(vv(TMPm, TMPm.ap[:, lc]), vv(FNx, FNx.ap[:, g0 + 1:g0 + 1 + NP]), fprev,
                          vv(MG, MG.ap[:, lc]), ALU.subtract, ALU.subtract)
                    k.act(vv(AE, AE.ap[:, 1, lc]), vv(TMPm, TMPm.ap[:, lc]), AF.Exp)
                    k.ts(mcar, vv(TMPm, TMPm.ap[:, NP - 1:NP]), -1.0, ALU.mult)
                else:
                    def v3(t, off=0):
                        if t is FNx or t is GGt:
                            return t.ap[:, g0 + off:g0 + off + NP].rearrange("p (i t) -> p i t", t=4)
                        return t.ap[:, 0:NP].rearrange("p (i t) -> p i t", t=4)
                    fprev3 = FNx.ap[:, g0:g0 + NP].rearrange("p (i t) -> p i t", t=4)[:, :, 0:1].to_broadcast(
                        [4, NS, 4])
                    k.tt(vv(GL, v3(GL)), vv(GGt, v3(GGt)), vv(FNx, fprev3), ALU.subtract)
                    k.tt(vv(MG, v3(MG)[:, :, 0]), vv(GL, v3(GL)[:, :, 0]), m0T, ALU.max)
                    for t in range(1, 4):
                        k.tt(vv(MG, v3(MG)[:, :, t]), vv(GL, v3(GL)[:, :, t]), vv(MG, v3(MG)[:, :, t - 1]), ALU.max)
                    m03 = m0T.ap.rearrange("p (i o) -> p i o", o=1).to_broadcast([4, NS, 4])
                    k.tt(vv(TMPm, v3(TMPm)), vv(m0T, m03), vv(MG, v3(MG)), ALU.subtract)
                    k.act(vv(AE, AE.ap[:, 0, lc]), vv(TMPm, TMPm.ap[:, lc]), AF.Exp)
                    k.tt(vv(TMPm, v3(TMPm)), vv(FNx, v3(FNx, 1)), vv(FNx, fprev3), ALU.subtract)
                    k.tt(vv(TMPm, TMPm.ap[:, lc]), vv(TMPm, TMPm.ap[:, lc]), vv(MG, MG.ap[:, lc]), ALU.subtract)
                    k.act(vv(AE, AE.ap[:, 1, lc]), vv(TMPm, TMPm.ap[:, lc]), AF.Exp)
                    k.ts(msT, vv(TMPm, v3(TMPm)[:, :, 3]), -1.0, ALU.mult)
                gt = P_GT.get()
                p = ps_try()
                assert p is not None
                k.tr(vv(p, p.ap[0:NP, 0:4]), vv(GL, GL.ap[:, lc]), vv(ident_f, ident_f.ap[0:4, 0:4]))
                k.cp(vv(gt, gt.ap[0:NP, :]), vv(p, p.ap[0:NP, 0:4]), eng="act")
                psfree(p)
                identNP = vv(ident_f, ident_f.ap[0:NP, 0:NP])
                onesNP = vv(ones_b, ones_b.ap[0:NP, :])
                first = (j == 0 and g0 == 0)
                mgh, aeh = MGH[gi_ % 3], AEH[gi_ % 3]
                k.cp(vv(mgh, mgh.ap[:, 0, lc]), vv(MG, MG.ap[:, lc]), eng="dve")
                k.tt(vv(mgh, mgh.ap[:, 1, lc]), vv(MG, MG.ap[:, lc]), vv(mgh, mgh.ap[:, 0, lc]), ALU.subtract)
                k.cp(vv(aeh, aeh.ap[:, 0, :, lc]), vv(AE, AE.ap[:, :, lc]), eng="dve")
                k.tt(vv(aeh, aeh.ap[:, 1, :, lc]), vv(AE, AE.ap[:, :, lc]), vv(aeh, aeh.ap[:, 0, :, lc]), ALU.subtract)
                ctx.update(dict(g=g, g0=g0, NP=NP, tc=tc, gc=gc, lc=lc, gt=gt, identNP=identNP, onesNP=onesNP,
                                first=first, GL=GL, MG=MG, AE=AE, MGH=mgh, AEH=aeh, ready=True))

            def pro_gen(gi_, g, ctx):
                while len(busy) >= NPS:
                    yield
                prologue(gi_, g, ctx)
                return
                yield

            def wait_ready(ctx, gen_fn, h):
                while not ctx.get("ready"):
                    yield
                yield from gen_fn(ctx, h)

            gens = []
            sample_ctx = None
            for gi_, g in enumerate(groups):
                ctx = {}
                if g["NP"] == 128:
                    gens.append(pro_gen(gi_, g, ctx))
                    for h in range(4):
                        gens.append(wait_ready(ctx, gh_prompt_gen, h))
                else:
                    sample_ctx = (gi_, g, ctx)
            run_interleaved(gens, WH + 1)
            if sample_ctx is not None:
                gi_, g, ctx = sample_ctx
                prologue(gi_, g, ctx)
                CSB[:] = borrow_slot((k.slot_i + 2) % NSLOT, True)
                SSB[:] = borrow_slot((k.slot_i + 3) % NSLOT, False)
                for h in range(4):
                    run_interleaved([gh_sample_gen(ctx, h)], 1)
            MARKS.append(("merge", j, k.eng["pe"].cnt))
            HMc = [Z[ZMQ + hh] if vc == 0 else Z[ZMK + hh] for hh in range(4) for vc in range(2)]
            HGc = [Z[ZGQ + hh] if vc == 0 else Z[ZGK + hh] for hh in range(4) for vc in range(2)]
            yacc_t = Z_t[:, ZMO:ZMO + 8, :].rearrange("p c n -> p (c n)").bitcast(F32).rearrange("p (c n) -> p c n", n=NCOL)
            yacc = [V(yacc_t[:, q, :], [Zb[ZMO + 2 * q], Zb[ZMO + 2 * q + 1]]) for q in range(4)]
            yT = [Z[ZGR + q] for q in range(8)]
            for half in range(2):
                sgm = slab(win, 8, GM + half * 512, 512)
                sbm = slab(W["w_br_m"], 8, half * 512, 512)
                for q in range(4):
                    for (c0, n) in cgs:
                        pg = fm_mm(sgm, 8, q * 128, 128, hT, c0, n)
                        pm = fm_mm(sbm, 8, q * 128, 128, HMc, c0, n)
                        t = mrg[0]
                        k.act(vv(t, t.ap[:, 0:n]), pg, AF.Sigmoid)
                        k.tt(vv(yacc[q], yacc[q].ap[:, c0:c0 + n]), vv(t, t.ap[:, 0:n]), pm, ALU.mult)
                sgg = slab(win, 8, GG + half * 512, 512)
                sbg = slab(W["w_br_g"], 8, half * 512, 512)
                for q in range(4):
                    for (c0, n) in cgs:
                        pg = fm_mm(sgg, 8, q * 128, 128, hT, c0, n)
                        pm = fm_mm(sbg, 8, q * 128, 128, HGc, c0, n)
                        t = mrg[1]
                        k.act(vv(t, t.ap[:, 0:n]), pg, AF.Sigmoid)
                        k.tt(vv(t, t.ap[:, 0:n]), vv(t, t.ap[:, 0:n]), pm, ALU.mult)
                        k.tt(vv(yT[half * 4 + q], yT[half * 4 + q].ap[:, c0:c0 + n]), vv(t, t.ap[:, 0:n]),
                             vv(yacc[q], yacc[q].ap[:, c0:c0 + n]), ALU.add)
            for half in range(2):
                so = slab(W["w_out"], 8, half * 512, 512)
                for q in range(4):
                    dc = half * 4 + q
                    for (c0, n) in cgs:
                        pv = fm_mm(so, 8, q * 128, 128, yT, c0, n)
                        xv = vv(xT[dc], xT[dc].ap[:, c0:c0 + n])
                        k.tt(xv, pv, xv, ALU.add)

        WH = 4
        PEN = "pool"
        busy = set()

        def ps_try():
            for _ in range(NPS):
                idx = k.ps_i % NPS
                k.ps_i += 1
                if idx not in busy and idx not in pinned:
                    busy.add(idx)
                    return PS[idx]
            return None

        def psw():
            while True:
                p = ps_try()
                if p is not None:
                    return p
                yield

        def psfree(p):
            busy.discard(PS.index(p))

        def run_interleaved(gens, width):
            it = iter(gens)
            active = []
            rounds = 0
            while True:
                while len(active) < width:
                    try:
                        active.append(next(it))
                    except StopIteration:
                        break
                if not active:
                    break
                for gn in list(active):
                    try:
                        next(gn)
                    except StopIteration:
                        active.remove(gn)
                rounds += 1
                assert rounds < 100000, "interleave livelock (PSUM banks exhausted?)"

        def inter_gen(gens, width):
            it = iter(gens)
            active = []
            while True:
                while len(active) < width:
                    try:
                        active.append(next(it))
                    except StopIteration:
                        break
                if not active:
                    return
                for gn in list(active):
                    try:
                        next(gn)
                    except StopIteration:
                        active.remove(gn)
                yield

        def hn_gen(HRv, NP, gname, h, gate, dst, g0):
            SQ = P_SQ.get()
            SQv = vv(SQ, SQ.ap[:, :, 0:NP])
            k.act(SQv, HRv, AF.Square)
            p = yield from psw()
            pv = vv(p, p.ap[:, 0:NP])
            for vc in range(2):
                k.mm(pv, ones_b, vv(SQ, SQ.ap[:, vc, 0:NP]), start=(vc == 0), stop=(vc == 1))
            yield
            LR = P_LR.get()
            lv = vv(LR, LR.ap[:, 0, 0:NP])
            rv = vv(LR, LR.ap[:, 1, 0:NP])
            k.act(lv, pv, AF.Ln, bias=EPSC, scale=1.0 / 256.0)
            psfree(p)
            k.act(rv, lv, AF.Exp, scale=-0.5)
            T1 = P_T1.get()
            T1v = vv(T1, T1.ap[:, :, 0:NP])
            k.tt(T1v, HRv, vv(LR, LR.ap[:, 1:2, 0:NP].to_broadcast([128, 2, NP])), ALU.mult)
            dsti, gatei = dst, gate
            dst3 = V(Z_t[:, dsti:dsti + 5:4, g0:g0 + NP], [Zb[dsti], Zb[dsti + 4]])
            gate3 = V(Z_t[:, gatei:gatei + 2, g0:g0 + NP], [Zb[gatei], Zb[gatei + 1]])
            k.tt(dst3, T1v, gate3, ALU.mult)

        def gh_common_front(ctx, h):
            g, g0, NP, tc, gc, lc, gt = ctx["g"], ctx["g0"], ctx["NP"], ctx["tc"], ctx["gc"], ctx["lc"], ctx["gt"]
            GL, MG, AE = ctx["GL"], ctx["MG"], ctx["AE"]
            mq, mkk = Z[ZMQ + h], Z[ZMK + h]
            pST = yield from psw()
            STv = vv(pST, pST.ap[0:NP, 0:NP])
            k.mm(STv, vv(mkk, mkk.ap[:, gc]), vv(mq, mq.ap[:, gc]))
            pMb = yield from psw()
            Mbv = vv(pMb, pMb.ap[0:NP, 0:NP])
            mgh, aeh = ctx["MGH"], ctx["AEH"]
            k.mm(Mbv, vv(sel, sel.ap[:, h, 0:NP]), vv(mgh, mgh.ap[:, 0, lc]), start=True, stop=False)
            k.mm(Mbv, vv(sel, sel.ap[:, h, 0:NP]), vv(mgh, mgh.ap[:, 1, lc]), start=False, stop=False)
            k.mm(Mbv, vv(ident_b, ident_b.ap[0:NP, 0:NP]), g["mbig"], start=False, stop=True)
            pAB = yield from psw()
            ABp = vv(pAB, pAB.ap[:, 0:2 * NP].rearrange("p (a n) -> p a n", n=NP))
            k.mm(ABp, vv(sel, sel.ap[:, h, :]), vv(aeh, aeh.ap[:, 0, :, lc]), start=True, stop=False)
            k.mm(ABp, vv(sel, sel.ap[:, h, :]), vv(aeh, aeh.ap[:, 1, :, lc]), start=False, stop=True)
            yield
            WT = P_WT.get()
            WTv = vv(WT, WT.ap[0:NP, 0:NP])
            k.act(WTv, Mbv, AF.Exp, bias=vv(gt, gt.ap[0:NP, h:h + 1]), scale=-1.0)
            psfree(pMb)
            AB = P_AB.get()
            k.cp(vv(AB, AB.ap[:, :, 0:NP]), ABp, eng="act")
            psfree(pAB)
            PT = P_PT.get()
            PTv = vv(PT, PT.ap[0:NP, 0:NP])
            k.tt(PTv, WTv, STv, ALU.mult)
            psfree(pST)
            QA = P_QA.get()
            k.tt(vv(QA, QA.ap[:, 0:NP]), vv(mq, mq.ap[:, gc]), vv(AB, AB.ap[:, 0, 0:NP]), ALU.mult, eng=PEN)
            return dict(WT=WT, AB=AB, PT=PT, PTv=PTv, QA=QA)

        def gla_front(ctx, h):
            g, g0, NP, gc = ctx["g"], ctx["g0"], ctx["NP"], ctx["gc"]
            gq, gk, kd = Z[ZGQ + h], Z[ZGK + h], Z[ZKD + h]
            pA = yield from psw()
            Av = vv(pA, pA.ap[0:NP, 0:NP])
            k.mm(Av, vv(gk, gk.ap[:, gc]), vv(gq, gq.ap[:, gc]))
            pK32 = yield from psw()
            pK = V(pK32.ap.bitcast(BF16), pK32.bufs)
            k.tr(vv(pK, pK.ap[0:NP, 0:128]), vv(kd, kd.ap[:, gc]), ident_b)
            yield
            PT2 = P_PT.get()
            PT2v = vv(PT2, PT2.ap[0:NP, 0:NP])
            k.tt(PT2v, Av, g["m01"], ALU.mult)
            psfree(pA)
            KD = P_KD.get()
            k.cp(vv(KD, KD.ap[0:NP, :]), vv(pK, pK.ap[0:NP, 0:128]), eng="act")
            psfree(pK32)
            return dict(PT2v=PT2v, KD=KD)

        def gh_prompt_gen(ctx, h):
            g, g0, NP, tc, gc, lc = ctx["g"], ctx["g0"], ctx["NP"], ctx["tc"], ctx["gc"], ctx["lc"]
            first = ctx["first"]
            mq = Z[ZMQ + h]
            o = g["owners"][0]
            last = NP - 1
            fr = yield from gh_common_front(ctx, h)
            WT, AB, PTv, QA = fr["WT"], fr["AB"], fr["PTv"], fr["QA"]
            QAv = vv(QA, QA.ap[:, 0:NP])
            C = Cst[h]
            nT = vv(nTp, nTp.ap[:, h:h + 1])
            QN = P_QN.get()
            if not first:
                pC = yield from psw()
                for vc in range(2):
                    k.tr(vv(pC, pC.ap[:, vc * 128:(vc + 1) * 128]), vv(C, C.ap[:, vc, :]), ident_f)
                k.ts(vv(QN, QN.ap[:, 0:NP]), QAv, nT, ALU.mult, eng=PEN)
                yield
                ctb = P_CTB.get()
                k.cp(ctb, vv(pC, pC.ap[:, 0:256]), eng="act")
                psfree(pC)
            KW = P_KW.get()
            KWv = vv(KW, KW.ap[0:NP, :])
            k.ts(KWv, vv(mk_tm[tc], mk_tm[tc].ap[0:NP, h * 128:(h + 1) * 128]), vv(WT, WT.ap[0:NP, last:last + 1]),
                 ALU.mult, eng=PEN)
            pN = yield from psw()
            for vc in range(2):
                nv = vv(pN, pN.ap[:, vc * NP:(vc + 1) * NP])
                k.mm(nv, vv(mv_tm[tc], mv_tm[tc].ap[0:NP, h, vc * 128:(vc + 1) * 128]), PTv, start=True, stop=first)
                if not first:
                    k.mm(nv, vv(ctb, ctb.ap[:, vc * 128:(vc + 1) * 128]), QAv, start=False, stop=True)
            Dv = vv(pN, pN.ap[:, 2 * NP:3 * NP])
            k.mm(Dv, ctx["onesNP"], PTv, start=True, stop=first)
            if not first:
                k.mm(Dv, ones_b, vv(QN, QN.ap[:, 0:NP]), start=False, stop=True)
            pU = yield from psw()
            for vc in range(2):
                k.mm(vv(pU, pU.ap[:, vc * 128:(vc + 1) * 128]),
                     vv(mv_tm[tc], mv_tm[tc].ap[0:NP, h, vc * 128:(vc + 1) * 128]), KWv)
            k.mm(vv(pU, pU.ap[:, 256:257]), KWv, vv(ones_b, ones_b.ap[0:NP, 0:1]))
            yield
            HD = P_HD.get()
            k.act(vv(HD, HD.ap[:, 0:NP]), Dv, AF.Abs)
            k.tt(vv(HD, HD.ap[:, 0:NP]), vv(HD, HD.ap[:, 0:NP]), vv(AB, AB.ap[:, 1, 0:NP]), ALU.max)
            RD = P_RD.get()
            RDv = vv(RD, RD.ap[:, 0:NP])
            k.act(vv(HD, HD.ap[:, 0:NP]), vv(HD, HD.ap[:, 0:NP]), AF.Ln)
            k.act(RDv, vv(HD, HD.ap[:, 0:NP]), AF.Exp, scale=-1.0)
            HR = P_HR.get()
            HRv = vv(HR, HR.ap[:, :, 0:NP])
            k.tt(HRv, vv(pN, pN.ap[:, 0:2 * NP].rearrange("p (a n) -> p a n", n=NP)),
                 vv(RD, RD.ap[:, 0:NP].rearrange("p (a n) -> p a n", a=1).to_broadcast([128, 2, NP])), ALU.mult)
            psfree(pN)
            aend = vv(AB, AB.ap[:, 0, last:last + 1])
            Cf = vv(C, C.ap.rearrange("p a d -> p (a d)"))
            k.stt(Cf, Cf, aend, vv(pU, pU.ap[:, 0:256]), ALU.mult, ALU.add)
            k.stt(nT, nT, aend, vv(pU, pU.ap[:, 256:257]), ALU.mult, ALU.add)
            psfree(pU)
            yield
            yield from hn_gen(HRv, NP, "mlstm_norm", h, ZMO + 2 * h, ZMQ + h, g0)
            gq = Z[ZGQ + h]
            gf = yield from gla_front(ctx, h)
            PT2v, KD = gf["PT2v"], gf["KD"]
            S = Sst[h]
            if not first:
                sbf = P_SBF.get()
                k.cp(sbf, S, eng="act")
            pO = yield from psw()
            for vc in range(2):
                ov = vv(pO, pO.ap[:, vc * NP:(vc + 1) * NP])
                k.mm(ov, vv(gv_tm[tc], gv_tm[tc].ap[0:NP, h, vc * 128:(vc + 1) * 128]), PT2v, start=True, stop=first)
                if not first:
                    k.mm(ov, vv(sbf, sbf.ap[:, vc * 128:(vc + 1) * 128]), vv(gq, gq.ap[:, gc]), start=False, stop=True)
            pU = yield from psw()
            k.mm(vv(pU, pU.ap[:, 0:256]), vv(KD, KD.ap[0:NP, :]), vv(gv_tm[tc], gv_tm[tc].ap[0:NP, h, :]))
            yield
            HR2 = P_HR.get()
            HR2v = vv(HR2, HR2.ap[:, :, 0:NP])
            k.cp(HR2v, vv(pO, pO.ap[:, 0:2 * NP].rearrange("p (a n) -> p a n", n=NP)), eng="act")
            psfree(pO)
            k.stt(S, S, vv(DEC, DEC.ap[:, h, o["dcol"]:o["dcol"] + 1]), vv(pU, pU.ap[:, 0:256]), ALU.mult, ALU.add)
            psfree(pU)
            yield
            yield from hn_gen(HR2v, NP, "gla_norm", h, ZGR + 2 * h, ZGQ + h, g0)

        sidx = [0]
        CSB = []
        SSB = []
        PFD = 4
        OWW = 4

        def gh_sample_gen(ctx, h):
            g, g0, NP, tc, gc, lc = ctx["g"], ctx["g0"], ctx["NP"], ctx["tc"], ctx["gc"], ctx["lc"]
            owners = g["owners"]
            nown = len(owners)
            fr = yield from gh_common_front(ctx, h)
            WT, AB, PTv, QA = fr["WT"], fr["AB"], fr["PTv"], fr["QA"]
            gf = yield from gla_front(ctx, h)
            QN = P_QN.get()
            pN = [(yield from psw()), (yield from psw())]
            pD = yield from psw()
            pDN = yield from psw()
            Dv = vv(pD, pD.ap[:, 0:NP])
            for vc in range(2):
                k.mm(vv(pN[vc], pN[vc].ap[:, 0:NP]), vv(mv_tm[tc], mv_tm[tc].ap[0:NP, h, vc * 128:(vc + 1) * 128]), PTv,
                     start=True, stop=False)
            k.mm(Dv, ctx["onesNP"], PTv, start=True, stop=False)

            def load_C(oj):
                Cj = CSB[oj % 8]
                k.dma("sp", Cj.ap, Cs_in[owners[oj]["i"], h].rearrange("(c p) d -> p c d", p=128), writes=Cj.bufs)

            def load_S(oj):
                Sj = SSB[oj % 8]
                k.dma("sp", Sj.ap, Ss_in[owners[oj]["i"], h], writes=Sj.bufs)

            for oj in range(PFD):
                load_C(oj)

            def own_m(oi, o):
                i = o["i"]
                oc = slice(o["c0"], o["c0"] + o["n"])
                last = o["c0"] + o["n"] - 1
                C = CSB[oi % 8]
                if oi + PFD < nown:
                    load_C(oi + PFD)
                nT = vv(nTs, nTs.ap[:, h * NS + i:h * NS + i + 1])
                pC = yield from psw()
                for vc in range(2):
                    k.tr(vv(pC, pC.ap[:, vc * 128:(vc + 1) * 128]), vv(C, C.ap[:, vc, :]), ident_f)
                k.ts(vv(QN, QN.ap[:, oc]), vv(QA, QA.ap[:, oc]), nT, ALU.mult, eng=PEN)
                KW = P_KW.get()
                KWv = vv(KW, KW.ap[0:NP, :])
                k.ts(KWv, vv(mk_tm[tc], mk_tm[tc].ap[0:NP, h * 128:(h + 1) * 128]), vv(WT, WT.ap[0:NP, last:last + 1]),
                     ALU.mult, eng=PEN)
                yield
                ctb = P_CTB.get()
                k.cp(ctb, vv(pC, pC.ap[:, 0:256]), eng="act")
                for vc in range(2):
                    k.mm(vv(pN[vc], pN[vc].ap[:, oc]), vv(ctb, ctb.ap[:, vc * 128:(vc + 1) * 128]), vv(QA, QA.ap[:, oc]),
                         start=False, stop=(oi == nown - 1))
                for vc in range(2):
                    k.mm(vv(pC, pC.ap[:, 256 + vc * 128:256 + (vc + 1) * 128]),
                         vv(mv_tm[tc], mv_tm[tc].ap[0:NP, h, vc * 128:(vc + 1) * 128]), KWv)
                k.mm(vv(pDN, pDN.ap[:, oi:oi + 1]), KWv, vv(ones_b, ones_b.ap[0:NP, 0:1]))
                yield
                aend = vv(AB, AB.ap[:, 0, last:last + 1])
                Cf = vv(C, C.ap.rearrange("p a d -> p (a d)"))
                k.stt(Cf, Cf, aend, vv(pC, pC.ap[:, 256:512]), ALU.mult, ALU.add)
                psfree(pC)
                k.dma(STQ, Cs_o[i, h].rearrange("(c p) d -> p c d", p=128), C.ap, reads=C.bufs, store=True)

            for oj in range(PFD - 1):
                load_S(oj)
            yield from inter_gen([own_m(oi, o) for oi, o in enumerate(owners)], OWW)
            k.mm(Dv, ones_b, vv(QN, QN.ap[:, 0:NP]), start=False, stop=True)
            nTh = vv(nTs, nTs.ap[:, h * NS:(h + 1) * NS])
            aend16 = vv(AB, AB.ap[:, 0, 0:NP].rearrange("p (i t) -> p i t", t=4)[:, :, 3])
            k.tt(nTh, nTh, aend16, ALU.mult)
            k.tt(nTh, nTh, vv(pDN, pDN.ap[:, 0:NS]), ALU.add)
            psfree(pDN)
            HD = P_HD.get()
            k.act(vv(HD, HD.ap[:, 0:NP]), Dv, AF.Abs)
            psfree(pD)
            k.tt(vv(HD, HD.ap[:, 0:NP]), vv(HD, HD.ap[:, 0:NP]), vv(AB, AB.ap[:, 1, 0:NP]), ALU.max)
            RD = P_RD.get()
            RDv = vv(RD, RD.ap[:, 0:NP])
            k.act(vv(HD, HD.ap[:, 0:NP]), vv(HD, HD.ap[:, 0:NP]), AF.Ln)
            k.act(RDv, vv(HD, HD.ap[:, 0:NP]), AF.Exp, scale=-1.0)
            HR = P_HR.get()
            HRv = vv(HR, HR.ap[:, :, 0:NP])
            for vc in range(2):
                k.tt(vv(HR, HR.ap[:, vc, 0:NP]), vv(pN[vc], pN[vc].ap[:, 0:NP]), RDv, ALU.mult)
                psfree(pN[vc])
            hn_m = hn_gen(HRv, NP, "mlstm_norm", h, ZMO + 2 * h, ZMQ + h, g0)
            gq = Z[ZGQ + h]
            PT2v, KD = gf["PT2v"], gf["KD"]
            pO = [(yield from psw()), (yield from psw())]
            for vc in range(2):
                k.mm(vv(pO[vc], pO[vc].ap[:, 0:NP]), vv(gv_tm[tc], gv_tm[tc].ap[0:NP, h, vc * 128:(vc + 1) * 128]),
                     PT2v, start=True, stop=False)

            def own_g(oi, o):
                i = o["i"]
                S = SSB[oi % 8]
                if oi + PFD - 1 < nown:
                    load_S(oi + PFD - 1)
                KW = P_KW.get()
                KDo = vv(KW, KW.ap[0:NP, :])
                k.ts(KDo, vv(KD, KD.ap[0:NP, :]), vv(ind, ind.ap[:, i:i + 1]), ALU.mult, eng=PEN)
                pU = yield from psw()
                k.mm(vv(pU, pU.ap[:, 0:256]), KDo, vv(gv_tm[tc], gv_tm[tc].ap[0:NP, h, :]))
                yield
                sbf = P_SBF.get()
                k.cp(sbf, S, eng="act")
                for vc in range(2):
                    k.mm(vv(pO[vc], pO[vc].ap[:, o["c0"]:o["c0"] + o["n"]]), vv(sbf, sbf.ap[:, vc * 128:(vc + 1) * 128]),
                         vv(gq, gq.ap[:, g0 + o["c0"]:g0 + o["c0"] + o["n"]]), start=False, stop=(oi == nown - 1))
                yield
                k.stt(S, S, vv(DEC, DEC.ap[:, h, o["dcol"]:o["dcol"] + 1]), vv(pU, pU.ap[:, 0:256]), ALU.mult, ALU.add)
                psfree(pU)
                k.dma(STQ, Ss_o[i, h], S.ap, reads=S.bufs, store=True)

            yield from inter_gen([hn_m] + [own_g(oi, o) for oi, o in enumerate(owners)], OWW + 1)
            HR2 = P_HR.get()
            HR2v = vv(HR2, HR2.ap[:, :, 0:NP])
            for vc in range(2):
                k.cp(vv(HR2, HR2.ap[:, vc, 0:NP]), vv(pO[vc], pO[vc].ap[:, 0:NP]), eng="act")
                psfree(pO[vc])
            yield from hn_gen(HR2v, NP, "gla_norm", h, ZGR + 2 * h, ZGQ + h, g0)

        def head_norm(HRv, NP, gname, h, gate, dst, g0, gla_h=None):
            if dst is None:
                dst = [Z[ZGQ + gla_h], Z[ZGK + gla_h]]
            SQ = P_SQ.get()
            SQv = vv(SQ, SQ.ap[:, :, 0:NP])
            k.act(SQv, HRv, AF.Square)
            p = ps()
            pv = vv(p, p.ap[:, 0:NP])
            for vc in range(2):
                k.mm(pv, ones_b, vv(SQ, SQ.ap[:, vc, 0:NP]), start=(vc == 0), stop=(vc == 1))
            lv = vv(lnv, lnv.ap[:, 0:NP])
            rv = vv(rstd, rstd.ap[:, 0:NP])
            k.act(lv, pv, AF.Ln, bias=EPSC, scale=1.0 / 256.0)
            k.act(rv, lv, AF.Exp, scale=-0.5)
            T1 = P_T1.get()
            for vc in range(2):
                k.stt(vv(T1, T1.ap[:, vc, 0:NP]), vv(HRv, HRv.ap[:, vc, :]), gain(gname, h * 2 + vc), rv, ALU.mult,
                      ALU.mult)
                k.tt(vv(dst[vc], dst[vc].ap[:, g0:g0 + NP]), vv(T1, T1.ap[:, vc, 0:NP]),
                     vv(gate[vc], gate[vc].ap[:, g0:g0 + NP]), ALU.mult)

        def cross_attn(j, cgs, with_s):
            rmsnorm(xT, "ca_norm", hT, cgs)
            qT = Z[0:8]
            aT = Z[8:16]
            for half in range(2):
                sq = slab(W["ca_wq"], 8, half * 512, 512)
                for f in range(4):
                    for (c0, n) in cgs:
                        pv = fm_mm(sq, 8, f * 128, 128, hT, c0, n)
                        k.act(vv(qT[half * 4 + f], qT[half * 4 + f].ap[:, c0:c0 + n]), pv, AF.Identity, scale=1.0 / 16.0)
            for h in range(4):
                ET = P_ET.get()
                for mc in range(2):
                    p = ps()
                    for c in range(2):
                        k.mm(p, vv(mkT_mem, mkT_mem.ap[:, h * 2 + c, mc * 128:(mc + 1) * 128]),
                             vv(qT[h * 2 + c], qT[h * 2 + c].ap[:, 0:512]), start=(c == 0), stop=(c == 1))
                    k.act(vv(ET, ET.ap[:, mc, :]), p, AF.Exp)
                pd = ps()
                for mc in range(2):
                    k.mm(pd, ones_b, vv(ET, ET.ap[:, mc, :]), start=(mc == 0), stop=(mc == 1))
                k.act(lnv, pd, AF.Ln)
                k.act(rstd, lnv, AF.Exp, scale=-1.0)
                for c in range(2):
                    p = ps()
                    for mc in range(2):
                        k.mm(p, vv(mv_mem, mv_mem.ap[:, mc, h * 256 + c * 128:h * 256 + (c + 1) * 128]),
                             vv(ET, ET.ap[:, mc, :]), start=(mc == 0), stop=(mc == 1))
                    k.tt(vv(aT[h * 2 + c], aT[h * 2 + c].ap[:, 0:512]), p, rstd, ALU.mult)
            MARKS.append(("ca_s", j, k.eng["pe"].cnt))
            so_pre = None
            if with_s:
                so_pre = [slab(W["ca_wo"], 8, half * 512, 512) for half in range(2)]
                halves = [V(TM_t[:, q, 0:2048].bitcast(F32), [TMb[q]]) for q in range(4)]
                halves += [V(Z_t[:, c0:c0 + 4, :].rearrange("p c n -> p (c n)")[:, 0:2048].bitcast(F32), Zb[c0:c0 + 4])
                           for c0 in (16, 20, 24, 28)]
                for si_ in (k.slot_i % NSLOT, (k.slot_i + 1) % NSLOT):
                    subs = borrow_slot(si_, False)
                    f = slots[si_][0][:, 0:4096].bitcast(F32)
                    halves.append(V(f[:, 0:1024], [sb_.bufs[0] for sb_ in subs[0:4]]))
                    halves.append(V(f[:, 1024:2048], [sb_.bufs[0] for sb_ in subs[4:8]]))
                sets = [halves[0:4], halves[4:8], halves[8:12]]
                NPRE = 2
                Vbf = [V(TM_t[:, 4, 0:2048].rearrange("p (m n) -> p m n", n=D), [TMb[4]]),
                       V(Z_t[:, 32:36, :].rearrange("p c n -> p (c n)")[:, 0:2048].rearrange("p (m n) -> p m n", n=D),
                         Zb[32:36])]

                def load_kv(i):
                    K0, K1, V0, V1 = sets[i % 3]
                    for mc, (Kh, Vh) in enumerate([(K0, V0), (K1, V1)]):
                        k.dma("sp", Kh.ap, ck_in[i, mc * 128:(mc + 1) * 128, :], writes=Kh.bufs)
                    for mc, (Kh, Vh) in enumerate([(K0, V0), (K1, V1)]):
                        k.dma("sp", Vh.ap, cv_in[i, mc * 128:(mc + 1) * 128, :], writes=Vh.bufs)

                for i in range(NPRE):
                    load_kv(i)
                for i in range(NS):
                    if i + NPRE < NS:
                        load_kv(i + NPRE)
                    K0, K1, V0, V1 = sets[i % 3]
                    Kh = [K0, K1]
                    Vh = [V0, V1]
                    KT = KTs8[i % 2]
                    Vb = Vbf[i % 2]
                    k.cp(vv(Vb, Vb.ap[:, 0, :]), Vh[0], eng="pool")
                    k.cp(vv(Vb, Vb.ap[:, 1, 0:512]), vv(Vh[1], Vh[1].ap[:, 0:512]), eng="dve")
                    k.cp(vv(Vb, Vb.ap[:, 1, 512:1024]), vv(Vh[1], Vh[1].ap[:, 512:1024]), eng="act")
                    for hc in range(8):
                        pK = ps()
                        for mc in range(2):
                            k.tr(vv(pK, pK.ap[:, mc * 128:(mc + 1) * 128]), vv(Kh[mc], Kh[mc].ap[:, hc * 128:(hc + 1) * 128]),
                                 ident_f)
                        k.cpx(KT[hc], vv(pK, pK.ap[:, 0:256]))
                    sc = slice(512 + 4 * i, 512 + 4 * i + 4)
                    p = ps()
                    for h in range(4):
                        for mc in range(2):
                            col = (h * 2 + mc) * 4
                            for c in range(2):
                                kt = KT[h * 2 + c]
                                k.mm(vv(p, p.ap[:, col:col + 4]), vv(kt, kt.ap[:, mc * 128:(mc + 1) * 128]),
                                     vv(qT[h * 2 + c], qT[h * 2 + c].ap[:, sc]), start=(c == 0), stop=(c == 1))
                    k.act(ETs, vv(p, p.ap[:, 0:32]), AF.Exp)
                    pd = ps()
                    e4 = ETs.ap.rearrange("p (h m t) -> p h m t", h=4, m=2)
                    for mc in range(2):
                        k.mm(vv(pd, pd.ap[:, 0:16].rearrange("p (h t) -> p h t", t=4)), ones_b, vv(ETs, e4[:, :, mc, :]),
                             start=(mc == 0), stop=(mc == 1))
                    k.act(LNs, vv(pd, pd.ap[:, 0:16]), AF.Ln)
                    k.act(RDs, LNs, AF.Exp, scale=-1.0)
                    po = ps()
                    for h in range(4):
                        for c in range(2):
                            col = (h * 2 + c) * 4
                            for mc in range(2):
                                k.mm(vv(po, po.ap[:, col:col + 4]),
                                     vv(Vb, Vb.ap[:, mc, h * 256 + c * 128:h * 256 + (c + 1) * 128]),
                                     vv(ETs, e4[:, h, mc, :]), start=(mc == 0), stop=(mc == 1))
                    for h in range(4):
                        for c in range(2):
                            col = (h * 2 + c) * 4
                            k.tt(vv(aT[h * 2 + c], aT[h * 2 + c].ap[:, sc]), vv(po, po.ap[:, col:col + 4]),
                                 vv(RDs, RDs.ap[:, h * 4:h * 4 + 4]), ALU.mult)
            for half in range(2):
                so = so_pre[half] if so_pre is not None else slab(W["ca_wo"], 8, half * 512, 512)
                for q in range(4):
                    dc = half * 4 + q
                    for (c0, n) in cgs:
                        pv = fm_mm(so, 8, q * 128, 128, aT, c0, n)
                        xv = vv(xT[dc], xT[dc].ap[:, c0:c0 + n])
                        k.tt(xv, pv, xv, ALU.add)

        ONEC = k.sbv("onec", [128, 1], F32)
        k.memset(ONEC, 1.0)
        ONEC4 = vv(ONEC, ONEC.ap[0:4, :])
        ONES4 = vv(ones_c, ones_c.ap[0:4, :])

        STAGE = 99
        TILES = [0, 1, 2, 3]
        MARKS = []
        for j in TILES:
            with_s = (j == NTILE - 1)
            cgs = [(0, 512)] + ([(512, 64)] if with_s else [])
            def mark(nm):
                MARKS.append((nm, j, k.eng["pe"].cnt))
            mark("start")
            mark("load_x")
            load_x(j, with_s, fetched=(j != TILES[0]))
            mark("ffn1")
            if STAGE >= 2:
                ffn("ffn1", cgs)
            mark("mix")
            if STAGE >= 3:
                mix(j, cgs, with_s)
            mark("ca")
            if j == TILES[0] and STAGE >= 1:
                mem_kv()
            if STAGE >= 4:
                cross_attn(j, cgs, with_s)
            mark("ffn2")
            if j + 1 < NTILE:
                fetch_x(j + 1, j + 1 == NTILE - 1)
            if STAGE >= 5:
                ffn("ffn2", cgs)
            mark("final")
            final_out(j, cgs, with_s)
            mark("end")

        if STAGE >= 3:
            for h in range(4):
                k.dma("sp", Cp_o[h].rearrange("(c p) d -> p c d", p=128), Cst[h].ap, reads=Cst[h].bufs, store=True)
                k.dma("sp", Sp_o[h], Sst[h].ap, reads=Sst[h].bufs, store=True)
            p = ps()
            k.tr(vv(p, p.ap[0:4, 0:128]), nTp, ident_f)
            npst = V(n_tm.ap[0:4, :], n_tm.bufs)
            k.cp(npst, vv(p, p.ap[0:4, 0:128]), eng="act")
            k.dma("sp", np_o, npst.ap, reads=npst.bufs, store=True)
            k.dma("sp", mp_o, mcar.ap, reads=mcar.bufs, store=True)
            if 3 in TILES:
              p = ps()
              k.tr(vv(p, p.ap[0:64, 0:128]), nTs, ident_f)
              k.cp(n_tm, vv(p, p.ap[0:64, 0:128]), eng="act")
              k.dma("sp", ns_o, n_tm.ap, reads=n_tm.bufs, store=True)
              k.dma("sp", ms_o, msT.ap, reads=msT.bufs, store=True)

        fin = {}
        for (s, v) in k.final:
            key = id(s)
            if key not in fin or fin[key][1] < v:
                fin[key] = (s, v)

        def replay(h, e):
            for waits, fn, inc in e.prog:
                for (s, v) in waits:
                    h.wait_ge(s, v)
                fn(h).then_inc(inc[0], inc[1])

        print("SBUF bytes remaining/partition:", nc.sbuf_bytes_remaining() if callable(nc.sbuf_bytes_remaining)
              else nc.sbuf_bytes_remaining, "instr:", {n: len(e.prog) for n, e in k.eng.items()})
        with nc.Block() as block:
            @block.tensor
            def _(h):
                replay(h, k.eng["pe"])

            @block.scalar
            def _(h):
                replay(h, k.eng["act"])

            @block.vector
            def _(h):
                replay(h, k.eng["dve"])

            @block.gpsimd
            def _(h):
                replay(h, k.eng["pool"])

            @block.sync
            def _(h):
                replay(h, k.eng["sp"])
                for (s, v) in fin.values():
                    h.wait_ge(s, v)
    return nc


_NC_CACHE = {}


def kernel(**inputs):
    f = lambda a: np.ascontiguousarray(np.asarray(a, dtype=np.float32))
    if "nc" not in _NC_CACHE:
        _NC_CACHE["nc"] = build_nc()
    nc = _NC_CACHE["nc"]
    wnames = ["ffn1_norm", "ffn1_wg", "ffn1_wu", "ffn1_wd", "mix_norm", "w_in", "b_if", "gla_wa2", "gla_ba",
              "mlstm_norm", "gla_norm", "w_br_m", "w_br_g", "w_out", "ca_norm", "mem_norm", "ca_wq", "ca_wk", "ca_wv",
              "ca_wo", "ffn2_norm", "ffn2_wg", "ffn2_wu", "ffn2_wd"]
    wd = {n: f(inputs[n][0]) for n in wnames}
    wd["final_norm"] = f(inputs["final_norm"])
    in_maps = []
    for c in range(NCORES):
        s0, s1 = c * NS, (c + 1) * NS
        m = dict(wd)
        m["x_p"] = f(inputs["x_prompt"][c])
        m["x_s"] = f(inputs["x_sample"][s0:s1].reshape(NS * STK, D))
        m["mem"] = f(inputs["mem_prompt"][c])
        m["C_s"] = f(inputs["state_mlstm_C"][0, s0:s1])
        m["n_s"] = f(inputs["state_mlstm_n"][0, s0:s1].reshape(NS * 4, 128))
        m["m_s"] = f(inputs["state_mlstm_m"][0, s0:s1])
        m["S_s"] = f(inputs["state_gla_S"][0, s0:s1])
        m["ck"] = f(inputs["cache_mem_k"][0, s0:s1].reshape(NS, 256, D))
        m["cv"] = f(inputs["cache_mem_v"][0, s0:s1].reshape(NS, 256, D))
        in_maps.append(m)
    res = run_bass_kernel_spmd(nc, in_maps, core_ids=list(range(NCORES)))
    R = res.results
    y_prompt = np.stack([R[c]["y_p"] for c in range(NCORES)]).astype(np.float32)
    y_sample = np.concatenate([R[c]["y_s"].reshape(NS, STK, D) for c in range(NCORES)]).astype(np.float32)
    Cp = np.stack([R[c]["Cp"] for c in range(NCORES)])[None].astype(np.float32)
    npp = np.stack([R[c]["np"] for c in range(NCORES)])[None].astype(np.float32)
    mp = np.stack([R[c]["mp"].reshape(4) for c in range(NCORES)])[None].astype(np.float32)
    Sp = np.stack([R[c]["Sp"] for c in range(NCORES)])[None].astype(np.float32)
    mkp = np.stack([R[c]["mkp"].reshape(256, 4, 256) for c in range(NCORES)])[None].astype(np.float32)
    mvp = np.stack([R[c]["mvp"].reshape(256, 4, 256) for c in range(NCORES)])[None].astype(np.float32)
    Cs = np.concatenate([R[c]["Cs"] for c in range(NCORES)])[None].astype(np.float32)
    ns = np.concatenate([R[c]["ns"].reshape(4, NS, 128).transpose(1, 0, 2) for c in range(NCORES)])[None].astype(
        np.float32)
    ms = np.concatenate([R[c]["ms"].reshape(4, NS).T for c in range(NCORES)])[None].astype(np.float32)
    Ss = np.concatenate([R[c]["Ss"] for c in range(NCORES)])[None].astype(np.float32)
    return (y_prompt, y_sample, Cp, npp, mp, Sp, mkp, mvp, Cs, np.ascontiguousarray(ns), np.ascontiguousarray(ms), Ss)
```
